# Optimizing a Trainium2 kernel written in Bass

```python
import jax, jax.numpy as jnp
from jax import lax
import numpy as np

D_MODEL = 1024
BATCH = 4
SEQ = 4096
DEPTH = 2
DEC_BATCH = 32
DEC_SEQ = 8
PAST_LEN = 8192
PAGE_SIZE = 128

HEAD_DIM = 64
N_HEADS_A = 8
N_HEADS_B = 8
D_MIX = (N_HEADS_A + N_HEADS_B) * HEAD_DIM
N_IDX_HEADS = 8
D_IDX = 64
TOPK_MAX = 256
Q_BLOCK = 128
ROPE_THETA = 10000.0
CHUNK = 128
D_CG = 2 * D_MODEL
N_GROUPS_C = 8
D_FF = 2816
CONV_W = 3
N_ATTN = (DEPTH + 1) // 2
N_CMLP = DEPTH // 2
ALPHA = (2 * DEPTH) ** 0.25
BETA = (8 * DEPTH) ** -0.25
LN_EPS = 1e-5
FORGET_BIAS = 3.0
AB_SIZES = (N_HEADS_A * HEAD_DIM, N_HEADS_A * HEAD_DIM, N_HEADS_A * HEAD_DIM,
            N_IDX_HEADS * D_IDX, D_IDX, N_IDX_HEADS,
            N_HEADS_B * HEAD_DIM, N_HEADS_B * HEAD_DIM, N_HEADS_B * HEAD_DIM, N_HEADS_B)
P_AB = 3 * N_HEADS_A * HEAD_DIM + N_IDX_HEADS * D_IDX + D_IDX + N_IDX_HEADS + 3 * N_HEADS_B * HEAD_DIM + N_HEADS_B

kernel_name = 'hybrid_dsa_fox_chunkgmlp_convffn_step'


def layer_norm(x, g, b):
    xf = x.astype(jnp.float32)
    mu = jnp.mean(xf, -1, keepdims=True)
    var = jnp.mean(jnp.square(xf - mu), -1, keepdims=True)
    return ((xf - mu) * lax.rsqrt(var + LN_EPS)).astype(x.dtype) * g + b


def adaln(c, w_mod, b_mod):
    mod = jax.nn.silu(c) @ w_mod + b_mod
    return jnp.split(mod[:, None, :], 6, axis=-1)


def rope(x, pos):
    half = x.shape[-1] // 2
    inv = ROPE_THETA ** (-jnp.arange(half, dtype=jnp.float32) / half)
    ang = pos.astype(jnp.float32)[:, None] * inv[None, :]
    cos = jnp.cos(ang)[:, None, :]
    sin = jnp.sin(ang)[:, None, :]
    x1 = x[..., :half].astype(jnp.float32)
    x2 = x[..., half:].astype(jnp.float32)
    return jnp.concatenate([x1 * cos - x2 * sin, x2 * cos + x1 * sin], -1).astype(x.dtype)


def project_ab(h, w_in, b_f, pos):
    B, T, _ = h.shape
    offs = []
    acc = 0
    for s in AB_SIZES[:-1]:
        acc += s
        offs.append(acc)
    qa, ka, va, qi, ki, wi, qb, kb, vb, fb = jnp.split(h @ w_in, offs, axis=-1)
    qa = rope(qa.reshape(B, T, N_HEADS_A, HEAD_DIM), pos)
    ka = rope(ka.reshape(B, T, N_HEADS_A, HEAD_DIM), pos)
    va = va.reshape(B, T, N_HEADS_A, HEAD_DIM)
    qi = rope(qi.reshape(B, T, N_IDX_HEADS, D_IDX), pos)
    ki = rope(ki[:, :, None, :], pos)[:, :, 0, :]
    qb = qb.reshape(B, T, N_HEADS_B, HEAD_DIM)
    kb = kb.reshape(B, T, N_HEADS_B, HEAD_DIM)
    vb = vb.reshape(B, T, N_HEADS_B, HEAD_DIM)
    logf = jax.nn.log_sigmoid(fb.astype(jnp.float32) + b_f.astype(jnp.float32))
    return qa, ka, va, qi, ki, wi, qb, kb, vb, logf


def indexer_topk(qi, wi, ki, q_pos, k_sel):
    logits = jnp.einsum('bqhd,bld->bqhl', qi, ki, preferred_element_type=jnp.float32) * (D_IDX ** -0.5)
    score = jnp.einsum('bqh,bqhl->bql', wi.astype(jnp.float32) * (N_IDX_HEADS ** -0.5), jax.nn.relu(logits))
    key_pos = jnp.arange(ki.shape[1], dtype=jnp.int32)
    causal = key_pos[None, :] <= q_pos[:, None]
    score = jnp.where(causal[None], score, -jnp.inf)
    _, sel = lax.top_k(score, k_sel)
    valid = sel <= q_pos[None, :, None]
    return sel, valid


def sparse_attend(q, k_g, v_g, valid):
    logits = jnp.einsum('bqhd,bqkhd->bqhk', q, k_g, preferred_element_type=jnp.float32) * (HEAD_DIM ** -0.5)
    logits = jnp.where(valid[:, :, None, :], logits, -jnp.inf)
    p = jax.nn.softmax(logits, axis=-1)
    return jnp.einsum('bqhk,bqkhd->bqhd', p.astype(v_g.dtype), v_g)


def take_rows(a, idx):
    return jax.vmap(lambda aa, ii: aa[ii])(a, idx)


def dsa_prompt(qa, ka, va, qi, ki, wi):
    B, T = qa.shape[:2]
    nb = T // Q_BLOCK
    k_sel = min(TOPK_MAX, T // 4)

    def to_blocks(a):
        return jnp.swapaxes(a.reshape((B, nb, Q_BLOCK) + a.shape[2:]), 0, 1)

    q_pos_blocks = jnp.arange(T, dtype=jnp.int32).reshape(nb, Q_BLOCK)

    def one_block(args):
        qa_b, qi_b, wi_b, qp = args
        sel, valid = indexer_topk(qi_b, wi_b, ki, qp, k_sel)
        return sparse_attend(qa_b, take_rows(ka, sel), take_rows(va, sel), valid)

    out = lax.map(one_block, (to_blocks(qa), to_blocks(qi), to_blocks(wi), q_pos_blocks))
    return jnp.swapaxes(out, 0, 1).reshape(B, T, N_HEADS_A * HEAD_DIM)


def dsa_sample(qa, ka, va, qi, ki, wi, pool_k, pool_v, pool_kidx, page_table):
    DB, Tn = qa.shape[:2]
    n_pages = page_table.shape[1]
    past = n_pages * PAGE_SIZE
    ki_past = pool_kidx[page_table].reshape(DB, past, D_IDX)
    ki_all = jnp.concatenate([ki_past, ki.astype(ki_past.dtype)], axis=1)
    q_pos = past + jnp.arange(Tn, dtype=jnp.int32)
    k_sel = min(TOPK_MAX, (past + Tn) // 4)
    sel, valid = indexer_topk(qi, wi, ki_all, q_pos, k_sel)
    in_past = (sel < past)[..., None, None]
    sp = jnp.minimum(sel, past - 1)
    phys = jax.vmap(lambda pt, s: pt[s])(page_table, sp // PAGE_SIZE)
    off = sp % PAGE_SIZE
    sn = jnp.clip(sel - past, 0, Tn - 1)
    k_g = jnp.where(in_past, pool_k[phys, off], take_rows(ka, sn).astype(pool_k.dtype))
    v_g = jnp.where(in_past, pool_v[phys, off], take_rows(va, sn).astype(pool_v.dtype))
    return sparse_attend(qa.astype(k_g.dtype), k_g, v_g, valid).reshape(DB, Tn, N_HEADS_A * HEAD_DIM)


def fox_prompt(q, k, v, logf):
    B, T = q.shape[:2]
    nb = T // Q_BLOCK
    cum_k = jnp.transpose(jnp.cumsum(logf, axis=1), (0, 2, 1))
    key_pos = jnp.arange(T, dtype=jnp.int32)
    q_blocks = jnp.swapaxes(q.reshape(B, nb, Q_BLOCK, N_HEADS_B, HEAD_DIM), 0, 1)
    cq_blocks = jnp.swapaxes(cum_k.reshape(B, N_HEADS_B, nb, Q_BLOCK), 0, 2).swapaxes(1, 2)
    q_pos_blocks = jnp.arange(T, dtype=jnp.int32).reshape(nb, Q_BLOCK)

    def one_block(args):
        q_b, cq_b, qp = args
        s = jnp.einsum('bqhd,bshd->bhqs', q_b, k, preferred_element_type=jnp.float32) * (HEAD_DIM ** -0.5)
        s = s + cq_b[..., None] - cum_k[:, :, None, :]
        s = jnp.where((key_pos[None, :] <= qp[:, None])[None, None], s, -jnp.inf)
        p = jax.nn.softmax(s, axis=-1)
        return jnp.einsum('bhqs,bshd->bqhd', p.astype(v.dtype), v)

    out = lax.map(one_block, (q_blocks, cq_blocks, q_pos_blocks))
    return jnp.swapaxes(out, 0, 1).reshape(B, T, N_HEADS_B * HEAD_DIM)


def fox_sample(q, k, v, logf, pool_k, pool_v, pool_logf, page_table):
    DB, Tn = q.shape[:2]
    n_pages = page_table.shape[1]
    past = n_pages * PAGE_SIZE
    k_past = pool_k[page_table]
    v_past = pool_v[page_table]
    lf_past = pool_logf[page_table].reshape(DB, past, N_HEADS_B).astype(jnp.float32)
    cum_k = jnp.transpose(jnp.cumsum(jnp.concatenate([lf_past, logf], axis=1), axis=1), (0, 2, 1))
    cq = cum_k[:, :, past:]
    qc = q.astype(k_past.dtype)
    s_past = jnp.einsum('bqhd,bnphd->bhqnp', qc, k_past, preferred_element_type=jnp.float32).reshape(DB, N_HEADS_B, Tn, past)
    s_new = jnp.einsum('bqhd,bshd->bhqs', qc, k.astype(k_past.dtype), preferred_element_type=jnp.float32)
    s = jnp.concatenate([s_past, s_new], axis=-1) * (HEAD_DIM ** -0.5) + cq[..., None] - cum_k[:, :, None, :]
    key_pos = jnp.arange(past + Tn, dtype=jnp.int32)
    q_pos = past + jnp.arange(Tn, dtype=jnp.int32)
    s = jnp.where((key_pos[None, :] <= q_pos[:, None])[None, None], s, -jnp.inf)
    p = jax.nn.softmax(s, axis=-1).astype(v_past.dtype)
    out = jnp.einsum('bhqnp,bnphd->bqhd', p[..., :past].reshape(DB, N_HEADS_B, Tn, n_pages, PAGE_SIZE), v_past) \
        + jnp.einsum('bhqs,bshd->bqhd', p[..., past:], v.astype(v_past.dtype))
    return out.reshape(DB, Tn, N_HEADS_B * HEAD_DIM)


def chunk_gate_inputs(h, w_in, lnv_g, lnv_b):
    z = jax.nn.gelu(h @ w_in, approximate=False)
    u, v = jnp.split(z, 2, axis=-1)
    return u, layer_norm(v, lnv_g, lnv_b)


def spatial_mix(v, w_s, b_s):
    B, T, _ = v.shape
    vg = v.reshape(B, T // CHUNK, CHUNK, N_GROUPS_C, D_CG // N_GROUPS_C)
    tri = (jnp.arange(CHUNK)[:, None] >= jnp.arange(CHUNK)[None, :]).astype(w_s.dtype)
    mixed = jnp.einsum('gts,bnsgc->bntgc', w_s * tri[None], vg) + b_s.T[:, :, None]
    return mixed.reshape(B, T, D_CG)


def conv_ffn(h, prev, w_up, w_conv, b_conv, w_down):
    T = h.shape[1]
    up = h @ w_up
    full = jnp.concatenate([prev.astype(up.dtype), up], axis=1)
    y = b_conv
    for k in range(CONV_W):
        y = y + w_conv[k] * full[:, k:k + T]
    gate, val = jnp.split(y, 2, axis=-1)
    out = (jax.nn.gelu(gate, approximate=False) * val) @ w_down
    return out, full[:, T:]


def setup_inputs(seed: int = 0) -> dict:
    key = jax.random.key(seed)
    ks = jax.random.split(key, 32)
    n_pages = PAST_LEN // PAGE_SIZE
    n_used = DEC_BATCH * n_pages
    n_phys = n_used + n_used // 4
    d = D_MODEL

    def nrm(i, shape, scale):
        return jax.random.normal(ks[i], shape, jnp.float32) * scale

    return {
        'x_prompt': nrm(0, (BATCH, SEQ, d), 1.0),
        'x_sample': nrm(1, (DEC_BATCH, DEC_SEQ, d), 1.0),
        'cache_a_k': nrm(2, (N_ATTN, n_phys, PAGE_SIZE, N_HEADS_A, HEAD_DIM), 1.0),
        'cache_a_v': nrm(3, (N_ATTN, n_phys, PAGE_SIZE, N_HEADS_A, HEAD_DIM), 1.0),
        'cache_a_kidx': nrm(4, (N_ATTN, n_phys, PAGE_SIZE, D_IDX), 1.0),
        'cache_b_k': nrm(5, (N_ATTN, n_phys, PAGE_SIZE, N_HEADS_B, HEAD_DIM), 1.0),
        'cache_b_v': nrm(6, (N_ATTN, n_phys, PAGE_SIZE, N_HEADS_B, HEAD_DIM), 1.0),
        'cache_b_logf': jax.nn.log_sigmoid(FORGET_BIAS + nrm(7, (N_ATTN, n_phys, PAGE_SIZE, N_HEADS_B), 1.0)),
        'state_ffn_conv': nrm(8, (DEPTH, DEC_BATCH, CONV_W - 1, 2 * D_FF), 1.0),
        'page_table': jax.random.permutation(ks[9], n_phys)[:n_used].reshape(DEC_BATCH, n_pages).astype(jnp.int32),
        'c_prompt': nrm(10, (BATCH, d), 1.0),
        'c_sample': nrm(11, (DEC_BATCH, d), 1.0),
        'w_mod': nrm(12, (DEPTH, d, 6 * d), 0.2 * d ** -0.5),
        'b_mod': nrm(13, (DEPTH, 6 * d), 0.01),
        'ln1_g': 1.0 + nrm(14, (DEPTH, d), 0.02),
        'ln1_b': nrm(15, (DEPTH, d), 0.02),
        'ln2_g': 1.0 + nrm(16, (DEPTH, d), 0.02),
        'ln2_b': nrm(17, (DEPTH, d), 0.02),
        'w_in_ab': nrm(18, (N_ATTN, d, P_AB), d ** -0.5),
        'b_forget': FORGET_BIAS + nrm(19, (N_ATTN, N_HEADS_B), 0.1),
        'w_out_ab': nrm(20, (N_ATTN, D_MIX, d), BETA * D_MIX ** -0.5),
        'w_in_c': nrm(21, (N_CMLP, d, 2 * D_CG), d ** -0.5),
        'lnv_g': 1.0 + nrm(22, (N_CMLP, D_CG), 0.02),
        'lnv_b': nrm(23, (N_CMLP, D_CG), 0.02),
        'w_spatial': nrm(24, (N_CMLP, N_GROUPS_C, CHUNK, CHUNK), 0.5 * CHUNK ** -0.5),
        'b_spatial': 1.0 + nrm(25, (N_CMLP, N_GROUPS_C, CHUNK), 0.02),
        'w_out_c': nrm(26, (N_CMLP, D_CG, d), BETA * D_CG ** -0.5),
        'w_up': nrm(27, (DEPTH, d, 2 * D_FF), d ** -0.5),
        'w_conv': nrm(28, (DEPTH, CONV_W, 2 * D_FF), CONV_W ** -0.5),
        'b_conv': nrm(29, (DEPTH, 2 * D_FF), 0.02),
        'w_down': nrm(30, (DEPTH, D_FF, d), BETA * D_FF ** -0.5),
    }


def reference(x_prompt, x_sample, cache_a_k, cache_a_v, cache_a_kidx, cache_b_k, cache_b_v, cache_b_logf,
              state_ffn_conv, page_table, c_prompt, c_sample,
              w_mod, b_mod, ln1_g, ln1_b, ln2_g, ln2_b,
              w_in_ab, b_forget, w_out_ab,
              w_in_c, lnv_g, lnv_b, w_spatial, b_spatial, w_out_c,
              w_up, w_conv, b_conv, w_down):
    t_p = x_prompt.shape[1]
    t_s = x_sample.shape[1]
    past = page_table.shape[1] * PAGE_SIZE
    pos_p = jnp.arange(t_p, dtype=jnp.int32)
    pos_s = past + jnp.arange(t_s, dtype=jnp.int32)
    xp, xs = x_prompt, x_sample
    ak_p, av_p, aki_p, bk_p, bv_p, blf_p, conv_p = [], [], [], [], [], [], []
    ak_s, av_s, aki_s, bk_s, bv_s, blf_s, conv_s, cv_s = [], [], [], [], [], [], [], []
    for i in range(DEPTH):
        j = i // 2
        sh1p, sc1p, g1p, sh2p, sc2p, g2p = adaln(c_prompt, w_mod[i], b_mod[i])
        sh1s, sc1s, g1s, sh2s, sc2s, g2s = adaln(c_sample, w_mod[i], b_mod[i])
        hp = xp * (1 + sc1p) + sh1p
        hs = xs * (1 + sc1s) + sh1s
        if i % 2 == 0:
            qa, ka, va, qi, ki, wi, qb, kb, vb, lf = project_ab(hp, w_in_ab[j], b_forget[j], pos_p)
            mix_p = jnp.concatenate([dsa_prompt(qa, ka, va, qi, ki, wi), fox_prompt(qb, kb, vb, lf)], axis=-1) @ w_out_ab[j]
            ak_p.append(ka); av_p.append(va); aki_p.append(ki)
            bk_p.append(kb); bv_p.append(vb); blf_p.append(lf)
            qa, ka, va, qi, ki, wi, qb, kb, vb, lf = project_ab(hs, w_in_ab[j], b_forget[j], pos_s)
            a_out = dsa_sample(qa, ka, va, qi, ki, wi, cache_a_k[j], cache_a_v[j], cache_a_kidx[j], page_table)
            b_out = fox_sample(qb, kb, vb, lf, cache_b_k[j], cache_b_v[j], cache_b_logf[j], page_table)
            mix_s = jnp.concatenate([a_out.astype(hs.dtype), b_out.astype(hs.dtype)], axis=-1) @ w_out_ab[j]
            ak_s.append(ka); av_s.append(va); aki_s.append(ki)
            bk_s.append(kb); bv_s.append(vb); blf_s.append(lf)
        else:
            u, v = chunk_gate_inputs(hp, w_in_c[j], lnv_g[j], lnv_b[j])
            mix_p = (u * spatial_mix(v, w_spatial[j], b_spatial[j])) @ w_out_c[j]
            u, v = chunk_gate_inputs(hs, w_in_c[j], lnv_g[j], lnv_b[j])
            pad = (-t_s) % CHUNK
            mixed = spatial_mix(jnp.pad(v, ((0, 0), (0, pad), (0, 0))), w_spatial[j], b_spatial[j])[:, :t_s]
            mix_s = (u * mixed) @ w_out_c[j]
            cv_s.append(v)
        xp = layer_norm(ALPHA * xp + (1 + g1p) * mix_p, ln1_g[i], ln1_b[i])
        xs = layer_norm(ALPHA * xs + (1 + g1s) * mix_s, ln1_g[i], ln1_b[i])
        hp = xp * (1 + sc2p) + sh2p
        hs = xs * (1 + sc2s) + sh2s
        prev_p = jnp.zeros((xp.shape[0], CONV_W - 1, 2 * D_FF), xp.dtype)
        ff_p, cp = conv_ffn(hp, prev_p, w_up[i], w_conv[i], b_conv[i], w_down[i])
        ff_s, cs = conv_ffn(hs, state_ffn_conv[i], w_up[i], w_conv[i], b_conv[i], w_down[i])
        conv_p.append(cp); conv_s.append(cs)
        xp = layer_norm(ALPHA * xp + (1 + g2p) * ff_p, ln2_g[i], ln2_b[i])
        xs = layer_norm(ALPHA * xs + (1 + g2s) * ff_s, ln2_g[i], ln2_b[i])
    return (xp, xs,
            jnp.stack(ak_p), jnp.stack(av_p), jnp.stack(aki_p), jnp.stack(bk_p), jnp.stack(bv_p), jnp.stack(blf_p),
            jnp.stack(conv_p),
            jnp.stack(ak_s), jnp.stack(av_s), jnp.stack(aki_s), jnp.stack(bk_s), jnp.stack(bv_s), jnp.stack(blf_s),
            jnp.stack(conv_s), jnp.stack(cv_s))
```

```python
from concourse.bass_utils import run_bass_kernel_spmd
import concourse.bass as bass
import concourse.mybir as mybir

SEM_MAX = 30000


def _rect(ap):
    t = ap.tensor
    name = t.name
    pat = list(ap.ap)
    esz = mybir.dt.size(ap.dtype) if hasattr(mybir.dt, "size") else None
    if esz is None:
        esz = {"float32": 4, "bfloat16": 2, "int32": 4, "uint32": 4, "float16": 2,
               "uint8": 1, "int8": 1, "uint16": 2, "int16": 2}[str(ap.dtype).split(".")[-1]]
    space = str(ap.space) if hasattr(ap, "space") else ""
    if "DRAM" in space.upper() or "Dram" in type(t).__name__ or "DRam" in type(t).__name__:
        lo = ap.offset
        hi = lo
        for (s, c) in pat:
            if c > 1:
                if s >= 0:
                    hi += s * (c - 1)
                else:
                    lo += s * (c - 1)
        return (name, 0, 1, lo * esz, (hi + 1) * esz)
    pstep, pcount = pat[0]
    if pstep == 0:
        pstep = 1 << 40
    p0 = ap.offset // pstep if pstep < (1 << 40) else 0
    foff = ap.offset - p0 * pstep if pstep < (1 << 40) else ap.offset
    lo = foff
    hi = foff
    for (s, c) in pat[1:]:
        if c > 1:
            if s >= 0:
                hi += s * (c - 1)
            else:
                lo += s * (c - 1)
    if name.startswith("pb"):
        return (name, (p0 // 32) * 32, ((p0 + pcount + 31) // 32) * 32, 0, 2048)
    return (name, p0, p0 + pcount, lo * esz, (hi + 1) * esz)


def _overlap(a, b):
    return a[1] < b[2] and b[1] < a[2] and a[3] < b[4] and b[3] < a[4]


def _covers(a, b):
    return a[1] <= b[1] and a[2] >= b[2] and a[3] <= b[3] and a[4] >= b[4]


class Op:
    __slots__ = ("eng", "fn", "deps", "sig", "sigval", "is_dma", "dsem", "idx", "seq")


class Sched:
    COMPUTE = ("pe", "act", "dve", "pool")

    def __init__(self, nc):
        self.nc = nc
        self.ops = []
        self.recs = {}
        self.engs = {
            "pe": nc.tensor, "act": nc.scalar, "dve": nc.vector, "pool": nc.gpsimd, "sp": nc.sync,
        }

    def add(self, eng, fn, reads=(), writes=(), dma=False):
        op = Op()
        op.eng = eng
        op.fn = fn
        op.is_dma = dma
        op.sig = False
        op.sigval = None
        op.dsem = None
        op.idx = len(self.ops)
        deps = set()
        accs = [(_rect(a), False) for a in reads] + [(_rect(a), True) for a in writes]
        for rect, is_w in accs:
            lst = self.recs.get(rect[0])
            if lst is None:
                continue
            for r in lst:
                if (is_w or r[1]) and _overlap(rect, r[0]):
                    deps.add(r[2])
        for rect, is_w in accs:
            lst = self.recs.setdefault(rect[0], [])
            if is_w:
                lst[:] = [r for r in lst if not _covers(rect, r[0])]
                lst.append([rect, True, op.idx])
            else:
                done = False
                if not dma:
                    for r in lst:
                        if (not r[1]) and r[0] == rect and self.ops[r[2]].eng == eng \
                                and not self.ops[r[2]].is_dma:
                            r[2] = op.idx
                            done = True
                            break
                if not done:
                    lst.append([rect, False, op.idx])
        deps.discard(op.idx)
        if eng == "pe" and not dma:
            deps = {d for d in deps if not (self.ops[d].eng == "pe" and not self.ops[d].is_dma)}
        op.deps = sorted(deps)
        for d in op.deps:
            self.ops[d].sig = True
        self.ops.append(op)
        return op

    def barrier(self):
        last = {}
        for op in self.ops:
            if op.eng != "barrier" and not op.is_dma:
                last[op.eng] = op
        op = Op()
        op.eng = "barrier"
        op.fn = None
        op.is_dma = False
        op.sig = False
        op.sigval = None
        op.dsem = None
        op.idx = len(self.ops)
        op.deps = [o.idx for o in last.values()]
        for o in last.values():
            o.sig = True
        self.ops.append(op)
        self.recs.clear()

    def emit(self, final_wait_eng="sp", n_dma_sems=24):
        nc = self.nc
        sems = {}
        cnt = {}
        epoch = {}

        def new_sem(tag):
            return nc.alloc_semaphore(name=f"s_{tag}_{len(sems)}")

        for e in self.COMPUTE:
            sems[e] = new_sem(e)
            cnt[e] = 0
        dma_sems = [new_sem("dma%d" % i) for i in range(n_dma_sems)]
        dma_cnt = [0] * n_dma_sems
        dma_rr = 0
        waited = {}
        all_sem_objs = []

        def do_wait(weng, semobj, key, val):
            k = (weng, key)
            if waited.get(k, 0) >= val:
                return
            waited[k] = val
            self.engs[weng].wait_ge(semobj, val)

        last_dma = []
        dma_last = {}
        for op in self.ops:
            e = op.eng
            if e == "barrier":
                for w in ("pe", "act", "dve", "pool", "sp"):
                    for d in op.deps:
                        semobj, val = self.ops[d].sigval
                        do_wait(w, semobj, ("c", id(semobj)), val)
                    for so, v in dma_last.values():
                        do_wait(w, so, ("d", id(so)), v)
                continue
            for d in op.deps:
                dop = self.ops[d]
                if dop.is_dma:
                    so, val = dop.dsem
                    do_wait(e, so, ("d", id(so)), val)
                else:
                    semobj, val = dop.sigval
                    do_wait(e, semobj, ("c", id(semobj)), val)
            if op.is_dma:
                si = dma_rr
                dma_rr = (dma_rr + 1) % n_dma_sems
                if dma_cnt[si] + 16 > SEM_MAX:
                    dma_sems[si] = new_sem("dmaX")
                    dma_cnt[si] = 0
                if dma_cnt[si] > 0:
                    do_wait(e, dma_sems[si], ("d", id(dma_sems[si])), dma_cnt[si])
                ins = op.fn()
                dma_cnt[si] += 16
                ins.then_inc(dma_sems[si], 16)
                op.dsem = (dma_sems[si], dma_cnt[si])
                dma_last[id(dma_sems[si])] = (dma_sems[si], dma_cnt[si])
                last_dma.append((si, dma_sems[si], dma_cnt[si]))
            else:
                ins = op.fn()
                if op.sig:
                    if cnt[e] + 1 > SEM_MAX:
                        sems[e] = new_sem(e + "X")
                        cnt[e] = 0
                    cnt[e] += 1
                    ins.then_inc(sems[e], 1)
                    op.sigval = (sems[e], cnt[e])
        fin = {}
        for si, so, v in last_dma:
            fin[id(so)] = (so, max(v, fin.get(id(so), (None, 0))[1]), si)
        for so, v, si in fin.values():
            self.engs[final_wait_eng].wait_ge(so, v)
        return len(self.ops)


import math
from contextlib import ExitStack
import numpy as np
import concourse.bass as bass
import concourse.mybir as mybir

F32 = mybir.dt.float32
BF16 = mybir.dt.bfloat16
I32 = mybir.dt.int32
U32 = mybir.dt.uint32
AF = mybir.ActivationFunctionType
ALU = mybir.AluOpType
AX = mybir.AxisListType

D = 1024
NKB = 32
NS = 18
SLOT0 = 14
DFF = 2816
DUP = 5632
PAB = 3664
DCG = 2048
NEG = -30000.0
NEGBIG = -1.0e30
ALPHA = 4.0 ** 0.25
LN_EPS = 1e-5
NBIS = 16
WI_SCALE = (64 ** -0.5) * (8 ** -0.5)
NPHYS = 2560
NPG = 64

CH = [("qa", 0, 512, False), ("ka", 512, 1024, True), ("va", 1024, 1536, True), ("qi", 1536, 2048, False),
      ("kiwi", 2048, 2120, True), ("qb", 2120, 2632, False), ("kb", 2632, 3144, True),
      ("vb", 3144, 3656, True), ("fb", 3656, 3664, True)]


class B:
    def __init__(self, nc, n_phys=NPHYS):
        self.nc = nc
        self.S = Sched(nc)
        self.n_phys = n_phys
        self.pb = [nc.alloc_psum_tensor("pb%d" % i, [128, 512], F32).ap() for i in range(8)]
        self.rot = {}
        self.dram = {}

    def bank(self, group, banks):
        i = self.rot.get(group, 0)
        self.rot[group] = i + 1
        return self.pb[banks[i % len(banks)]]

    def dma(self, out, in_, q="sp"):
        eng = {"sp": self.nc.sync, "pool": self.nc.gpsimd, "act": self.nc.scalar}[q]
        self.S.add(q, lambda: eng.dma_start(out=out, in_=in_), [in_], [out], dma=True)

    def gather(self, out, src2d, idxcol):
        nc = self.nc
        self.S.add("pool", lambda: nc.gpsimd.indirect_dma_start(
            out=out, out_offset=None, in_=src2d,
            in_offset=bass.IndirectOffsetOnAxis(ap=idxcol.bitcast(U32), axis=0)),
            [src2d, idxcol], [out], dma=True)

    def _e(self, eng):
        return {"dve": self.nc.vector, "pool": self.nc.gpsimd}[eng]

    def tt(self, out, in0, in1, op, eng="dve"):
        e = self._e(eng)
        self.S.add(eng, lambda: e.tensor_tensor(out=out, in0=in0, in1=in1, op=op), [in0, in1], [out])

    def ts(self, out, in0, s1, s2=None, op0=ALU.mult, op1=None, accum=None, eng="dve"):
        e = self._e(eng)
        rd = [in0] + [x for x in (s1, s2) if hasattr(x, "tensor")]
        wr = [out] + ([accum] if accum is not None else [])
        kw = {}
        if op1 is not None:
            kw["op1"] = op1
        if accum is not None:
            kw["accum_out"] = accum
        self.S.add(eng, lambda: e.tensor_scalar(out=out, in0=in0, scalar1=s1, scalar2=s2, op0=op0, **kw), rd, wr)

    def stt(self, out, in0, scalar, in1, op0, op1):
        nc = self.nc
        rd = [in0, in1] + ([scalar] if hasattr(scalar, "tensor") else [])
        self.S.add("dve", lambda: nc.vector.scalar_tensor_tensor(out=out, in0=in0, scalar=scalar, in1=in1,
                                                                 op0=op0, op1=op1), rd, [out])

    def act(self, out, in_, func, bias=None, scale=1.0, accum=None):
        nc = self.nc
        rd = [in_] + [x for x in (bias, scale) if hasattr(x, "tensor")]
        wr = [out] + ([accum] if accum is not None else [])
        kw = {}
        if bias is not None:
            kw["bias"] = bias
        if accum is not None:
            kw["accum_out"] = accum
        self.S.add("act", lambda: nc.scalar.activation(out=out, in_=in_, func=func, scale=scale, **kw), rd, wr)

    def cp(self, out, in_, eng="act"):
        nc = self.nc
        if eng == "act":
            self.S.add("act", lambda: nc.scalar.copy(out=out, in_=in_), [in_], [out])
        else:
            e = self._e(eng)
            self.S.add(eng, lambda: e.tensor_copy(out=out, in_=in_), [in_], [out])

    def memset(self, out, val, eng="pool"):
        e = self._e(eng)
        self.S.add(eng, lambda: e.memset(out, val), [], [out])

    def mm(self, out, lhsT, rhs, start, stop):
        nc = self.nc
        self.S.add("pe", lambda: nc.tensor.matmul(out, lhsT=lhsT, rhs=rhs, start=start, stop=stop),
                   [lhsT, rhs], [out])

    def tr(self, out, in_, ident):
        nc = self.nc
        self.S.add("pe", lambda: nc.tensor.transpose(out, in_, ident), [in_, ident], [out])

    def reduce(self, out, in_, op, absval=False):
        nc = self.nc
        kw = {"apply_absolute_value": True} if absval else {}
        self.S.add("dve", lambda: nc.vector.tensor_reduce(out=out, in_=in_, axis=AX.X, op=op, **kw), [in_], [out])

    def recip(self, out, in_):
        nc = self.nc
        self.S.add("dve", lambda: nc.vector.reciprocal(out=out, in_=in_), [in_], [out])

    def bn(self, mv, x, M, n, stats):
        nc = self.nc
        nch = (n + 511) // 512
        for c in range(nch):
            a = x[:M, c * 512:min(n, (c + 1) * 512)]
            o = stats[:M, c, :]
            self.S.add("dve", lambda a=a, o=o: nc.vector.bn_stats(out=o, in_=a), [a], [o])
        si = stats[:M, 0:nch, :]
        mo = mv[:M, 0:2]
        self.S.add("dve", lambda: nc.vector.bn_aggr(out=mo, in_=si), [si], [mo])

    def transposes(self, dst, src_bf, M, nchunk, ident_b, evac="act"):
        done = 0
        while done < nchunk:
            n = min(8, nchunk - done)
            pbk = self.bank("tr", [4, 5]).bitcast(BF16).rearrange("p (a c) -> p a c", a=8)
            for k in range(n):
                self.tr(pbk[:, k, 0:M], src_bf[:M, (done + k) * 128:(done + k + 1) * 128], ident_b[:M, :M])
            self.cp(dst[:, done:done + n, 0:M], pbk[:, 0:n, 0:M], eng=evac)
            done += n

    def layernorm_affine(self, out, x, M, n, g_bc, b_bc, tmp, stats, mv):
        self.bn(mv, x, M, n, stats)
        self.act(mv[:M, 2:3], mv[:M, 1:2], AF.Sqrt, bias=self.eps_col[:M, :], scale=1.0)
        self.recip(mv[:M, 3:4], mv[:M, 2:3])
        self.ts(tmp[:M, :n], x[:M, :n], mv[:M, 0:1], mv[:M, 3:4], op0=ALU.subtract, op1=ALU.mult)
        self.tt(tmp[:M, :n], tmp[:M, :n], g_bc[:M, :n], ALU.mult, eng="pool")
        self.tt(out[:M, :n], tmp[:M, :n], b_bc[:M, :n], ALU.add, eng="pool")


DEBUG_OUT = set()


def declare(b, with_sample=True):
    nc = b.nc
    T = {}

    def I(name, shape, dt=F32):
        T[name] = nc.dram_tensor(name, list(shape), dt, kind="ExternalInput").ap()

    def O(name, shape, dt=F32):
        T[name] = nc.dram_tensor(name, list(shape), dt, kind="ExternalOutput").ap()

    def Sx(name, shape, dt=F32):
        kind = "ExternalOutput" if name in DEBUG_OUT else "Internal"
        T[name] = nc.dram_tensor(name, list(shape), dt, kind=kind).ap()

    I("xk", [NKB * 128, D]); I("cstab", [NKB * 128, 64]); I("cP", [128, D])
    I("identf", [128, 128]); I("tinc", [128, 128]); I("triT01", [128, 128]); I("triqk", [128, 128])
    I("jm", [1, NKB * 128]); I("blkbias", [1, NKB]); I("haloflag", [128, 1]); I("pow2", [128, NBIS])
    I("w_mod", [2, D, 6 * D]); I("b_mod", [2, 6 * D])
    for n in ("ln1_g", "ln1_b", "ln2_g", "ln2_b"):
        I(n, [2, D])
    I("w_in_ab", [D, PAB]); I("b_forget", [1, 8]); I("w_out_ab", [D, D])
    I("w_in_c", [D, 2 * DCG]); I("lnv_g", [1, DCG]); I("lnv_b", [1, DCG])
    I("w_spatial", [8, 128, 128]); I("b_spatial", [8, 128]); I("w_out_c", [DCG, D])
    I("w_up", [2, D, DUP]); I("w_conv", [2, 3, DUP]); I("b_conv", [2, DUP]); I("w_down", [2, DFF, D])
    I("xs", [32, D]); I("cstab_s", [32, 64]); I("cS", [32, D])
    if with_sample:
        I("cache_a_k", [b.n_phys * 128, 512]); I("cache_a_v", [b.n_phys * 128, 512])
        I("cache_a_kidx", [b.n_phys * 128, 64])
        I("cache_b_k", [b.n_phys * 128, 512]); I("cache_b_v", [b.n_phys * 128, 512])
        I("cache_b_logf", [b.n_phys * 128, 8])
        I("state_conv", [2, 8, DUP]); I("page_table", [1, 4 * NPG], I32); I("iota128", [128, 1], I32)
        I("smask01", [32, 4, 8]); I("sidxmask", [32, 32]); I("segmask", [32, 4]); I("tinc32", [32, 32])
        I("selrow", [32, 4, 128]); I("bd01", [64, 8, 64])
    O("y_p", [NS * 128, D]); O("y_s", [32, D])
    O("ak_p", [NKB * 128, 512]); O("av_p", [NKB * 128, 512]); O("aki_p", [NKB * 128, 64])
    O("bk_p", [NKB * 128, 512]); O("bv_p", [NKB * 128, 512]); O("blf_p", [NKB * 128, 8])
    O("conv_p", [2, 2, DUP])
    O("ak_s", [32, 512]); O("av_s", [32, 512]); O("aki_s", [32, 64])
    O("bk_s", [32, 512]); O("bv_s", [32, 512]); O("blf_s", [32, 8])
    O("conv_s", [2, 8, DUP]); O("cv_s", [32, DCG])
    if "dbg_u" in DEBUG_OUT:
        O("dbg_u", [128, 520]); O("dbg_y", [128, 512]); O("dbg_w", [128, 176])
    Sx("modP", [2, 128, 6 * D]); Sx("modS", [2, 32, 6 * D])
    Sx("kaT_s", [128, 4, NKB * 128], BF16); Sx("kbT_s", [128, 4, NKB * 128], BF16)
    Sx("kiT2_s", [128, NKB * 128], BF16)
    Sx("va_s", [NKB, 128, 576], BF16); Sx("vb_s", [NKB, 128, 576], BF16)
    Sx("qaT_s", [NS, 128, 4, 128], BF16); Sx("qiT_s", [NS, 128, 4, 128], BF16); Sx("qbT_s", [NS, 128, 4, 128], BF16)
    Sx("wi_s", [NS, 128, 8])
    Sx("x1_s", [NS * 128, D]); Sx("x2_s", [NS * 128, D]); Sx("x3_s", [NS * 128, D]); Sx("ffp_s", [NS * 128, D])
    Sx("gT_s", [NS, 128, 16, 128], BF16)
    Sx("mixTa_s", [NS, 128, 4, 128], BF16)
    Sx("xs1_s", [32, D]); Sx("xs2_s", [32, D]); Sx("xs3_s", [32, D]); Sx("ffps_s", [32, D])
    Sx("gTs_s", [128, 16, 32], BF16)
    Sx("sq_s", [3, 128, 4, 32], BF16)
    Sx("sk_s", [4, 32, 576], BF16)
    Sx("skiT_s", [128, 32], BF16); Sx("swi_s", [32, 8]); Sx("slf_s", [32, 8])
    Sx("mixTs_s", [128, 8, 32], BF16)
    return T


def consts(b, T):
    nc = b.nc
    g = {}

    def al(name, shape, dt):
        return nc.alloc_sbuf_tensor("g_" + name, list(shape), dt).ap()

    g["identf"] = al("identf", [128, 128], F32)
    g["identb"] = al("identb", [128, 128], BF16)
    g["ones_f"] = al("ones_f", [128, 128], F32)
    g["ones_b"] = al("ones_b", [128, 128], BF16)
    g["tinc"] = al("tinc", [128, 128], F32)
    g["triT01"] = al("triT01", [128, 128], BF16)
    g["triqk"] = al("triqk", [128, 128], F32)
    g["LF"] = al("LF", [128, NKB, 8], F32)
    g["haloflag"] = al("haloflag", [128, 1], F32)
    g["blkbias"] = al("blkbias", [128, NKB], F32)
    g["pow2"] = al("pow2", [128, NBIS], F32)
    b.eps_col = al("eps_col", [128, 1], F32)
    tmp = al("c_tmp", [128, 128], F32)
    b.dma(g["identf"], T["identf"])
    b.cp(g["identb"], g["identf"], eng="dve")
    b.memset(g["ones_f"], 1.0)
    b.memset(g["ones_b"], 1.0)
    b.memset(b.eps_col, LN_EPS)
    b.dma(g["tinc"], T["tinc"])
    b.dma(tmp, T["triT01"])
    b.cp(g["triT01"], tmp, eng="dve")
    b.dma(g["triqk"], T["triqk"])
    b.dma(g["haloflag"], T["haloflag"])
    b.dma(g["blkbias"], T["blkbias"].partition_broadcast(128))
    b.dma(g["pow2"], T["pow2"])
    return g


def phase0(b, T, g):
    nc = b.nc
    with ExitStack() as es:
        def al(name, shape, dt):
            return es.enter_context(nc.sbuf_tensor(name, list(shape), dt)).ap()
        cf = al("p0_cf", [128, D], F32)
        cb = al("p0_cb", [128, D], BF16)
        cT = {"P": al("p0_cTP", [128, 8, 128], BF16), "S": al("p0_cTS", [128, 8, 32], BF16)}
        for grp, M, src in (("P", 128, T["cP"]), ("S", 32, T["cS"])):
            b.dma(cf[:M, :], src)
            b.act(cb[:M, :], cf[:M, :], AF.Silu)
            b.transposes(cT[grp], cb, M, 8, g["identb"])
        wch = [al("p0_w%d" % i, [128, 8, 512], BF16) for i in range(2)]
        bch = [al("p0_b%d" % i, [128, 512], F32) for i in range(2)]
        och = [al("p0_o%d" % i, [128, 512], F32) for i in range(4)]
        it = 0
        for i in range(2):
            for j in range(12):
                w = wch[it % 2]
                bb = bch[it % 2]
                b.dma(w, T["w_mod"][i, :, j * 512:(j + 1) * 512].rearrange("(k p) n -> p k n", p=128), q="pool")
                b.dma(bb, T["b_mod"][i:i + 1, j * 512:(j + 1) * 512].partition_broadcast(128))
                for gi, (grp, M, dst) in enumerate((("P", 128, T["modP"]), ("S", 32, T["modS"]))):
                    ps = b.bank("mm", [0, 1, 2, 3])
                    for k in range(8):
                        b.mm(ps[:M, :], cT[grp][:, k, :M], w[:, k, :], k == 0, k == 7)
                    o = och[(it * 2 + gi) % 4]
                    b.tt(o[:M, :], ps[:M, :], bb[:M, :], ALU.add)
                    if (j // 2) in (1, 2, 4, 5):
                        b.ts(o[:M, :], o[:M, :], 1.0, None, op0=ALU.add, eng="pool")
                    b.dma(dst[i, :, j * 512:(j + 1) * 512], o[:M, :])
                it += 1
    b.S.barrier()


def rope(b, out, ps, M, H, cs, t):
    x = ps[:M, 0:H * 64].rearrange("p (h d) -> p h d", h=H)
    o = out[:M, 0:H * 64].rearrange("p (h d) -> p h d", h=H)
    x1, x2 = x[:, :, 0:32], x[:, :, 32:64]
    cosb = cs[:M, 0:32].unsqueeze(1).to_broadcast([M, H, 32])
    sinb = cs[:M, 32:64].unsqueeze(1).to_broadcast([M, H, 32])
    ta = t[0][:M, 0:H * 32].rearrange("p (h d) -> p h d", h=H)
    tb = t[1][:M, 0:H * 32].rearrange("p (h d) -> p h d", h=H)
    b.tt(ta, x1, cosb, ALU.mult)
    b.tt(tb, x2, sinb, ALU.mult)
    b.tt(o[:, :, 0:32], ta, tb, ALU.subtract, eng="pool")
    b.tt(ta, x2, cosb, ALU.mult)
    b.tt(tb, x1, sinb, ALU.mult)
    b.tt(o[:, :, 32:64], ta, tb, ALU.add, eng="pool")


def phaseA(b, T, g, do_sample=True):
    nc = b.nc
    with ExitStack() as es:
        def al(name, shape, dt):
            return es.enter_context(nc.sbuf_tensor(name, list(shape), dt)).ap()
        w = al("pa_w", [128, 8, PAB], BF16)
        for (_, c0, c1, _) in CH:
            b.dma(w[:, :, c0:c1], T["w_in_ab"][:, c0:c1].rearrange("(k p) n -> p k n", p=128), q="pool")
        sc = {"P": al("pa_scP", [128, D], F32), "S": al("pa_scS", [32, D], F32)}
        sh = {"P": al("pa_shP", [128, D], F32), "S": al("pa_shS", [32, D], F32)}
        b.dma(sc["P"], T["modP"][0, :, D:2 * D]); b.dma(sh["P"], T["modP"][0, :, 0:D])
        b.dma(sc["S"], T["modS"][0, :, D:2 * D]); b.dma(sh["S"], T["modS"][0, :, 0:D])
        bfb = al("pa_bfb", [128, 8], F32)
        b.dma(bfb, T["b_forget"].partition_broadcast(128))
        xt = [al("pa_x%d" % i, [128, D], F32) for i in range(2)]
        cs = [al("pa_cs%d" % i, [128, 64], F32) for i in range(2)]
        hf = al("pa_hf", [128, D], F32)
        hb = al("pa_hb", [128, D], BF16)
        hT = [al("pa_hT%d" % i, [128, 8, 128], BF16) for i in range(2)]
        rt = [al("pa_rt%d" % i, [128, 256], F32) for i in range(2)]
        of = [al("pa_of%d" % i, [128, 512], F32) for i in range(3)]
        ob = [al("pa_ob%d" % i, [128, 512], BF16) for i in range(3)]
        oT = [al("pa_oT%d" % i, [128, 4, 128], BF16) for i in range(3)]
        vaug = [al("pa_va%d" % i, [128, 8, 72], BF16) for i in range(3)]
        for v in vaug:
            b.memset(v, 1.0)
        kid = al("pa_kid", [128, 128], BF16)
        sm = al("pa_sm", [128, 32], F32)
        cnt = {"of": 0, "ob": 0, "oT": 0, "va": 0}

        def nxt(lst, key):
            i = cnt[key]
            cnt[key] += 1
            return lst[i % len(lst)]

        blocks = [("P", p) for p in range(NKB)] + ([("S", 0)] if do_sample else [])
        for bi, (grp, p) in enumerate(blocks):
            M = 128 if grp == "P" else 32
            full = (grp == "S") or (p >= SLOT0)
            s = p - SLOT0
            x = xt[bi % 2]
            c = cs[bi % 2]
            if grp == "P":
                b.dma(x, T["xk"][p * 128:(p + 1) * 128, :])
                b.dma(c, T["cstab"][p * 128:(p + 1) * 128, :])
            else:
                b.dma(x[:M, :], T["xs"])
                b.dma(c[:M, :], T["cstab_s"])
            b.tt(hf[:M, :], x[:M, :], sc[grp][:M, :], ALU.mult)
            b.tt(hb[:M, :], hf[:M, :], sh[grp][:M, :], ALU.add)
            h_T = hT[bi % 2]
            b.transposes(h_T, hb, M, 8, g["identb"])
            rows = slice(p * 128, (p + 1) * 128)
            for (name, c0, c1, konly) in CH:
                if not (full or konly):
                    continue
                n = c1 - c0
                ps = b.bank("mm", [0, 1, 2, 3])
                for k in range(8):
                    b.mm(ps[:M, :n], h_T[:, k, :M], w[:, k, c0:c1], k == 0, k == 7)
                if name in ("qa", "qi", "qb"):
                    o_b = nxt(ob, "ob")
                    if name == "qb":
                        b.cp(o_b[:M, :], ps[:M, :512])
                    else:
                        rope(b, o_b, ps, M, 8, c, rt)
                    o_T = nxt(oT, "oT")
                    b.transposes(o_T, o_b, M, 4, g["identb"])
                    qi_ = {"qa": 0, "qi": 1, "qb": 2}[name]
                    if grp == "P":
                        b.dma(T[name + "T_s"][s], o_T)
                    else:
                        b.dma(T["sq_s"][qi_], o_T[:, :, 0:32])
                elif name in ("ka", "kb"):
                    o_f = nxt(of, "of")
                    if name == "ka":
                        rope(b, o_f, ps, M, 8, c, rt)
                    else:
                        b.cp(o_f[:M, :], ps[:M, :512])
                    o_b = nxt(ob, "ob")
                    b.cp(o_b[:M, :], o_f[:M, :], eng="pool")
                    if grp == "P":
                        b.dma(T["a" + "k_p" if name == "ka" else "bk_p"][rows, :], o_f)
                        o_T = nxt(oT, "oT")
                        b.transposes(o_T, o_b, M, 4, g["identb"])
                        b.dma(T[name + "T_s"][:, :, rows], o_T)
                    else:
                        b.dma(T["ak_s" if name == "ka" else "bk_s"], o_f[:M, :])
                        b.dma(T["sk_s"][0 if name == "ka" else 2][:, 0:512], o_b[:M, :])
                elif name in ("va", "vb"):
                    o_f = nxt(of, "of")
                    b.cp(o_f[:M, :], ps[:M, :512])
                    va = nxt(vaug, "va")
                    b.cp(va[:M, :, 0:64], o_f[:M, :].rearrange("p (h d) -> p h d", h=8), eng="pool")
                    if grp == "P":
                        b.dma(T["av_p" if name == "va" else "bv_p"][rows, :], o_f)
                        b.dma(T[name + "_s"][p], va.rearrange("p h d -> p (h d)"))
                    else:
                        b.dma(T["av_s" if name == "va" else "bv_s"], o_f[:M, :])
                        o_b = nxt(ob, "ob")
                        b.cp(o_b[:M, :], o_f[:M, :], eng="dve")
                        b.dma(T["sk_s"][1 if name == "va" else 3][:, 0:512], o_b[:M, :])
                elif name == "kiwi":
                    o_f = nxt(of, "of")
                    rope(b, o_f, ps, M, 1, c, rt)
                    b.cp(kid[:M, 0:64], o_f[:M, 0:64], eng="pool")
                    b.cp(kid[:M, 64:128], o_f[:M, 0:64], eng="pool")
                    o_T = nxt(oT, "oT")
                    b.transposes(o_T, kid, M, 1, g["identb"])
                    if grp == "P":
                        b.dma(T["aki_p"][rows, :], o_f[:, 0:64])
                        b.dma(T["kiT2_s"][:, rows], o_T[:, 0, :])
                    else:
                        b.dma(T["aki_s"], o_f[:M, 0:64])
                        b.dma(T["skiT_s"], o_T[:, 0, 0:32])
                    if full:
                        b.ts(sm[:M, 0:8], ps[:M, 64:72], WI_SCALE, None, op0=ALU.mult)
                        b.dma(T["wi_s"][s] if grp == "P" else T["swi_s"], sm[:M, 0:8])
                elif name == "fb":
                    b.tt(sm[:M, 8:16], ps[:M, 0:8], bfb[:M, :], ALU.add)
                    b.act(sm[:M, 16:24], sm[:M, 8:16], AF.Exp, scale=-1.0)
                    b.act(sm[:M, 24:32], sm[:M, 16:24], AF.Ln, bias=g["ones_f"][:M, 0:1], scale=1.0)
                    if grp == "P":
                        b.ts(g["LF"][:, p, :], sm[:, 24:32], -1.0, None, op0=ALU.mult)
                        b.dma(T["blf_p"][rows, :], g["LF"][:, p, :])
                    else:
                        b.ts(sm[:M, 8:16], sm[:M, 24:32], -1.0, None, op0=ALU.mult)
                        b.dma(T["blf_s"], sm[:M, 8:16])
                        b.dma(T["slf_s"], sm[:M, 8:16])
    b.S.barrier()


GROUPS = [[0, 1], [2, 3, 4, 5], [6, 7, 8, 9], [10, 11, 12, 13], [14, 15, 16, 17]]


def residual_ln(b, al_tiles, ps_list, xt, M, gp1, lng, lnb, out_dram, extra=None):
    acc, tmp, stats, mv, res = al_tiles
    for n2, ps in enumerate(ps_list):
        cols = slice(n2 * 512, (n2 + 1) * 512)
        if extra is not None:
            b.tt(acc[:M, cols], ps[:M, :512], extra[:M, cols], ALU.add)
            b.tt(acc[:M, cols], acc[:M, cols], gp1[:M, cols], ALU.mult, eng="pool")
        else:
            b.tt(acc[:M, cols], ps[:M, :512], gp1[:M, cols], ALU.mult)
    b.stt(tmp[:M, :], xt[:M, :], ALPHA, acc[:M, :], ALU.mult, ALU.add)
    b.layernorm_affine(res, tmp, M, D, lng, lnb, acc, stats, mv)
    b.dma(out_dram, res[:M, :])


def ln_tiles(al, pfx):
    return (al(pfx + "_acc", [128, D], F32), al(pfx + "_tmp", [128, D], F32), al(pfx + "_st", [128, 4, 6], F32),
            al(pfx + "_mv", [128, 4], F32), al(pfx + "_res", [128, D], F32))


def phaseB1(b, T, g):
    nc = b.nc
    with ExitStack() as es:
        def al(name, shape, dt):
            return es.enter_context(nc.sbuf_tensor(name, list(shape), dt)).ap()
        kaT = al("b1_kaT", [128, 4, NKB * 128], BF16)
        for i in range(4):
            b.dma(kaT[:, i, :], T["kaT_s"][:, i, :])
        kiT2 = al("b1_kiT2", [128, NKB * 128], BF16)
        b.dma(kiT2, T["kiT2_s"])
        vaA = al("b1_vaA", [128, NKB, 576], BF16)
        for i in range(4):
            b.dma(vaA[:, 8 * i:8 * i + 8, :], T["va_s"][8 * i:8 * i + 8].rearrange("j p f -> p j f"))
        JM = al("b1_JM", [128, NKB * 128], F32)
        b.dma(JM, T["jm"].partition_broadcast(128))
        qaT = [al("b1_qa%d" % i, [128, 4, 128], BF16) for i in range(2)]
        qiT = [al("b1_qi%d" % i, [128, 4, 128], BF16) for i in range(2)]
        wi = [al("b1_wi%d" % i, [128, 8], F32) for i in range(2)]
        score = al("b1_score", [128, NKB * 128], F32)
        junk = al("b1_junk", [128, NKB * 128], BF16)
        sel = al("b1_sel", [128, NKB * 128], BF16)
        selT = al("b1_selT", [128, NKB, 128], BF16)
        rr = [al("b1_r%d" % i, [128, 512], F32) for i in range(3)]
        sm = al("b1_sm", [128, 8 + NBIS], F32)
        halfs = sm[:, 8:8 + NBIS]
        pT = [al("b1_pT%d" % i, [128, 4, 128], BF16) for i in range(3)]
        pTm = [al("b1_pTm%d" % i, [128, 4, 128], BF16) for i in range(3)]
        rc = al("b1_rc", [128, 8], F32)
        onb = al("b1_onb", [128, 512], BF16)
        oT = [al("b1_oT%d" % i, [128, 4, 128], BF16) for i in range(2)]
        it = 0
        dbg_slots, dbg_stage = NS, 9
        for s in range(dbg_slots):
            nkb = SLOT0 + 1 + s
            NK = nkb * 128
            diag = slice((nkb - 1) * 128, nkb * 128)
            qa, qi, w_ = qaT[s % 2], qiT[s % 2], wi[s % 2]
            b.dma(qa, T["qaT_s"][s]); b.dma(qi, T["qiT_s"][s]); b.dma(w_, T["wi_s"][s])
            for cix in range((nkb + 3) // 4):
                nblk = min(4, nkb - 4 * cix)
                n = nblk * 128
                cols = slice(512 * cix, 512 * cix + n)
                for h in range(8):
                    hp, base = h // 2, 64 * (h % 2)
                    ps = b.bank("mm", [0, 1, 2, 3])
                    b.mm(ps[:, :n], qi[base:base + 64, hp, :], kiT2[base:base + 64, cols], True, True)
                    r = rr[it % 3]
                    it += 1
                    b.act(r[:, :n], ps[:, :n], AF.Relu)
                    if h == 0:
                        b.ts(score[:, cols], r[:, :n], w_[:, 0:1], None, op0=ALU.mult)
                    else:
                        b.stt(score[:, cols], r[:, :n], w_[:, h:h + 1], score[:, cols], ALU.mult, ALU.add)
            if dbg_stage < 2:
                continue
            b.reduce(sm[:, 0:1], score[:, :NK], ALU.max, absval=True)
            b.ts(sm[:, 1:2], sm[:, 0:1], 1.0001, 1e-20, op0=ALU.mult, op1=ALU.add)
            b.ts(halfs, g["pow2"], sm[:, 1:2], None, op0=ALU.mult)
            b.tt(score[:, :NK], score[:, :NK], JM[:, :NK], ALU.add, eng="pool")
            b.tt(score[:, diag], score[:, diag], g["triqk"], ALU.add, eng="pool")
            b.memset(sm[:, 2:3], 0.0, eng="dve")
            for k in range(NBIS):
                b.ts(junk[:, :NK], score[:, :NK], sm[:, 2:3], None, op0=ALU.is_ge, op1=ALU.add, accum=sm[:, 3:4])
                if k < NBIS - 1:
                    b.ts(sm[:, 4:5], sm[:, 3:4], 255.5, -0.5, op0=ALU.is_ge, op1=ALU.add)
                    b.stt(sm[:, 2:3], sm[:, 4:5], halfs[:, k:k + 1], sm[:, 2:3], ALU.mult, ALU.add)
                else:
                    b.ts(sm[:, 4:5], sm[:, 3:4], 255.5, -1.0, op0=ALU.is_ge, op1=ALU.add)
                    b.stt(sm[:, 5:6], sm[:, 4:5], halfs[:, k:k + 1], sm[:, 2:3], ALU.mult, ALU.add)
            if dbg_stage < 3:
                continue
            b.ts(sel[:, :NK], score[:, :NK], sm[:, 5:6], None, op0=ALU.is_ge)
            b.transposes(selT, sel, 128, nkb, g["identb"], evac="act")
            if dbg_stage < 4:
                continue
            pbO = [b.pb[6].rearrange("p (h d) -> p h d", h=4), b.pb[7].rearrange("p (h d) -> p h d", h=4)]
            for j in range(nkb):
                kc = slice(j * 128, (j + 1) * 128)
                for par in range(2):
                    ps = b.bank("mm", [0, 1, 2, 3]).rearrange("p (h q) -> p h q", h=4)
                    base = 64 * par
                    for idx in range(4):
                        b.mm(ps[:, idx, :], kaT[base:base + 64, idx, kc], qa[base:base + 64, idx, :], True, True)
                    p_, pm = pT[it % 3], pTm[it % 3]
                    b.act(p_, ps, AF.Exp, scale=0.125)
                    b.tt(pm, p_, selT[:, j, :].unsqueeze(1).to_broadcast([128, 4, 128]), ALU.mult,
                         eng=("pool" if it % 2 else "dve"))
                    it += 1
                    for idx in range(4):
                        h = 2 * idx + par
                        b.mm(pbO[h // 4][:, h % 4, 0:68], pm[:, idx, :], vaA[:, j, h * 72:h * 72 + 68],
                             j == 0 and par == 0 and idx in (0, 2), j == nkb - 1)
            for hg in range(2):
                b.ts(rc[:, 4 * hg:4 * hg + 4], pbO[hg][:, :, 64], 1e-30, None, op0=ALU.max)
            b.recip(rc, rc)
            for hg in range(2):
                b.tt(onb[:, hg * 256:(hg + 1) * 256].rearrange("p (h d) -> p h d", h=4), pbO[hg][:, :, 0:64],
                     rc[:, 4 * hg:4 * hg + 4].unsqueeze(2).to_broadcast([128, 4, 64]), ALU.mult)
            o_T = oT[s % 2]
            b.transposes(o_T, onb, 128, 4, g["identb"])
            b.dma(T["mixTa_s"][s], o_T)
    b.S.barrier()


def cumsum_blocks(b, g, al, LF, nblk, pfx):
    n = nblk * 8
    ck = al(pfx + "_ck", [128, nblk, 8], F32)
    ta = al(pfx + "_ta", [128, nblk, 8], F32)
    tb = al(pfx + "_tb", [128, nblk, 8], F32)
    tot = al(pfx + "_tot", [128, nblk, 8], F32)
    lf2 = LF.rearrange("p j h -> p (j h)")
    ps1 = b.bank("mm", [0, 1, 2, 3])
    b.mm(ps1[:, :n], g["tinc"], lf2, True, True)
    ps2 = b.bank("mm", [0, 1, 2, 3])
    b.mm(ps2[:, :n], g["ones_f"], lf2, True, True)
    b.cp(tot.rearrange("p j h -> p (j h)"), ps2[:, :n])
    b.cp(ta, tot, eng="dve")
    d = 1
    cur, oth = ta, tb
    while d < nblk:
        b.tt(oth[:, d:, :], cur[:, d:, :], cur[:, :nblk - d, :], ALU.add)
        b.cp(oth[:, :d, :], cur[:, :d, :], eng="dve")
        cur, oth = oth, cur
        d *= 2
    incl = cur
    b.tt(oth, incl, tot, ALU.subtract)
    b.tt(ck.rearrange("p j h -> p (j h)"), ps1[:, :n], oth.rearrange("p j h -> p (j h)"), ALU.add)
    return ck, incl


def phaseB2(b, T, g):
    nc = b.nc
    with ExitStack() as es:
        def al(name, shape, dt):
            return es.enter_context(nc.sbuf_tensor(name, list(shape), dt)).ap()
        kbT = al("b2_kbT", [128, 4, NKB * 128], BF16)
        for i in range(4):
            b.dma(kbT[:, i, :], T["kbT_s"][:, i, :])
        vbA = al("b2_vbA", [128, NKB, 576], BF16)
        for i in range(4):
            b.dma(vbA[:, 8 * i:8 * i + 8, :], T["vb_s"][8 * i:8 * i + 8].rearrange("j p f -> p j f"))
        wout = al("b2_wout", [128, 8, D], BF16)
        for i in range(2):
            b.dma(wout[:, :, i * 512:(i + 1) * 512],
                  T["w_out_ab"][:, i * 512:(i + 1) * 512].rearrange("(k p) n -> p k n", p=128), q="pool")
        gp1 = al("b2_gp1", [128, D], F32); b.dma(gp1, T["modP"][0, :, 2 * D:3 * D])
        lng = al("b2_lng", [128, D], F32); b.dma(lng, T["ln1_g"][0:1, :].partition_broadcast(128))
        lnb = al("b2_lnb", [128, D], F32); b.dma(lnb, T["ln1_b"][0:1, :].partition_broadcast(128))
        ck, incl = cumsum_blocks(b, g, al, g["LF"], NKB, "b2")
        nck = al("b2_nck", [128, NKB, 8], F32)
        b.tt(nck, g["blkbias"].unsqueeze(2).to_broadcast([128, NKB, 8]), ck, ALU.subtract)
        bias = [al("b2_bias%d" % i, [128, NKB, 8], F32) for i in range(2)]
        qbT = [al("b2_qb%d" % i, [128, 4, 128], BF16) for i in range(2)]
        mTa = [al("b2_mTa%d" % i, [128, 4, 128], BF16) for i in range(2)]
        xt = [al("b2_x%d" % i, [128, D], F32) for i in range(2)]
        pT = [al("b2_pT%d" % i, [128, 4, 128], BF16) for i in range(3)]
        rc = al("b2_rc", [128, 8], F32)
        onb = al("b2_onb", [128, 512], BF16)
        oT = [al("b2_oT%d" % i, [128, 4, 128], BF16) for i in range(2)]
        lt = ln_tiles(al, "b2")
        it = 0
        for s in range(NS):
            nkb = SLOT0 + 1 + s
            qb, bs, mta, x = qbT[s % 2], bias[s % 2], mTa[s % 2], xt[s % 2]
            b.dma(qb, T["qbT_s"][s]); b.dma(mta, T["mixTa_s"][s])
            b.dma(x, T["xk"][(SLOT0 + s) * 128:(SLOT0 + s + 1) * 128, :])
            b.tt(bs[:, :nkb, :], nck[:, :nkb, :], incl[:, nkb - 1, :].unsqueeze(1).to_broadcast([128, nkb, 8]), ALU.add)
            pbO = [b.pb[6].rearrange("p (h d) -> p h d", h=4), b.pb[7].rearrange("p (h d) -> p h d", h=4)]
            for j in range(nkb):
                kc = slice(j * 128, (j + 1) * 128)
                for par in range(2):
                    ps = b.bank("mm", [0, 1, 2, 3]).rearrange("p (h q) -> p h q", h=4)
                    base = 64 * par
                    for idx in range(4):
                        b.mm(ps[:, idx, :], kbT[base:base + 64, idx, kc], qb[base:base + 64, idx, :], True, True)
                    p_ = pT[it % 3]
                    it += 1
                    for idx in range(4):
                        h = 2 * idx + par
                        b.act(p_[:, idx, :], ps[:, idx, :], AF.Exp, bias=bs[:, j, h:h + 1], scale=0.125)
                    if j == nkb - 1:
                        b.tt(p_, p_, g["triT01"].unsqueeze(1).to_broadcast([128, 4, 128]), ALU.mult)
                    for idx in range(4):
                        h = 2 * idx + par
                        b.mm(pbO[h // 4][:, h % 4, 0:68], p_[:, idx, :], vbA[:, j, h * 72:h * 72 + 68],
                             j == 0 and par == 0 and idx in (0, 2), j == nkb - 1)
            for hg in range(2):
                b.ts(rc[:, 4 * hg:4 * hg + 4], pbO[hg][:, :, 64], 1e-30, None, op0=ALU.max)
            b.recip(rc, rc)
            for hg in range(2):
                b.tt(onb[:, hg * 256:(hg + 1) * 256].rearrange("p (h d) -> p h d", h=4), pbO[hg][:, :, 0:64],
                     rc[:, 4 * hg:4 * hg + 4].unsqueeze(2).to_broadcast([128, 4, 64]), ALU.mult)
            o_T = oT[s % 2]
            b.transposes(o_T, onb, 128, 4, g["identb"])
            pss = []
            for n2 in range(2):
                ps = b.bank("mm", [0, 1, 2, 3])
                for kc_ in range(8):
                    lhs = mta[:, kc_, :] if kc_ < 4 else o_T[:, kc_ - 4, :]
                    b.mm(ps[:, :512], lhs, wout[:, kc_, n2 * 512:(n2 + 1) * 512], kc_ == 0, kc_ == 7)
                pss.append(ps)
            residual_ln(b, lt, pss, x, 128, gp1, lng, lnb, T["x1_s"][s * 128:(s + 1) * 128, :])
    b.S.barrier()


def to_token_major(b, g, dst_rows, src, nrow, nch, al_row):
    row = al_row
    done = 0
    while done < nch:
        n = min(4, nch - done)
        ps = b.bank("mm", [0, 1, 2, 3])
        for k in range(n):
            b.tr(ps[:nrow, k * 128:(k + 1) * 128], src[:, done + k, :], g["identf"])
        b.cp(row[:nrow, done * 128:(done + n) * 128], ps[:nrow, :n * 128])
        done += n
    b.dma(dst_rows, row[:nrow, :])


def phaseC(b, T, g, li, xin_p, xout_p, xin_s, xout_s, do_sample=True):
    nc = b.nc
    with ExitStack() as es0:
        def al0(name, shape, dt):
            return es0.enter_context(nc.sbuf_tensor(name, list(shape), dt)).ap()
        pfx = "c%d" % li
        lastP = al0(pfx + "_lastP", [128, 44, 2], F32)
        lastS = al0(pfx + "_lastS", [128, 44, 8], F32)
        wcT = al0(pfx + "_wcT", [128, 44, 4], F32)
        stT = al0(pfx + "_stT", [128, 44, 8], F32)
        with ExitStack() as es:
            wrow = es.enter_context(nc.sbuf_tensor(pfx + "_wrow", [8, DUP], F32)).ap()
            b.dma(wrow[0:3, :], T["w_conv"][li])
            b.dma(wrow[3:4, :], T["b_conv"][li:li + 1, :])
            ps = b.bank("mm", [0, 1, 2, 3])
            psv = ps[:, 0:176].rearrange("p (c k) -> p c k", k=4)
            for ch in range(44):
                b.tr(psv[:, ch, :], wrow[0:4, ch * 128:(ch + 1) * 128], g["identf"][0:4, 0:4])
            b.cp(wcT, psv)
            if do_sample:
                b.dma(wrow[0:8, :], T["state_conv"][li])
                ps = b.bank("mm", [0, 1, 2, 3])
                psv = ps[:, 0:352].rearrange("p (c k) -> p c k", k=8)
                for ch in range(44):
                    b.tr(psv[:, ch, :], wrow[0:8, ch * 128:(ch + 1) * 128], g["identf"][0:8, 0:8])
                b.cp(stT, psv)
        b.S.barrier()
        b.memset(lastP, 0.0)
        for fh in range(2):
            with ExitStack() as es:
                def al(name, shape, dt):
                    return es.enter_context(nc.sbuf_tensor(name, list(shape), dt)).ap()
                p2 = pfx + "p%d" % fh
                wug = al(p2 + "_wug", [128, 8, 1408], BF16)
                wuv = al(p2 + "_wuv", [128, 8, 1408], BF16)
                wdn = al(p2 + "_wdn", [128, 11, D], BF16)
                for j in range(2):
                    cs_ = slice(1408 * fh + 704 * j, 1408 * fh + 704 * (j + 1))
                    b.dma(wug[:, :, 704 * j:704 * (j + 1)], T["w_up"][li, :, cs_].rearrange("(k p) n -> p k n", p=128), q="pool")
                    cs_ = slice(DFF + 1408 * fh + 704 * j, DFF + 1408 * fh + 704 * (j + 1))
                    b.dma(wuv[:, :, 704 * j:704 * (j + 1)], T["w_up"][li, :, cs_].rearrange("(k p) n -> p k n", p=128), q="pool")
                b.dma(wdn, T["w_down"][li, 1408 * fh:1408 * (fh + 1), :].rearrange("(k p) n -> p k n", p=128), q="pool")
                bc = {}
                for grp, M, src in (("P", 128, T["modP"]), ("S", 32, T["modS"])):
                    if grp == "S" and not do_sample:
                        continue
                    bc[grp] = {}
                    for nm, c0 in (("sh", 3), ("sc", 4)) + ((("g", 5),) if fh == 1 else ()):
                        t_ = al(p2 + "_%s%s" % (nm, grp), [M, D], F32)
                        b.dma(t_, src[li, :, c0 * D:(c0 + 1) * D])
                        bc[grp][nm] = t_
                if fh == 1:
                    lng = al(p2 + "_lng", [128, D], F32); b.dma(lng, T["ln2_g"][li:li + 1, :].partition_broadcast(128))
                    lnb = al(p2 + "_lnb", [128, D], F32); b.dma(lnb, T["ln2_b"][li:li + 1, :].partition_broadcast(128))
                    lt = ln_tiles(al, p2)
                    ffp = al(p2 + "_ffp", [128, D], F32)
                else:
                    ffo = [al(p2 + "_ffo%d" % i, [128, D], F32) for i in range(2)]
                xt = [al(p2 + "_x%d" % i, [128, D], F32) for i in range(4)]
                hf = al(p2 + "_hf", [128, D], F32)
                hb = al(p2 + "_hb", [128, D], BF16)
                h2T = al(p2 + "_h2T", [128, 8, 512], BF16)
                gT = al(p2 + "_gT", [128, 11, 512], BF16)
                ut = [al(p2 + "_u%d" % i, [128, 520], F32) for i in range(2)]
                yt = [al(p2 + "_y%d" % i, [128, 512], F32) for i in range(2)]
                gl = al(p2 + "_gl", [128, 512], F32)
                halo = al(p2 + "_halo", [128, 22, 2], F32)
                b.memset(halo, 0.0)
                groups = [("P", sl) for sl in GROUPS] + ([("S", None)] if do_sample else [])
                for gi, (grp, slots) in enumerate(groups):
                    if grp == "P":
                        M, nsl, nseg, L = 128, len(slots), 1, 128 * len(slots)
                    else:
                        M, nsl, nseg, L = 32, 1, 4, 8
                    Mtot = M * nsl
                    for si in range(nsl):
                        x = xt[si]
                        if grp == "P":
                            b.dma(x, xin_p[slots[si] * 128:(slots[si] + 1) * 128, :])
                        else:
                            b.dma(x[:M, :], xin_s)
                        b.tt(hf[:M, :], x[:M, :], bc[grp]["sc"][:M, :], ALU.mult)
                        b.tt(hb[:M, :], hf[:M, :], bc[grp]["sh"][:M, :], ALU.add, eng="pool")
                        b.transposes(h2T[:, :, si * M:(si + 1) * M], hb, M, 8, g["identb"])
                    for gcl in range(11):
                        ys = []
                        for wi_, (wsrc, choff) in enumerate(((wug, 0), (wuv, 22))):
                            ch = choff + 11 * fh + gcl
                            hch = 11 * wi_ + gcl
                            ps = b.bank("mm", [0, 1, 2, 3])
                            for k in range(8):
                                b.mm(ps[:, :Mtot], wsrc[:, k, gcl * 128:(gcl + 1) * 128], h2T[:, k, :Mtot], k == 0, k == 7)
                            u = ut[wi_][:, 0:nseg * (L + 2)].rearrange("p (s l) -> p s l", s=nseg)
                            b.cp(u[:, :, 2:2 + L], ps[:, :Mtot].rearrange("p (s l) -> p s l", s=nseg))
                            if grp == "P":
                                if gi == 1:
                                    b.ts(u[:, 0, 0:2], halo[:, hch, :], g["haloflag"][:, 0:1], None, op0=ALU.mult, eng="pool")
                                else:
                                    b.cp(u[:, 0, 0:2], halo[:, hch, :], eng="pool")
                                b.cp(halo[:, hch, :], u[:, 0, L:L + 2], eng="pool")
                                if gi == len(GROUPS) - 1:
                                    b.cp(lastP[:, ch, :], u[:, 0, L:L + 2], eng="pool")
                            else:
                                b.cp(u[:, :, 0:2], stT[:, ch, :].rearrange("p (s r) -> p s r", s=4), eng="pool")
                                b.cp(lastS[:, ch, :].rearrange("p (s r) -> p s r", s=4), u[:, :, L:L + 2], eng="pool")
                            y = yt[wi_][:, 0:nseg * L].rearrange("p (s l) -> p s l", s=nseg)
                            b.ts(y, u[:, :, 0:L], wcT[:, ch, 0:1], wcT[:, ch, 3:4], op0=ALU.mult, op1=ALU.add)
                            b.stt(y, u[:, :, 1:L + 1], wcT[:, ch, 1:2], y, ALU.mult, ALU.add)
                            b.stt(y, u[:, :, 2:L + 2], wcT[:, ch, 2:3], y, ALU.mult, ALU.add)
                            ys.append(yt[wi_][:, 0:Mtot])
                            if "dbg_u" in T and li == 0 and fh == 0 and gi == 1 and gcl == 0 and wi_ == 0:
                                b.dma(T["dbg_u"], ut[wi_])
                                b.dma(T["dbg_y"], yt[wi_])
                                b.dma(T["dbg_w"], wcT.rearrange("p c k -> p (c k)"))
                        b.act(gl[:, :Mtot], ys[0], AF.Gelu)
                        b.tt(gT[:, gcl, :Mtot], gl[:, :Mtot], ys[1], ALU.mult, eng="pool")
                    for si in range(nsl):
                        tok = slice(si * M, (si + 1) * M)
                        pss = []
                        for n2 in range(2):
                            ps = b.bank("mm", [0, 1, 2, 3])
                            for gcl in range(11):
                                b.mm(ps[:M, :512], gT[:, gcl, tok], wdn[:, gcl, n2 * 512:(n2 + 1) * 512], gcl == 0, gcl == 10)
                            pss.append(ps)
                        if grp == "P":
                            rows = slice(slots[si] * 128, (slots[si] + 1) * 128)
                            fdst, xdst = T["ffp_s"][rows, :], xout_p[rows, :]
                        else:
                            fdst, xdst = T["ffps_s"], xout_s
                        if fh == 0:
                            fo = ffo[si % 2]
                            for n2 in range(2):
                                b.cp(fo[:M, n2 * 512:(n2 + 1) * 512], pss[n2][:M, :512])
                            b.dma(fdst, fo[:M, :])
                        else:
                            b.dma(ffp[:M, :], fdst)
                            residual_ln(b, lt, pss, xt[si], M, bc[grp]["g"], lng, lnb, xdst, extra=ffp)
            b.S.barrier()
        with ExitStack() as es:
            row = es.enter_context(nc.sbuf_tensor(pfx + "_row", [8, DUP], F32)).ap()
            to_token_major(b, g, T["conv_p"][li], lastP, 2, 44, row)
            if do_sample:
                to_token_major(b, g, T["conv_s"][li], lastS, 8, 44, row)
    b.S.barrier()


def phaseD(b, T, g, do_sample=True):
    nc = b.nc
    with ExitStack() as es:
        def al(name, shape, dt):
            return es.enter_context(nc.sbuf_tensor(name, list(shape), dt)).ap()
        w = al("d1_w", [128, 8, 2 * DCG], BF16)
        for j in range(8):
            b.dma(w[:, :, j * 512:(j + 1) * 512], T["w_in_c"][:, j * 512:(j + 1) * 512].rearrange("(k p) n -> p k n", p=128), q="pool")
        wsf = al("d1_wsf", [128, 8, 128], F32)
        b.dma(wsf, T["w_spatial"].rearrange("g t s -> t g s"))
        wsb = al("d1_wsb", [128, 8, 128], BF16)
        b.cp(wsb, wsf, eng="dve")
        wsT = al("d1_wsT", [128, 8, 128], BF16)
        b.transposes(wsT, wsb.rearrange("p g s -> p (g s)"), 128, 8, g["identb"])
        b.tt(wsT, wsT, g["triT01"].unsqueeze(1).to_broadcast([128, 8, 128]), ALU.mult)
        bsp = al("d1_bsp", [128, 8, 128], F32)
        b.dma(bsp.rearrange("p g t -> p (g t)"), T["b_spatial"].rearrange("g t -> (g t)").unsqueeze(0).partition_broadcast(128))
        lvg = al("d1_lvg", [128, DCG], F32); b.dma(lvg, T["lnv_g"].partition_broadcast(128))
        lvb = al("d1_lvb", [128, DCG], F32); b.dma(lvb, T["lnv_b"].partition_broadcast(128))
        bc = {}
        for grp, M, src in (("P", 128, T["modP"]), ("S", 32, T["modS"])):
            if grp == "S" and not do_sample:
                continue
            bc[grp] = {}
            for nm, c0 in (("sh", 0), ("sc", 1)):
                t_ = al("d1_%s%s" % (nm, grp), [M, D], F32)
                b.dma(t_, src[1, :, c0 * D:(c0 + 1) * D])
                bc[grp][nm] = t_
        if do_sample:
            wsS = al("d1_wsS", [32, 8, 32], BF16)
            b.memset(wsS, 0.0)
            for i in range(4):
                b.dma(wsS[i * 8:(i + 1) * 8, :, i * 8:(i + 1) * 8], wsT[0:8, :, 0:8])
            bspS = al("d1_bspS", [128, 8, 32], F32)
            for i in range(4):
                b.dma(bspS[:, :, i * 8:(i + 1) * 8], T["b_spatial"][:, 0:8].unsqueeze(0).partition_broadcast(128))
        xt = al("d1_x", [128, D], F32)
        hf = al("d1_hf", [128, D], F32)
        hb = al("d1_hb", [128, D], BF16)
        hT = al("d1_hT", [128, 8, 512], BF16)
        uT = al("d1_uT", [128, 16, 512], BF16)
        vf = al("d1_vf", [128, DCG], F32)
        vt = al("d1_vt", [128, DCG], F32)
        vln = al("d1_vln", [128, DCG], F32)
        vlb = al("d1_vlb", [128, DCG], BF16)
        st = al("d1_st", [128, 4, 6], F32)
        mv = al("d1_mv", [128, 4], F32)
        mx = [al("d1_mx%d" % i, [128, 4, 128], F32) for i in range(2)]
        gTt = [al("d1_gT%d" % i, [128, 16, 128], BF16) for i in range(2)]
        groups = [("P", sl) for sl in GROUPS] + ([("S", None)] if do_sample else [])
        it = 0
        for gi, (grp, slots) in enumerate(groups):
            M = 128 if grp == "P" else 32
            nsl = len(slots) if grp == "P" else 1
            Mtot = M * nsl
            for si in range(nsl):
                if grp == "P":
                    b.dma(xt, T["x2_s"][slots[si] * 128:(slots[si] + 1) * 128, :])
                else:
                    b.dma(xt[:M, :], T["xs2_s"])
                b.tt(hf[:M, :], xt[:M, :], bc[grp]["sc"][:M, :], ALU.mult)
                b.tt(hb[:M, :], hf[:M, :], bc[grp]["sh"][:M, :], ALU.add, eng="pool")
                b.transposes(hT[:, :, si * M:(si + 1) * M], hb, M, 8, g["identb"])
            for cc in range(16):
                ps = b.bank("mm", [0, 1, 2, 3])
                for k in range(8):
                    b.mm(ps[:, :Mtot], w[:, k, cc * 128:(cc + 1) * 128], hT[:, k, :Mtot], k == 0, k == 7)
                b.act(uT[:, cc, :Mtot], ps[:, :Mtot], AF.Gelu)
            for si in range(nsl):
                tok = slice(si * M, (si + 1) * M)
                for c4 in range(4):
                    ps = b.bank("mm", [0, 1, 2, 3])
                    for k in range(8):
                        b.mm(ps[:M, :512], hT[:, k, tok], w[:, k, DCG + c4 * 512:DCG + (c4 + 1) * 512], k == 0, k == 7)
                    b.act(vf[:M, c4 * 512:(c4 + 1) * 512], ps[:M, :512], AF.Gelu)
                b.layernorm_affine(vln, vf, M, DCG, lvg, lvb, vt, st, mv)
                b.cp(vlb[:M, :], vln[:M, :], eng="dve")
                if grp == "S":
                    b.dma(T["cv_s"], vln[:M, :])
                gt = gTt[it % 2]
                it += 1
                for c4 in range(4):
                    ps = b.bank("mm", [0, 1, 2, 3])
                    psv = ps[:, 0:4 * M].rearrange("p (c t) -> p c t", c=4)
                    for k in range(4):
                        cc = 4 * c4 + k
                        rhs = wsT[:, cc // 2, :] if grp == "P" else wsS[:, cc // 2, :]
                        b.mm(psv[:, k, :], vlb[:M, cc * 128:(cc + 1) * 128], rhs, True, True)
                    m_ = mx[c4 % 2]
                    bs_ = bsp if grp == "P" else bspS
                    b.tt(m_[:, :, :M].rearrange("p (a c) t -> p a c t", a=2), psv.rearrange("p (a c) t -> p a c t", a=2),
                         bs_[:, 2 * c4:2 * c4 + 2, :].unsqueeze(2).to_broadcast([128, 2, 2, M]), ALU.add)
                    b.tt(gt[:, 4 * c4:4 * c4 + 4, :M], m_[:, :, :M], uT[:, 4 * c4:4 * c4 + 4, tok], ALU.mult, eng="pool")
                if grp == "P":
                    b.dma(T["gT_s"][slots[si]], gt)
                else:
                    b.dma(T["gTs_s"], gt[:, :, 0:32])
    b.S.barrier()
    with ExitStack() as es:
        def al(name, shape, dt):
            return es.enter_context(nc.sbuf_tensor(name, list(shape), dt)).ap()
        wo = al("d2_wo", [128, 16, D], BF16)
        for j in range(4):
            b.dma(wo[:, 4 * j:4 * j + 4, :], T["w_out_c"][512 * j:512 * (j + 1), :].rearrange("(k p) n -> p k n", p=128), q="pool")
        gp = {"P": al("d2_gP", [128, D], F32)}
        b.dma(gp["P"], T["modP"][1, :, 2 * D:3 * D])
        if do_sample:
            gp["S"] = al("d2_gS", [32, D], F32)
            b.dma(gp["S"], T["modS"][1, :, 2 * D:3 * D])
        lng = al("d2_lng", [128, D], F32); b.dma(lng, T["ln1_g"][1:2, :].partition_broadcast(128))
        lnb = al("d2_lnb", [128, D], F32); b.dma(lnb, T["ln1_b"][1:2, :].partition_broadcast(128))
        lt = ln_tiles(al, "d2")
        xt = [al("d2_x%d" % i, [128, D], F32) for i in range(2)]
        gtt = [al("d2_g%d" % i, [128, 16, 128], BF16) for i in range(2)]
        units = [("P", s) for s in range(NS)] + ([("S", 0)] if do_sample else [])
        for ui, (grp, s) in enumerate(units):
            M = 128 if grp == "P" else 32
            x, gt = xt[ui % 2], gtt[ui % 2]
            if grp == "P":
                b.dma(x, T["x2_s"][s * 128:(s + 1) * 128, :]); b.dma(gt, T["gT_s"][s])
                dst = T["x3_s"][s * 128:(s + 1) * 128, :]
            else:
                b.dma(x[:M, :], T["xs2_s"]); b.dma(gt[:, :, 0:32], T["gTs_s"])
                dst = T["xs3_s"]
            pss = []
            for n2 in range(2):
                ps = b.bank("mm", [0, 1, 2, 3])
                for cc in range(16):
                    b.mm(ps[:M, :512], gt[:, cc, :M], wo[:, cc, n2 * 512:(n2 + 1) * 512], cc == 0, cc == 15)
                pss.append(ps)
            residual_ln(b, lt, pss, x, M, gp[grp], lng, lnb, dst)
    b.S.barrier()


def build_prompt_only(nc, upto=99):
    b = B(nc)
    T = declare(b, with_sample=False)
    g = consts(b, T)
    phase0(b, T, g)
    phaseA(b, T, g, do_sample=False)
    if upto >= 1:
        phaseB1(b, T, g)
    if upto >= 2:
        phaseB2(b, T, g)
    if upto >= 3:
        phaseC(b, T, g, 0, T["x1_s"], T["x2_s"], None, None, do_sample=False)
    if upto >= 4:
        phaseD(b, T, g, do_sample=False)
    if upto >= 5:
        phaseC(b, T, g, 1, T["x3_s"], T["y_p"], None, None, do_sample=False)
    return b, T


def phaseSA(b, T, g):
    nc = b.nc
    NPK = NPG * 128
    with ExitStack() as es:
        def al(name, shape, dt):
            return es.enter_context(nc.sbuf_tensor(name, list(shape), dt)).ap()
        ptb = al("sa_ptb", [128, 4 * NPG], I32)
        b.dma(ptb, T["page_table"].partition_broadcast(128))
        io = al("sa_io", [128, 1], I32)
        b.dma(io, T["iota128"])
        idx = al("sa_idx", [128, 4 * NPG], I32)
        b.ts(idx, ptb, 128.0, io[:, 0:1], op0=ALU.mult, op1=ALU.add)
        qT = [al("sa_q%d" % k, [128, 4, 32], BF16) for k in range(3)]
        for k in range(3):
            b.dma(qT[k], T["sq_s"][k])
        newtok = al("sa_newtok", [32, 4, 512], BF16)
        for k in range(4):
            b.dma(newtok[:, k, :], T["sk_s"][k][:, 0:512])
        kaTn = al("sa_kaTn", [128, 4, 32], BF16)
        kbTn = al("sa_kbTn", [128, 4, 32], BF16)
        b.transposes(kaTn, newtok[:, 0, :], 32, 4, g["identb"])
        b.transposes(kbTn, newtok[:, 2, :], 32, 4, g["identb"])
        skiT = al("sa_skiT", [128, 32], BF16); b.dma(skiT, T["skiT_s"])
        swi = al("sa_swi", [32, 8], F32); b.dma(swi, T["swi_s"])
        slf = al("sa_slf", [32, 8], F32); b.dma(slf, T["slf_s"])
        smask = al("sa_smask", [32, 4, 8], F32); b.dma(smask, T["smask01"])
        smask_b = al("sa_smaskb", [32, 4, 8], BF16); b.cp(smask_b, smask, eng="dve")
        sidxm = al("sa_sidxm", [32, 32], F32); b.dma(sidxm, T["sidxmask"])
        segm = al("sa_segm", [32, 4], F32); b.dma(segm, T["segmask"])
        tinc32 = al("sa_tinc32", [32, 32], F32); b.dma(tinc32, T["tinc32"])
        selrow = al("sa_selrow", [32, 4, 128], F32); b.dma(selrow, T["selrow"])
        bd01 = al("sa_bd01", [64, 8, 64], F32); b.dma(bd01, T["bd01"])
        mixTs = al("sa_mixTs", [128, 8, 32], BF16)
        kpg = [al("sa_kpg%d" % k, [128, 4, 512], BF16) for k in range(2)]
        vpg = [al("sa_vpg%d" % k, [128, 4, 512], BF16) for k in range(2)]
        kTp = [al("sa_kT%d" % k, [128, 16, 128], BF16) for k in range(2)]
        tmpS = [al("sa_tmpS%d" % k, [128, 128], F32) for k in range(2)]
        pTs = [al("sa_pT%d" % k, [128, 4, 64], BF16) for k in range(2)]
        pN = al("sa_pN", [32, 64], BF16)
        tmpN = al("sa_tmpN", [32, 32], F32)
        fin = al("sa_fin", [64, 512], F32)
        osel = al("sa_osel", [64, 64], F32)
        rcs = al("sa_rcs", [64, 2], F32)
        psV = b.pb[6]
        psD = b.pb[7]
        grp_ctr = [0]

        def attend(i, qsel, kcache, vcache, kTn, vnew, bias8, biasN8, selT, selTn, out_chunk0):
            q_ = qT[qsel]
            for gq in range(NPG // 4):
                gi = grp_ctr[0]
                grp_ctr[0] += 1
                kp, vp, kT_, pT_ = kpg[gi % 2], vpg[gi % 2], kTp[gi % 2], pTs[gi % 2]
                for p4 in range(4):
                    col = idx[:, i * NPG + gq * 4 + p4:i * NPG + gq * 4 + p4 + 1]
                    b.gather(kp[:, p4, :], kcache, col)
                    b.gather(vp[:, p4, :], vcache, col)
                b.transposes(kT_, kp.rearrange("p a f -> p (a f)"), 128, 16, g["identb"])
                pss = []
                for par in range(2):
                    ps = b.bank("mm", [0, 1, 2, 3])
                    psv = ps[:, 0:128].rearrange("p (a c q) -> p a c q", a=4, c=4)
                    base = 64 * par
                    for p4 in range(4):
                        for c_ in range(4):
                            b.mm(psv[:, p4, c_, :], kT_[base:base + 64, p4 * 4 + c_, :],
                                 q_[base:base + 64, c_, 8 * i:8 * i + 8], True, True)
                    pss.append(ps)
                pv = pT_.rearrange("p a (r c q) -> p a r c q", r=2, c=4)
                for par in range(2):
                    src = pss[par][:, 0:128].rearrange("p (a q) -> p a q", q=8)
                    dst = pv[:, :, par, :, :]
                    if bias8 is not None:
                        t_ = tmpS[par]
                        bview = bias8[:, par, gq * 4:gq * 4 + 4, :].rearrange("p a c -> p (a c)")
                        b.tt(t_.rearrange("p (a q) -> p a q", q=8), src, bview.unsqueeze(2).to_broadcast([128, 16, 8]), ALU.add)
                        b.act(dst, t_.rearrange("p (a c q) -> p a c q", a=4, c=4), AF.Exp, scale=0.125)
                    else:
                        b.act(dst, pss[par][:, 0:128].rearrange("p (a c q) -> p a c q", a=4, c=4), AF.Exp, scale=0.125)
                if selT is not None:
                    sl = selT[:, gq * 4:gq * 4 + 4, 8 * i:8 * i + 8]
                    pv3 = pT_.rearrange("p a (r q) -> p a r q", q=8)
                    b.tt(pv3, pv3, sl.unsqueeze(2).to_broadcast([128, 4, 8, 8]), ALU.mult, eng="pool")
                for p4 in range(4):
                    first = (gq == 0 and p4 == 0)
                    b.mm(psV[0:64, :], pT_[:, p4, :], vp[:, p4, :], first, False)
                    b.mm(psD[0:64, 0:2], pT_[:, p4, :], g["ones_b"][:, 0:2], first, False)
            pnv = pN.rearrange("p (r c q) -> p r c q", r=2, c=4)
            for par in range(2):
                ps = b.bank("mm", [0, 1, 2, 3])
                psv = ps[0:32, 0:32].rearrange("p (c q) -> p c q", c=4)
                base = 64 * par
                for c_ in range(4):
                    b.mm(psv[:, c_, :], kTn[base:base + 64, c_, :], q_[base:base + 64, c_, 8 * i:8 * i + 8], True, True)
                if biasN8 is not None:
                    b.tt(tmpN.rearrange("p (c q) -> p c q", c=4), psv,
                         biasN8[:, par, :].unsqueeze(2).to_broadcast([32, 4, 8]), ALU.add)
                    b.act(pnv[:, par, :, :], tmpN.rearrange("p (c q) -> p c q", c=4), AF.Exp, scale=0.125)
                else:
                    b.act(pnv[:, par, :, :], psv, AF.Exp, scale=0.125)
            pn3 = pN.rearrange("p (r q) -> p r q", q=8)
            if selTn is not None:
                b.tt(pn3, pn3, selTn[:, 8 * i:8 * i + 8].unsqueeze(1).to_broadcast([32, 8, 8]), ALU.mult)
            else:
                b.tt(pn3, pn3, smask_b[:, i, :].unsqueeze(1).to_broadcast([32, 8, 8]), ALU.mult)
            b.mm(psV[0:64, :], pN, vnew, False, True)
            b.mm(psD[0:64, 0:2], pN, g["ones_b"][0:32, 0:2], False, True)
            b.ts(rcs[:, 0:1], psD[0:64, 0:1], 1e-30, None, op0=ALU.max)
            b.recip(rcs[:, 1:2], rcs[:, 0:1])
            b.ts(fin, psV[0:64, :], rcs[:, 1:2], None, op0=ALU.mult)
            b.tt(fin, fin, bd01.rearrange("p h d -> p (h d)"), ALU.mult, eng="pool")
            b.reduce(osel, fin.rearrange("p (h d) -> p d h", h=8), ALU.add)
            ps = b.bank("mm", [0, 1, 2, 3])
            b.tr(ps[0:64, 0:64], osel, g["identf"][0:64, 0:64])
            for par in range(2):
                b.cp(mixTs[64 * par:64 * par + 64, out_chunk0:out_chunk0 + 4, 8 * i:8 * i + 8],
                     ps[0:64, par * 32:(par + 1) * 32].rearrange("p (c q) -> p c q", c=4), eng="dve")

        with ExitStack() as es_1:
            def al(name, shape, dt, es_=es_1):
                return es_.enter_context(nc.sbuf_tensor(name, list(shape), dt)).ap()
            LFs = al("sa_LFs", [128, 4, NPG, 8], F32)
            for i in range(4):
                for pg in range(NPG):
                    b.gather(LFs[:, i, pg, :], T["cache_b_logf"], idx[:, i * NPG + pg:i * NPG + pg + 1])
            bias8 = []
            cks, incls = [], []
            for i in range(4):
                ck, incl = cumsum_blocks(b, g, al, LFs[:, i, :, :], NPG, "sa_c%d" % i)
                cks.append(ck); incls.append(incl)
            totsel = al("sa_totsel", [32, 8], F32)
            b.ts(totsel, incls[0][0:32, NPG - 1, :], segm[:, 0:1], None, op0=ALU.mult)
            for i in range(1, 4):
                b.stt(totsel, incls[i][0:32, NPG - 1, :], segm[:, i:i + 1], totsel, ALU.mult, ALU.add)
            ckn = al("sa_ckn", [32, 8], F32)
            ps = b.bank("mm", [0, 1, 2, 3])
            b.mm(ps[0:32, 0:8], tinc32, slf, True, True)
            b.tt(ckn, ps[0:32, 0:8], totsel, ALU.add)
            biasN8 = []
            for i in range(4):
                ps = b.bank("mm", [0, 1, 2, 3])
                b.mm(ps[:, 0:8], selrow[:, i, :], ckn, True, True)
                cend = al("sa_cend%d" % i, [128, 8], F32)
                b.cp(cend, ps[:, 0:8])
                b8 = al("sa_b8_%d" % i, [128, 2, NPG, 4], F32)
                bn8 = al("sa_bn8_%d" % i, [32, 2, 4], F32)
                for par in range(2):
                    b.tt(b8[:, par, :, :], cend[:, par::2].unsqueeze(1).to_broadcast([128, NPG, 4]),
                         cks[i][:, :, par::2], ALU.subtract)
                    b.tt(bn8[:, par, :], cend[0:32, par::2], ckn[:, par::2], ALU.subtract)
                b.ts(b8, b8, 8.0, None, op0=ALU.mult, eng="pool")
                b.ts(bn8, bn8, 8.0, None, op0=ALU.mult, eng="pool")
                bias8.append(b8)
                biasN8.append(bn8)
            for i in range(4):
                attend(i, 2, T["cache_b_k"], T["cache_b_v"], kbTn, newtok[:, 3, :], bias8[i], biasN8[i], None, None, 4)

        b.S.barrier()
        with ExitStack() as es_2:
            def al(name, shape, dt, es_=es_2):
                return es_.enter_context(nc.sbuf_tensor(name, list(shape), dt)).ap()
            NKS = NPK + 32
            score = al("sa_score", [32, NKS], F32)
            junk = al("sa_junk", [32, NKS], BF16)
            sel = al("sa_sel", [32, NKS], BF16)
            selTs = al("sa_selTs", [128, NPG, 32], BF16)
            selTn = al("sa_selTn", [32, 32], BF16)
            qpad = al("sa_qpad", [128, 4, 4, 32], BF16)
            b.memset(qpad, 0.0)
            for i in range(4):
                b.cp(qpad[:, i, :, 8 * i:8 * i + 8], qT[1][:, :, 8 * i:8 * i + 8], eng="dve")
            kig = [al("sa_kig%d" % k, [128, 16, 128], BF16) for k in range(2)]
            kiT = [al("sa_kiT%d" % k, [128, 16, 128], BF16) for k in range(2)]
            rr = [al("sa_r%d" % k, [32, 512], F32) for k in range(3)]
            sm = al("sa_sm", [32, 8 + NBIS], F32)
            halfs = sm[:, 8:8 + NBIS]
            it = 0
            for cch in range(NPG // 4):
                kg, kt = kig[cch % 2], kiT[cch % 2]
                for i in range(4):
                    for p4 in range(4):
                        b.gather(kg[:, i * 4 + p4, 0:64], T["cache_a_kidx"], idx[:, i * NPG + cch * 4 + p4:i * NPG + cch * 4 + p4 + 1])
                b.cp(kg[:, :, 64:128], kg[:, :, 0:64], eng="pool")
                b.transposes(kt, kg.rearrange("p a f -> p (a f)"), 128, 16, g["identb"])
                cols = slice(cch * 512, (cch + 1) * 512)
                for h in range(8):
                    c_, base = h // 2, 64 * (h % 2)
                    ps = b.bank("mm", [0, 1, 2, 3])
                    for i in range(4):
                        b.mm(ps[0:32, :], qpad[base:base + 64, i, c_, :],
                             kt[base:base + 64, i * 4:(i + 1) * 4, :].rearrange("p a k -> p (a k)"), i == 0, i == 3)
                    r = rr[it % 3]
                    it += 1
                    b.act(r, ps[0:32, :], AF.Relu)
                    if h == 0:
                        b.ts(score[:, cols], r, swi[:, 0:1], None, op0=ALU.mult)
                    else:
                        b.stt(score[:, cols], r, swi[:, h:h + 1], score[:, cols], ALU.mult, ALU.add)
            ncols = slice(NPK, NKS)
            for h in range(8):
                c_, base = h // 2, 64 * (h % 2)
                ps = b.bank("mm", [0, 1, 2, 3])
                b.mm(ps[0:32, 0:32], qT[1][base:base + 64, c_, :], skiT[base:base + 64, :], True, True)
                r = rr[it % 3]
                it += 1
                b.act(r[:, 0:32], ps[0:32, 0:32], AF.Relu)
                if h == 0:
                    b.ts(score[:, ncols], r[:, 0:32], swi[:, 0:1], None, op0=ALU.mult)
                else:
                    b.stt(score[:, ncols], r[:, 0:32], swi[:, h:h + 1], score[:, ncols], ALU.mult, ALU.add)
            b.reduce(sm[:, 0:1], score, ALU.max, absval=True)
            b.ts(sm[:, 1:2], sm[:, 0:1], 1.0001, 1e-20, op0=ALU.mult, op1=ALU.add)
            b.ts(halfs, g["pow2"][0:32, :], sm[:, 1:2], None, op0=ALU.mult)
            b.tt(score[:, ncols], score[:, ncols], sidxm, ALU.add)
            b.memset(sm[:, 2:3], 0.0, eng="dve")
            for k in range(NBIS):
                b.ts(junk, score, sm[:, 2:3], None, op0=ALU.is_ge, op1=ALU.add, accum=sm[:, 3:4])
                if k < NBIS - 1:
                    b.ts(sm[:, 4:5], sm[:, 3:4], 255.5, -0.5, op0=ALU.is_ge, op1=ALU.add)
                    b.stt(sm[:, 2:3], sm[:, 4:5], halfs[:, k:k + 1], sm[:, 2:3], ALU.mult, ALU.add)
                else:
                    b.ts(sm[:, 4:5], sm[:, 3:4], 255.5, -1.0, op0=ALU.is_ge, op1=ALU.add)
                    b.stt(sm[:, 5:6], sm[:, 4:5], halfs[:, k:k + 1], sm[:, 2:3], ALU.mult, ALU.add)
            b.ts(sel, score, sm[:, 5:6], None, op0=ALU.is_ge)
            b.transposes(selTs, sel, 32, NPG, g["identb"])
            pbk = b.bank("tr", [4, 5]).bitcast(BF16)
            b.tr(pbk[0:32, 0:32], sel[:, ncols], g["identb"][0:32, 0:32])
            b.cp(selTn, pbk[0:32, 0:32])
            for i in range(4):
                attend(i, 0, T["cache_a_k"], T["cache_a_v"], kaTn, newtok[:, 1, :], None, None, selTs, selTn, 0)

        b.S.barrier()
        with ExitStack() as es_3:
            def al(name, shape, dt, es_=es_3):
                return es_.enter_context(nc.sbuf_tensor(name, list(shape), dt)).ap()
            wout = al("sa_wout", [128, 8, D], BF16)
            for j in range(2):
                b.dma(wout[:, :, j * 512:(j + 1) * 512],
                      T["w_out_ab"][:, j * 512:(j + 1) * 512].rearrange("(k p) n -> p k n", p=128), q="pool")
            gp1 = al("sa_gp1", [32, D], F32); b.dma(gp1, T["modS"][0, :, 2 * D:3 * D])
            lng = al("sa_lng", [32, D], F32); b.dma(lng, T["ln1_g"][0:1, :].partition_broadcast(32))
            lnb = al("sa_lnb", [32, D], F32); b.dma(lnb, T["ln1_b"][0:1, :].partition_broadcast(32))
            xs = al("sa_xs", [32, D], F32); b.dma(xs, T["xs"])
            lt = (al("sa_acc", [32, D], F32), al("sa_tmp", [32, D], F32), al("sa_st", [32, 4, 6], F32),
                  al("sa_mv", [32, 4], F32), al("sa_res", [32, D], F32))
            pss = []
            for n2 in range(2):
                ps = b.bank("mm", [0, 1, 2, 3])
                for kc_ in range(8):
                    b.mm(ps[0:32, :512], mixTs[:, kc_, :], wout[:, kc_, n2 * 512:(n2 + 1) * 512], kc_ == 0, kc_ == 7)
                pss.append(ps)
            residual_ln(b, lt, pss, xs, 32, gp1, lng, lnb, T["xs1_s"])
    b.S.barrier()


def build_full(nc, n_phys=NPHYS):
    b = B(nc, n_phys=n_phys)
    T = declare(b, with_sample=True)
    g = consts(b, T)
    phase0(b, T, g)
    phaseA(b, T, g, do_sample=True)
    phaseSA(b, T, g)
    phaseB1(b, T, g)
    phaseB2(b, T, g)
    phaseC(b, T, g, 0, T["x1_s"], T["x2_s"], T["xs1_s"], T["xs2_s"])
    phaseD(b, T, g)
    phaseC(b, T, g, 1, T["x3_s"], T["y_p"], T["xs3_s"], T["y_s"])
    return b, T


def rope_tab(pos):
    half = 32
    inv = (10000.0 ** (-np.arange(half, dtype=np.float32) / half)).astype(np.float32)
    ang = pos.astype(np.float32)[:, None] * inv[None, :]
    return np.concatenate([np.cos(ang), np.sin(ang)], axis=1).astype(np.float32)


def prep_inputs(inp, with_sample=True, n_phys=NPHYS, remap=None):
    f = np.float32
    ar = np.arange(128)
    identf = np.eye(128, dtype=f)
    tinc = (ar[:, None] <= ar[None, :]).astype(f)
    triT01 = (ar[:, None] <= ar[None, :]).astype(f)
    triqk = np.where(ar[None, :] <= ar[:, None], 0.0, NEGBIG).astype(f)
    pow2 = np.tile((2.0 ** -np.arange(NBIS)).astype(f)[None, :], (128, 1))
    shared = {
        "identf": identf, "tinc": tinc, "triT01": triT01, "triqk": triqk, "pow2": pow2,
        "w_mod": inp["w_mod"], "b_mod": inp["b_mod"],
        "ln1_g": inp["ln1_g"], "ln1_b": inp["ln1_b"], "ln2_g": inp["ln2_g"], "ln2_b": inp["ln2_b"],
        "w_in_ab": inp["w_in_ab"][0], "b_forget": inp["b_forget"], "w_out_ab": inp["w_out_ab"][0],
        "w_in_c": inp["w_in_c"][0], "lnv_g": inp["lnv_g"], "lnv_b": inp["lnv_b"],
        "w_spatial": inp["w_spatial"][0], "b_spatial": inp["b_spatial"][0], "w_out_c": inp["w_out_c"][0],
        "w_up": inp["w_up"], "w_conv": inp["w_conv"], "b_conv": inp["b_conv"], "w_down": inp["w_down"],
    }
    if with_sample:
        t8 = np.arange(8)
        smask01 = np.zeros((32, 4, 8), f)
        sidx = np.full((32, 32), NEGBIG, f)
        segmask = np.zeros((32, 4), f)
        for i in range(4):
            for t in range(8):
                segmask[i * 8 + t, i] = 1.0
                for q in range(8):
                    if t <= q:
                        smask01[i * 8 + t, i, q] = 1.0
                        sidx[i * 8 + q, i * 8 + t] = 0.0
        tinc32 = np.zeros((32, 32), f)
        for i in range(4):
            for a in range(8):
                for c_ in range(a, 8):
                    tinc32[i * 8 + a, i * 8 + c_] = 1.0
        selrow = np.zeros((32, 4, 128), f)
        for i in range(4):
            selrow[i * 8 + 7, i, :] = 1.0
        bd01 = np.zeros((64, 8, 64), f)
        for r_ in range(64):
            bd01[r_, 2 * ((r_ // 8) % 4) + (r_ // 32), :] = 1.0
        shared.update({"smask01": smask01, "sidxmask": sidx, "segmask": segmask, "tinc32": tinc32,
                       "selrow": selrow, "bd01": bd01, "iota128": np.arange(128, dtype=np.int32).reshape(128, 1)})
        if remap is None:
            shared.update({
                "cache_a_k": inp["cache_a_k"][0].reshape(n_phys * 128, 512),
                "cache_a_v": inp["cache_a_v"][0].reshape(n_phys * 128, 512),
                "cache_a_kidx": inp["cache_a_kidx"][0].reshape(n_phys * 128, 64),
                "cache_b_k": inp["cache_b_k"][0].reshape(n_phys * 128, 512),
                "cache_b_v": inp["cache_b_v"][0].reshape(n_phys * 128, 512),
                "cache_b_logf": inp["cache_b_logf"][0].reshape(n_phys * 128, 8),
            })
    maps = []
    for c in range(8):
        bb, half = c // 2, c % 2
        xp = inp["x_prompt"][bb]
        if half == 1:
            xk = np.ascontiguousarray(xp)
            pos = np.arange(4096)
            jm = np.zeros((1, 4096), f)
            blk = np.zeros((1, 32), f)
        else:
            xk = np.concatenate([np.zeros((2048, D), f), xp[:2048]], axis=0)
            pos = np.concatenate([np.zeros(2048), np.arange(2048)])
            jm = np.concatenate([np.full((1, 2048), NEGBIG, f), np.zeros((1, 2048), f)], axis=1)
            blk = np.concatenate([np.full((1, 16), NEG, f), np.zeros((1, 16), f)], axis=1)
        m = dict(shared)
        m.update({
            "xk": xk, "cstab": rope_tab(pos), "cP": np.tile(inp["c_prompt"][bb][None, :], (128, 1)),
            "jm": jm, "blkbias": blk, "haloflag": np.full((128, 1), float(half), f),
            "xs": np.ascontiguousarray(inp["x_sample"][4 * c:4 * c + 4].reshape(32, D)),
            "cstab_s": np.tile(rope_tab(8192 + np.arange(8)), (4, 1)),
            "cS": np.repeat(inp["c_sample"][4 * c:4 * c + 4], 8, axis=0),
        })
        if with_sample:
            m["state_conv"] = np.ascontiguousarray(inp["state_ffn_conv"][:, 4 * c:4 * c + 4].reshape(2, 8, DUP))
            pt = inp["page_table"][4 * c:4 * c + 4].reshape(1, 4 * NPG).astype(np.int32)
            if remap is not None:
                uniq = pt.reshape(-1)
                m["page_table"] = np.arange(uniq.size, dtype=np.int32).reshape(1, -1)
                for nm, w_ in (("cache_a_k", 512), ("cache_a_v", 512), ("cache_a_kidx", 64), ("cache_b_k", 512),
                               ("cache_b_v", 512), ("cache_b_logf", 8)):
                    m[nm] = np.ascontiguousarray(inp[nm][0][uniq].reshape(uniq.size * 128, w_))
            else:
                m["page_table"] = pt
        maps.append(m)
    return maps


def assemble(res):
    f = np.float32
    odd = [res[2 * bb + 1] for bb in range(4)]
    y_p = np.stack([np.concatenate([res[2 * bb]["y_p"][256:], res[2 * bb + 1]["y_p"][256:]], axis=0) for bb in range(4)])
    y_s = np.concatenate([r["y_s"] for r in res], axis=0).reshape(32, 8, D)
    ak_p = np.stack([r["ak_p"] for r in odd]).reshape(1, 4, 4096, 8, 64)
    av_p = np.stack([r["av_p"] for r in odd]).reshape(1, 4, 4096, 8, 64)
    aki_p = np.stack([r["aki_p"] for r in odd]).reshape(1, 4, 4096, 64)
    bk_p = np.stack([r["bk_p"] for r in odd]).reshape(1, 4, 4096, 8, 64)
    bv_p = np.stack([r["bv_p"] for r in odd]).reshape(1, 4, 4096, 8, 64)
    blf_p = np.stack([r["blf_p"] for r in odd]).reshape(1, 4, 4096, 8)
    conv_p = np.stack([r["conv_p"] for r in odd], axis=1)
    cat = lambda n: np.concatenate([r[n] for r in res], axis=0)
    ak_s = cat("ak_s").reshape(1, 32, 8, 8, 64)
    av_s = cat("av_s").reshape(1, 32, 8, 8, 64)
    aki_s = cat("aki_s").reshape(1, 32, 8, 64)
    bk_s = cat("bk_s").reshape(1, 32, 8, 8, 64)
    bv_s = cat("bv_s").reshape(1, 32, 8, 8, 64)
    blf_s = cat("blf_s").reshape(1, 32, 8, 8)
    conv_s = np.concatenate([r["conv_s"].reshape(2, 4, 2, DUP) for r in res], axis=1)
    cv_s = cat("cv_s").reshape(1, 32, 8, DCG)
    outs = (y_p, y_s, ak_p, av_p, aki_p, bk_p, bv_p, blf_p, conv_p, ak_s, av_s, aki_s, bk_s, bv_s, blf_s, conv_s, cv_s)
    return tuple(np.ascontiguousarray(o, dtype=f) for o in outs)


def kernel(**inputs):
    inp = {k: np.asarray(v) for k, v in inputs.items()}
    nc = bass.Bass("TRN2", target_bir_lowering=False)
    bld, T = build_full(nc)
    bld.S.emit()
    maps = prep_inputs(inp, with_sample=True)
    res = run_bass_kernel_spmd(nc, maps, core_ids=list(range(8)))
    return assemble(res.results)
```

```python
from concourse.bass_utils import run_bass_kernel_spmd
import concourse.bass as bass
import concourse.mybir as mybir

SEM_MAX = 30000


def _rect(ap):
    t = ap.tensor
    name = t.name
    pat = list(ap.ap)
    esz = mybir.dt.size(ap.dtype) if hasattr(mybir.dt, "size") else None
    if esz is None:
        esz = {"float32": 4, "bfloat16": 2, "int32": 4, "uint32": 4, "float16": 2,
               "uint8": 1, "int8": 1, "uint16": 2, "int16": 2}[str(ap.dtype).split(".")[-1]]
    space = str(ap.space) if hasattr(ap, "space") else ""
    if "DRAM" in space.upper() or "Dram" in type(t).__name__ or "DRam" in type(t).__name__:
        lo = ap.offset
        hi = lo
        for (s, c) in pat:
            if c > 1:
                if s >= 0:
                    hi += s * (c - 1)
                else:
                    lo += s * (c - 1)
        return (name, 0, 1, lo * esz, (hi + 1) * esz)
    pstep, pcount = pat[0]
    if pstep == 0:
        pstep = 1 << 40
    p0 = ap.offset // pstep if pstep < (1 << 40) else 0
    foff = ap.offset - p0 * pstep if pstep < (1 << 40) else ap.offset
    lo = foff
    hi = foff
    for (s, c) in pat[1:]:
        if c > 1:
            if s >= 0:
                hi += s * (c - 1)
            else:
                lo += s * (c - 1)
    if name.startswith("pb"):
        return (name, (p0 // 32) * 32, ((p0 + pcount + 31) // 32) * 32, 0, 2048)
    return (name, p0, p0 + pcount, lo * esz, (hi + 1) * esz)


def _overlap(a, b):
    return a[1] < b[2] and b[1] < a[2] and a[3] < b[4] and b[3] < a[4]


def _covers(a, b):
    return a[1] <= b[1] and a[2] >= b[2] and a[3] <= b[3] and a[4] >= b[4]


class Op:
    __slots__ = ("eng", "fn", "deps", "sig", "sigval", "is_dma", "dsem", "idx", "seq")


class Sched:
    COMPUTE = ("pe", "act", "dve", "pool")

    def __init__(self, nc):
        self.nc = nc
        self.ops = []
        self.recs = {}
        self.engs = {
            "pe": nc.tensor, "act": nc.scalar, "dve": nc.vector, "pool": nc.gpsimd, "sp": nc.sync,
        }

    def add(self, eng, fn, reads=(), writes=(), dma=False):
        op = Op()
        op.eng = eng
        op.fn = fn
        op.is_dma = dma
        op.sig = False
        op.sigval = None
        op.dsem = None
        op.idx = len(self.ops)
        deps = set()
        accs = [(_rect(a), False) for a in reads] + [(_rect(a), True) for a in writes]
        for rect, is_w in accs:
            lst = self.recs.get(rect[0])
            if lst is None:
                continue
            for r in lst:
                if (is_w or r[1]) and _overlap(rect, r[0]):
                    deps.add(r[2])
        for rect, is_w in accs:
            lst = self.recs.setdefault(rect[0], [])
            if is_w:
                lst[:] = [r for r in lst if not _covers(rect, r[0])]
                lst.append([rect, True, op.idx])
            else:
                done = False
                if not dma:
                    for r in lst:
                        if (not r[1]) and r[0] == rect and self.ops[r[2]].eng == eng \
                                and not self.ops[r[2]].is_dma:
                            r[2] = op.idx
                            done = True
                            break
                if not done:
                    lst.append([rect, False, op.idx])
        deps.discard(op.idx)
        if eng == "pe" and not dma:
            deps = {d for d in deps if not (self.ops[d].eng == "pe" and not self.ops[d].is_dma)}
        op.deps = sorted(deps)
        for d in op.deps:
            self.ops[d].sig = True
        self.ops.append(op)
        return op

    def barrier(self):
        last = {}
        for op in self.ops:
            if op.eng != "barrier" and not op.is_dma:
                last[op.eng] = op
        op = Op()
        op.eng = "barrier"
        op.fn = None
        op.is_dma = False
        op.sig = False
        op.sigval = None
        op.dsem = None
        op.idx = len(self.ops)
        op.deps = [o.idx for o in last.values()]
        for o in last.values():
            o.sig = True
        self.ops.append(op)
        self.recs.clear()

    def emit(self, final_wait_eng="sp", n_dma_sems=24):
        nc = self.nc
        sems = {}
        cnt = {}
        epoch = {}

        def new_sem(tag):
            return nc.alloc_semaphore(name=f"s_{tag}_{len(sems)}")

        for e in self.COMPUTE:
            sems[e] = new_sem(e)
            cnt[e] = 0
        dma_sems = [new_sem("dma%d" % i) for i in range(n_dma_sems)]
        dma_cnt = [0] * n_dma_sems
        dma_rr = 0
        waited = {}
        all_sem_objs = []

        def do_wait(weng, semobj, key, val):
            k = (weng, key)
            if waited.get(k, 0) >= val:
                return
            waited[k] = val
            self.engs[weng].wait_ge(semobj, val)

        last_dma = []
        dma_last = {}
        for op in self.ops:
            e = op.eng
            if e == "barrier":
                for w in ("pe", "act", "dve", "pool", "sp"):
                    for d in op.deps:
                        semobj, val = self.ops[d].sigval
                        do_wait(w, semobj, ("c", id(semobj)), val)
                    for so, v in dma_last.values():
                        do_wait(w, so, ("d", id(so)), v)
                continue
            for d in op.deps:
                dop = self.ops[d]
                if dop.is_dma:
                    so, val = dop.dsem
                    do_wait(e, so, ("d", id(so)), val)
                else:
                    semobj, val = dop.sigval
                    do_wait(e, semobj, ("c", id(semobj)), val)
            if op.is_dma:
                si = dma_rr
                dma_rr = (dma_rr + 1) % n_dma_sems
                if dma_cnt[si] + 16 > SEM_MAX:
                    dma_sems[si] = new_sem("dmaX")
                    dma_cnt[si] = 0
                if dma_cnt[si] > 0:
                    do_wait(e, dma_sems[si], ("d", id(dma_sems[si])), dma_cnt[si])
                ins = op.fn()
                dma_cnt[si] += 16
                ins.then_inc(dma_sems[si], 16)
                op.dsem = (dma_sems[si], dma_cnt[si])
                dma_last[id(dma_sems[si])] = (dma_sems[si], dma_cnt[si])
                last_dma.append((si, dma_sems[si], dma_cnt[si]))
            else:
                ins = op.fn()
                if op.sig:
                    if cnt[e] + 1 > SEM_MAX:
                        sems[e] = new_sem(e + "X")
                        cnt[e] = 0
                    cnt[e] += 1
                    ins.then_inc(sems[e], 1)
                    op.sigval = (sems[e], cnt[e])
        fin = {}
        for si, so, v in last_dma:
            fin[id(so)] = (so, max(v, fin.get(id(so), (None, 0))[1]), si)
        for so, v, si in fin.values():
            self.engs[final_wait_eng].wait_ge(so, v)
        return len(self.ops)


import math
from contextlib import ExitStack
import numpy as np
import concourse.bass as bass
import concourse.mybir as mybir

F32 = mybir.dt.float32
BF16 = mybir.dt.bfloat16
I32 = mybir.dt.int32
U32 = mybir.dt.uint32
AF = mybir.ActivationFunctionType
ALU = mybir.AluOpType
AX = mybir.AxisListType

D = 1024
NKB = 32
NS = 18
SLOT0 = 14
DFF = 2816
DUP = 5632
PAB = 3664
DCG = 2048
NEG = -30000.0
NEGBIG = -1.0e30
ALPHA = 4.0 ** 0.25
LN_EPS = 1e-5
NBIS = 16
WI_SCALE = (64 ** -0.5) * (8 ** -0.5)
NPHYS = 2560
NPG = 64

CH = [("qa", 0, 512, False), ("ka", 512, 1024, True), ("va", 1024, 1536, True), ("qi", 1536, 2048, False),
      ("kiwi", 2048, 2120, True), ("qb", 2120, 2632, False), ("kb", 2632, 3144, True),
      ("vb", 3144, 3656, True), ("fb", 3656, 3664, True)]


class B:
    def __init__(self, nc, n_phys=NPHYS):
        self.nc = nc
        self.S = Sched(nc)
        self.n_phys = n_phys
        self.pb = [nc.alloc_psum_tensor("pb%d" % i, [128, 512], F32).ap() for i in range(8)]
        self.rot = {}
        self.dram = {}

    def bank(self, group, banks):
        i = self.rot.get(group, 0)
        self.rot[group] = i + 1
        return self.pb[banks[i % len(banks)]]

    def dma(self, out, in_, q="sp"):
        eng = {"sp": self.nc.sync, "pool": self.nc.gpsimd, "act": self.nc.scalar}[q]
        self.S.add(q, lambda: eng.dma_start(out=out, in_=in_), [in_], [out], dma=True)

    def gather(self, out, src2d, idxcol):
        nc = self.nc
        self.S.add("pool", lambda: nc.gpsimd.indirect_dma_start(
            out=out, out_offset=None, in_=src2d,
            in_offset=bass.IndirectOffsetOnAxis(ap=idxcol.bitcast(U32), axis=0)),
            [src2d, idxcol], [out], dma=True)

    def _e(self, eng):
        return {"dve": self.nc.vector, "pool": self.nc.gpsimd}[eng]

    def tt(self, out, in0, in1, op, eng="dve"):
        e = self._e(eng)
        self.S.add(eng, lambda: e.tensor_tensor(out=out, in0=in0, in1=in1, op=op), [in0, in1], [out])

    def ts(self, out, in0, s1, s2=None, op0=ALU.mult, op1=None, accum=None, eng="dve"):
        e = self._e(eng)
        rd = [in0] + [x for x in (s1, s2) if hasattr(x, "tensor")]
        wr = [out] + ([accum] if accum is not None else [])
        kw = {}
        if op1 is not None:
            kw["op1"] = op1
        if accum is not None:
            kw["accum_out"] = accum
        self.S.add(eng, lambda: e.tensor_scalar(out=out, in0=in0, scalar1=s1, scalar2=s2, op0=op0, **kw), rd, wr)

    def stt(self, out, in0, scalar, in1, op0, op1):
        nc = self.nc
        rd = [in0, in1] + ([scalar] if hasattr(scalar, "tensor") else [])
        self.S.add("dve", lambda: nc.vector.scalar_tensor_tensor(out=out, in0=in0, scalar=scalar, in1=in1,
                                                                 op0=op0, op1=op1), rd, [out])

    def act(self, out, in_, func, bias=None, scale=1.0, accum=None):
        nc = self.nc
        rd = [in_] + [x for x in (bias, scale) if hasattr(x, "tensor")]
        wr = [out] + ([accum] if accum is not None else [])
        kw = {}
        if bias is not None:
            kw["bias"] = bias
        if accum is not None:
            kw["accum_out"] = accum
        self.S.add("act", lambda: nc.scalar.activation(out=out, in_=in_, func=func, scale=scale, **kw), rd, wr)

    def cp(self, out, in_, eng="act"):
        nc = self.nc
        if eng == "act":
            self.S.add("act", lambda: nc.scalar.copy(out=out, in_=in_), [in_], [out])
        else:
            e = self._e(eng)
            self.S.add(eng, lambda: e.tensor_copy(out=out, in_=in_), [in_], [out])

    def memset(self, out, val, eng="pool"):
        e = self._e(eng)
        self.S.add(eng, lambda: e.memset(out, val), [], [out])

    def mm(self, out, lhsT, rhs, start, stop):
        nc = self.nc
        self.S.add("pe", lambda: nc.tensor.matmul(out, lhsT=lhsT, rhs=rhs, start=start, stop=stop),
                   [lhsT, rhs], [out])

    def tr(self, out, in_, ident):
        nc = self.nc
        self.S.add("pe", lambda: nc.tensor.transpose(out, in_, ident), [in_, ident], [out])

    def reduce(self, out, in_, op, absval=False):
        nc = self.nc
        kw = {"apply_absolute_value": True} if absval else {}
        self.S.add("dve", lambda: nc.vector.tensor_reduce(out=out, in_=in_, axis=AX.X, op=op, **kw), [in_], [out])

    def recip(self, out, in_):
        nc = self.nc
        self.S.add("dve", lambda: nc.vector.reciprocal(out=out, in_=in_), [in_], [out])

    def bn(self, mv, x, M, n, stats):
        nc = self.nc
        nch = (n + 511) // 512
        for c in range(nch):
            a = x[:M, c * 512:min(n, (c + 1) * 512)]
            o = stats[:M, c, :]
            self.S.add("dve", lambda a=a, o=o: nc.vector.bn_stats(out=o, in_=a), [a], [o])
        si = stats[:M, 0:nch, :]
        mo = mv[:M, 0:2]
        self.S.add("dve", lambda: nc.vector.bn_aggr(out=mo, in_=si), [si], [mo])

    def transposes(self, dst, src_bf, M, nchunk, ident_b, evac="act"):
        done = 0
        while done < nchunk:
            n = min(8, nchunk - done)
            pbk = self.bank("tr", [4, 5]).bitcast(BF16).rearrange("p (a c) -> p a c", a=8)
            for k in range(n):
                self.tr(pbk[:, k, 0:M], src_bf[:M, (done + k) * 128:(done + k + 1) * 128], ident_b[:M, :M])
            self.cp(dst[:, done:done + n, 0:M], pbk[:, 0:n, 0:M], eng=evac)
            done += n

    def layernorm_affine(self, out, x, M, n, g_bc, b_bc, tmp, stats, mv):
        self.bn(mv, x, M, n, stats)
        self.act(mv[:M, 2:3], mv[:M, 1:2], AF.Sqrt, bias=self.eps_col[:M, :], scale=1.0)
        self.recip(mv[:M, 3:4], mv[:M, 2:3])
        self.ts(tmp[:M, :n], x[:M, :n], mv[:M, 0:1], mv[:M, 3:4], op0=ALU.subtract, op1=ALU.mult)
        self.tt(tmp[:M, :n], tmp[:M, :n], g_bc[:M, :n], ALU.mult, eng="pool")
        self.tt(out[:M, :n], tmp[:M, :n], b_bc[:M, :n], ALU.add, eng="pool")


DEBUG_OUT = set()


def declare(b, with_sample=True):
    nc = b.nc
    T = {}

    def I(name, shape, dt=F32):
        T[name] = nc.dram_tensor(name, list(shape), dt, kind="ExternalInput").ap()

    def O(name, shape, dt=F32):
        T[name] = nc.dram_tensor(name, list(shape), dt, kind="ExternalOutput").ap()

    def Sx(name, shape, dt=F32):
        kind = "ExternalOutput" if name in DEBUG_OUT else "Internal"
        T[name] = nc.dram_tensor(name, list(shape), dt, kind=kind).ap()

    I("xk", [NKB * 128, D]); I("cstab", [NKB * 128, 64]); I("cP", [128, D])
    I("identf", [128, 128]); I("tinc", [128, 128]); I("triT01", [128, 128]); I("triqk", [128, 128])
    I("jm", [1, NKB * 128]); I("blkbias", [1, NKB]); I("haloflag", [128, 1]); I("pow2", [128, NBIS])
    I("w_mod", [2, D, 6 * D]); I("b_mod", [2, 6 * D])
    for n in ("ln1_g", "ln1_b", "ln2_g", "ln2_b"):
        I(n, [2, D])
    I("w_in_ab", [D, PAB]); I("b_forget", [1, 8]); I("w_out_ab", [D, D])
    I("w_in_c", [D, 2 * DCG]); I("lnv_g", [1, DCG]); I("lnv_b", [1, DCG])
    I("w_spatial", [8, 128, 128]); I("b_spatial", [8, 128]); I("w_out_c", [DCG, D])
    I("w_up", [2, D, DUP]); I("w_conv", [2, 3, DUP]); I("b_conv", [2, DUP]); I("w_down", [2, DFF, D])
    I("xs", [32, D]); I("cstab_s", [32, 64]); I("cS", [32, D])
    if with_sample:
        I("cache_a_k", [b.n_phys * 128, 512]); I("cache_a_v", [b.n_phys * 128, 512])
        I("cache_a_kidx", [b.n_phys * 128, 64])
        I("cache_b_k", [b.n_phys * 128, 512]); I("cache_b_v", [b.n_phys * 128, 512])
        I("cache_b_logf", [b.n_phys * 128, 8])
        I("state_conv", [2, 8, DUP]); I("page_table", [1, 4 * NPG], I32); I("iota128", [128, 1], I32)
        I("smask01", [32, 4, 8]); I("sidxmask", [32, 32]); I("segmask", [32, 4]); I("tinc32", [32, 32])
        I("selrow", [32, 4, 128]); I("bd01", [64, 8, 64])
    O("y_p", [NS * 128, D]); O("y_s", [32, D])
    O("ak_p", [NKB * 128, 512]); O("av_p", [NKB * 128, 512]); O("aki_p", [NKB * 128, 64])
    O("bk_p", [NKB * 128, 512]); O("bv_p", [NKB * 128, 512]); O("blf_p", [NKB * 128, 8])
    O("conv_p", [2, 2, DUP])
    O("ak_s", [32, 512]); O("av_s", [32, 512]); O("aki_s", [32, 64])
    O("bk_s", [32, 512]); O("bv_s", [32, 512]); O("blf_s", [32, 8])
    O("conv_s", [2, 8, DUP]); O("cv_s", [32, DCG])
    if "dbg_u" in DEBUG_OUT:
        O("dbg_u", [128, 520]); O("dbg_y", [128, 512]); O("dbg_w", [128, 176])
    Sx("modP", [2, 128, 6 * D]); Sx("modS", [2, 32, 6 * D])
    Sx("kaT_s", [128, 4, NKB * 128], BF16); Sx("kbT_s", [128, 4, NKB * 128], BF16)
    Sx("kiT2_s", [128, NKB * 128], BF16)
    Sx("va_s", [NKB, 128, 576], BF16); Sx("vb_s", [NKB, 128, 576], BF16)
    Sx("qaT_s", [NS, 128, 4, 128], BF16); Sx("qiT_s", [NS, 128, 4, 128], BF16); Sx("qbT_s", [NS, 128, 4, 128], BF16)
    Sx("wi_s", [NS, 128, 8])
    Sx("x1_s", [NS * 128, D]); Sx("x2_s", [NS * 128, D]); Sx("x3_s", [NS * 128, D]); Sx("ffp_s", [NS * 128, D])
    Sx("gT_s", [NS, 128, 16, 128], BF16)
    Sx("mixTa_s", [NS, 128, 4, 128], BF16)
    Sx("xs1_s", [32, D]); Sx("xs2_s", [32, D]); Sx("xs3_s", [32, D]); Sx("ffps_s", [32, D])
    Sx("gTs_s", [128, 16, 32], BF16)
    Sx("sq_s", [3, 128, 4, 32], BF16)
    Sx("sk_s", [4, 32, 576], BF16)
    Sx("skiT_s", [128, 32], BF16); Sx("swi_s", [32, 8]); Sx("slf_s", [32, 8])
    Sx("mixTs_s", [128, 8, 32], BF16)
    return T


def consts(b, T):
    nc = b.nc
    g = {}

    def al(name, shape, dt):
        return nc.alloc_sbuf_tensor("g_" + name, list(shape), dt).ap()

    g["identf"] = al("identf", [128, 128], F32)
    g["identb"] = al("identb", [128, 128], BF16)
    g["ones_f"] = al("ones_f", [128, 128], F32)
    g["ones_b"] = al("ones_b", [128, 128], BF16)
    g["tinc"] = al("tinc", [128, 128], F32)
    g["triT01"] = al("triT01", [128, 128], BF16)
    g["triqk"] = al("triqk", [128, 128], F32)
    g["LF"] = al("LF", [128, NKB, 8], F32)
    g["haloflag"] = al("haloflag", [128, 1], F32)
    g["blkbias"] = al("blkbias", [128, NKB], F32)
    g["pow2"] = al("pow2", [128, NBIS], F32)
    b.eps_col = al("eps_col", [128, 1], F32)
    tmp = al("c_tmp", [128, 128], F32)
    b.dma(g["identf"], T["identf"])
    b.cp(g["identb"], g["identf"], eng="dve")
    b.memset(g["ones_f"], 1.0)
    b.memset(g["ones_b"], 1.0)
    b.memset(b.eps_col, LN_EPS)
    b.dma(g["tinc"], T["tinc"])
    b.dma(tmp, T["triT01"])
    b.cp(g["triT01"], tmp, eng="dve")
    b.dma(g["triqk"], T["triqk"])
    b.dma(g["haloflag"], T["haloflag"])
    b.dma(g["blkbias"], T["blkbias"].partition_broadcast(128))
    b.dma(g["pow2"], T["pow2"])
    return g


def phase0(b, T, g):
    nc = b.nc
    with ExitStack() as es:
        def al(name, shape, dt):
            return es.enter_context(nc.sbuf_tensor(name, list(shape), dt)).ap()
        cf = al("p0_cf", [128, D], F32)
        cb = al("p0_cb", [128, D], BF16)
        cT = {"P": al("p0_cTP", [128, 8, 128], BF16), "S": al("p0_cTS", [128, 8, 32], BF16)}
        for grp, M, src in (("P", 128, T["cP"]), ("S", 32, T["cS"])):
            b.dma(cf[:M, :], src)
            b.act(cb[:M, :], cf[:M, :], AF.Silu)
            b.transposes(cT[grp], cb, M, 8, g["identb"])
        wch = [al("p0_w%d" % i, [128, 8, 512], BF16) for i in range(2)]
        bch = [al("p0_b%d" % i, [128, 512], F32) for i in range(2)]
        och = [al("p0_o%d" % i, [128, 512], F32) for i in range(4)]
        it = 0
        for i in range(2):
            for j in range(12):
                w = wch[it % 2]
                bb = bch[it % 2]
                b.dma(w, T["w_mod"][i, :, j * 512:(j + 1) * 512].rearrange("(k p) n -> p k n", p=128), q="pool")
                b.dma(bb, T["b_mod"][i:i + 1, j * 512:(j + 1) * 512].partition_broadcast(128))
                for gi, (grp, M, dst) in enumerate((("P", 128, T["modP"]), ("S", 32, T["modS"]))):
                    ps = b.bank("mm", [0, 1, 2, 3])
                    for k in range(8):
                        b.mm(ps[:M, :], cT[grp][:, k, :M], w[:, k, :], k == 0, k == 7)
                    o = och[(it * 2 + gi) % 4]
                    b.tt(o[:M, :], ps[:M, :], bb[:M, :], ALU.add)
                    if (j // 2) in (1, 2, 4, 5):
                        b.ts(o[:M, :], o[:M, :], 1.0, None, op0=ALU.add, eng="pool")
                    b.dma(dst[i, :, j * 512:(j + 1) * 512], o[:M, :])
                it += 1
    b.S.barrier()


def rope(b, out, ps, M, H, cs, t):
    x = ps[:M, 0:H * 64].rearrange("p (h d) -> p h d", h=H)
    o = out[:M, 0:H * 64].rearrange("p (h d) -> p h d", h=H)
    x1, x2 = x[:, :, 0:32], x[:, :, 32:64]
    cosb = cs[:M, 0:32].unsqueeze(1).to_broadcast([M, H, 32])
    sinb = cs[:M, 32:64].unsqueeze(1).to_broadcast([M, H, 32])
    ta = t[0][:M, 0:H * 32].rearrange("p (h d) -> p h d", h=H)
    tb = t[1][:M, 0:H * 32].rearrange("p (h d) -> p h d", h=H)
    b.tt(ta, x1, cosb, ALU.mult)
    b.tt(tb, x2, sinb, ALU.mult)
    b.tt(o[:, :, 0:32], ta, tb, ALU.subtract, eng="pool")
    b.tt(ta, x2, cosb, ALU.mult)
    b.tt(tb, x1, sinb, ALU.mult)
    b.tt(o[:, :, 32:64], ta, tb, ALU.add, eng="pool")


def phaseA(b, T, g, do_sample=True):
    nc = b.nc
    with ExitStack() as es:
        def al(name, shape, dt):
            return es.enter_context(nc.sbuf_tensor(name, list(shape), dt)).ap()
        w = al("pa_w", [128, 8, PAB], BF16)
        for (_, c0, c1, _) in CH:
            b.dma(w[:, :, c0:c1], T["w_in_ab"][:, c0:c1].rearrange("(k p) n -> p k n", p=128), q="pool")
        sc = {"P": al("pa_scP", [128, D], F32), "S": al("pa_scS", [32, D], F32)}
        sh = {"P": al("pa_shP", [128, D], F32), "S": al("pa_shS", [32, D], F32)}
        b.dma(sc["P"], T["modP"][0, :, D:2 * D]); b.dma(sh["P"], T["modP"][0, :, 0:D])
        b.dma(sc["S"], T["modS"][0, :, D:2 * D]); b.dma(sh["S"], T["modS"][0, :, 0:D])
        bfb = al("pa_bfb", [128, 8], F32)
        b.dma(bfb, T["b_forget"].partition_broadcast(128))
        xt = [al("pa_x%d" % i, [128, D], F32) for i in range(2)]
        cs = [al("pa_cs%d" % i, [128, 64], F32) for i in range(2)]
        hf = al("pa_hf", [128, D], F32)
        hb = al("pa_hb", [128, D], BF16)
        hT = [al("pa_hT%d" % i, [128, 8, 128], BF16) for i in range(2)]
        rt = [al("pa_rt%d" % i, [128, 256], F32) for i in range(2)]
        of = [al("pa_of%d" % i, [128, 512], F32) for i in range(3)]
        ob = [al("pa_ob%d" % i, [128, 512], BF16) for i in range(3)]
        oT = [al("pa_oT%d" % i, [128, 4, 128], BF16) for i in range(3)]
        vaug = [al("pa_va%d" % i, [128, 8, 72], BF16) for i in range(3)]
        for v in vaug:
            b.memset(v, 1.0)
        kid = al("pa_kid", [128, 128], BF16)
        sm = al("pa_sm", [128, 32], F32)
        cnt = {"of": 0, "ob": 0, "oT": 0, "va": 0}

        def nxt(lst, key):
            i = cnt[key]
            cnt[key] += 1
            return lst[i % len(lst)]

        blocks = [("P", p) for p in range(NKB)] + ([("S", 0)] if do_sample else [])
        for bi, (grp, p) in enumerate(blocks):
            M = 128 if grp == "P" else 32
            full = (grp == "S") or (p >= SLOT0)
            s = p - SLOT0
            x = xt[bi % 2]
            c = cs[bi % 2]
            if grp == "P":
                b.dma(x, T["xk"][p * 128:(p + 1) * 128, :])
                b.dma(c, T["cstab"][p * 128:(p + 1) * 128, :])
            else:
                b.dma(x[:M, :], T["xs"])
                b.dma(c[:M, :], T["cstab_s"])
            b.tt(hf[:M, :], x[:M, :], sc[grp][:M, :], ALU.mult)
            b.tt(hb[:M, :], hf[:M, :], sh[grp][:M, :], ALU.add)
            h_T = hT[bi % 2]
            b.transposes(h_T, hb, M, 8, g["identb"])
            rows = slice(p * 128, (p + 1) * 128)
            for (name, c0, c1, konly) in CH:
                if not (full or konly):
                    continue
                n = c1 - c0
                ps = b.bank("mm", [0, 1, 2, 3])
                for k in range(8):
                    b.mm(ps[:M, :n], h_T[:, k, :M], w[:, k, c0:c1], k == 0, k == 7)
                if name in ("qa", "qi", "qb"):
                    o_b = nxt(ob, "ob")
                    if name == "qb":
                        b.cp(o_b[:M, :], ps[:M, :512])
                    else:
                        rope(b, o_b, ps, M, 8, c, rt)
                    o_T = nxt(oT, "oT")
                    b.transposes(o_T, o_b, M, 4, g["identb"])
                    qi_ = {"qa": 0, "qi": 1, "qb": 2}[name]
                    if grp == "P":
                        b.dma(T[name + "T_s"][s], o_T)
                    else:
                        b.dma(T["sq_s"][qi_], o_T[:, :, 0:32])
                elif name in ("ka", "kb"):
                    o_f = nxt(of, "of")
                    if name == "ka":
                        rope(b, o_f, ps, M, 8, c, rt)
                    else:
                        b.cp(o_f[:M, :], ps[:M, :512])
                    o_b = nxt(ob, "ob")
                    b.cp(o_b[:M, :], o_f[:M, :], eng="pool")
                    if grp == "P":
                        b.dma(T["a" + "k_p" if name == "ka" else "bk_p"][rows, :], o_f)
                        o_T = nxt(oT, "oT")
                        b.transposes(o_T, o_b, M, 4, g["identb"])
                        b.dma(T[name + "T_s"][:, :, rows], o_T)
                    else:
                        b.dma(T["ak_s" if name == "ka" else "bk_s"], o_f[:M, :])
                        b.dma(T["sk_s"][0 if name == "ka" else 2][:, 0:512], o_b[:M, :])
                elif name in ("va", "vb"):
                    o_f = nxt(of, "of")
                    b.cp(o_f[:M, :], ps[:M, :512])
                    va = nxt(vaug, "va")
                    b.cp(va[:M, :, 0:64], o_f[:M, :].rearrange("p (h d) -> p h d", h=8), eng="pool")
                    if grp == "P":
                        b.dma(T["av_p" if name == "va" else "bv_p"][rows, :], o_f)
                        b.dma(T[name + "_s"][p], va.rearrange("p h d -> p (h d)"))
                    else:
                        b.dma(T["av_s" if name == "va" else "bv_s"], o_f[:M, :])
                        o_b = nxt(ob, "ob")
                        b.cp(o_b[:M, :], o_f[:M, :], eng="dve")
                        b.dma(T["sk_s"][1 if name == "va" else 3][:, 0:512], o_b[:M, :])
                elif name == "kiwi":
                    o_f = nxt(of, "of")
                    rope(b, o_f, ps, M, 1, c, rt)
                    b.cp(kid[:M, 0:64], o_f[:M, 0:64], eng="pool")
                    b.cp(kid[:M, 64:128], o_f[:M, 0:64], eng="pool")
                    o_T = nxt(oT, "oT")
                    b.transposes(o_T, kid, M, 1, g["identb"])
                    if grp == "P":
                        b.dma(T["aki_p"][rows, :], o_f[:, 0:64])
                        b.dma(T["kiT2_s"][:, rows], o_T[:, 0, :])
                    else:
                        b.dma(T["aki_s"], o_f[:M, 0:64])
                        b.dma(T["skiT_s"], o_T[:, 0, 0:32])
                    if full:
                        b.ts(sm[:M, 0:8], ps[:M, 64:72], WI_SCALE, None, op0=ALU.mult)
                        b.dma(T["wi_s"][s] if grp == "P" else T["swi_s"], sm[:M, 0:8])
                elif name == "fb":
                    b.tt(sm[:M, 8:16], ps[:M, 0:8], bfb[:M, :], ALU.add)
                    b.act(sm[:M, 16:24], sm[:M, 8:16], AF.Exp, scale=-1.0)
                    b.act(sm[:M, 24:32], sm[:M, 16:24], AF.Ln, bias=g["ones_f"][:M, 0:1], scale=1.0)
                    if grp == "P":
                        b.ts(g["LF"][:, p, :], sm[:, 24:32], -1.0, None, op0=ALU.mult)
                        b.dma(T["blf_p"][rows, :], g["LF"][:, p, :])
                    else:
                        b.ts(sm[:M, 8:16], sm[:M, 24:32], -1.0, None, op0=ALU.mult)
                        b.dma(T["blf_s"], sm[:M, 8:16])
                        b.dma(T["slf_s"], sm[:M, 8:16])
    b.S.barrier()


GROUPS = [[0, 1], [2, 3, 4, 5], [6, 7, 8, 9], [10, 11, 12, 13], [14, 15, 16, 17]]


def residual_ln(b, al_tiles, ps_list, xt, M, gp1, lng, lnb, out_dram, extra=None):
    acc, tmp, stats, mv, res = al_tiles
    for n2, ps in enumerate(ps_list):
        cols = slice(n2 * 512, (n2 + 1) * 512)
        if extra is not None:
            b.tt(acc[:M, cols], ps[:M, :512], extra[:M, cols], ALU.add)
            b.tt(acc[:M, cols], acc[:M, cols], gp1[:M, cols], ALU.mult, eng="pool")
        else:
            b.tt(acc[:M, cols], ps[:M, :512], gp1[:M, cols], ALU.mult)
    b.stt(tmp[:M, :], xt[:M, :], ALPHA, acc[:M, :], ALU.mult, ALU.add)
    b.layernorm_affine(res, tmp, M, D, lng, lnb, acc, stats, mv)
    b.dma(out_dram, res[:M, :])


def ln_tiles(al, pfx):
    return (al(pfx + "_acc", [128, D], F32), al(pfx + "_tmp", [128, D], F32), al(pfx + "_st", [128, 4, 6], F32),
            al(pfx + "_mv", [128, 4], F32), al(pfx + "_res", [128, D], F32))


def phaseB1(b, T, g):
    nc = b.nc
    with ExitStack() as es:
        def al(name, shape, dt):
            return es.enter_context(nc.sbuf_tensor(name, list(shape), dt)).ap()
        kaT = al("b1_kaT", [128, 4, NKB * 128], BF16)
        for i in range(4):
            b.dma(kaT[:, i, :], T["kaT_s"][:, i, :])
        kiT2 = al("b1_kiT2", [128, NKB * 128], BF16)
        b.dma(kiT2, T["kiT2_s"])
        vaA = al("b1_vaA", [128, NKB, 576], BF16)
        for i in range(4):
            b.dma(vaA[:, 8 * i:8 * i + 8, :], T["va_s"][8 * i:8 * i + 8].rearrange("j p f -> p j f"))
        JM = al("b1_JM", [128, NKB * 128], BF16)
        b.dma(JM, T["jm"].partition_broadcast(128), q="pool")
        qaT = [al("b1_qa%d" % i, [128, 4, 128], BF16) for i in range(2)]
        qiT = [al("b1_qi%d" % i, [128, 4, 128], BF16) for i in range(2)]
        wi = [al("b1_wi%d" % i, [128, 8], F32) for i in range(2)]
        score = [al("b1_score%d" % i, [128, NKB * 128], F32) for i in range(2)]
        junk = al("b1_junk", [128, NKB * 128], BF16)
        sel = [al("b1_sel%d" % i, [128, NKB * 128], BF16) for i in range(2)]
        selT = [al("b1_selT%d" % i, [128, NKB, 128], BF16) for i in range(2)]
        rr = [al("b1_r%d" % i, [128, 512], F32) for i in range(3)]
        smt = [al("b1_sm%d" % i, [128, 8 + NBIS], F32) for i in range(2)]
        pT = [al("b1_pT%d" % i, [128, 4, 128], BF16) for i in range(3)]
        pTm = [al("b1_pTm%d" % i, [128, 4, 128], BF16) for i in range(3)]
        rc = al("b1_rc", [128, 8], F32)
        onb = al("b1_onb", [128, 512], BF16)
        oT = [al("b1_oT%d" % i, [128, 4, 128], BF16) for i in range(2)]
        ctr = {"r": 0, "p": 0}

        def prep(s):
            nkb = SLOT0 + 1 + s
            NK = nkb * 128
            diag = slice((nkb - 1) * 128, nkb * 128)
            qa, qi, w_ = qaT[s % 2], qiT[s % 2], wi[s % 2]
            sc_, sl_, sT_, sm = score[s % 2], sel[s % 2], selT[s % 2], smt[s % 2]
            halfs = sm[:, 8:8 + NBIS]
            b.dma(qa, T["qaT_s"][s]); b.dma(qi, T["qiT_s"][s]); b.dma(w_, T["wi_s"][s])
            yield
            for cix in range((nkb + 3) // 4):
                nblk = min(4, nkb - 4 * cix)
                n = nblk * 128
                cols = slice(512 * cix, 512 * cix + n)
                for h in range(8):
                    hp, base = h // 2, 64 * (h % 2)
                    ps = b.bank("mm", [0, 1, 2, 3])
                    b.mm(ps[:, :n], qi[base:base + 64, hp, :], kiT2[base:base + 64, cols], True, True)
                    r = rr[ctr["r"] % 3]
                    ctr["r"] += 1
                    b.act(r[:, :n], ps[:, :n], AF.Relu)
                    if h == 0:
                        b.ts(sc_[:, cols], r[:, :n], w_[:, 0:1], None, op0=ALU.mult)
                    else:
                        b.stt(sc_[:, cols], r[:, :n], w_[:, h:h + 1], sc_[:, cols], ALU.mult, ALU.add)
                    yield
            b.reduce(sm[:, 0:1], sc_[:, :NK], ALU.max, absval=True)
            b.ts(sm[:, 1:2], sm[:, 0:1], 1.0001, 1e-20, op0=ALU.mult, op1=ALU.add)
            b.ts(halfs, g["pow2"], sm[:, 1:2], None, op0=ALU.mult)
            b.tt(sc_[:, :NK], sc_[:, :NK], JM[:, :NK], ALU.add)
            b.tt(sc_[:, diag], sc_[:, diag], g["triqk"], ALU.add)
            b.memset(sm[:, 2:3], 0.0, eng="dve")
            yield
            for k in range(NBIS):
                b.ts(junk[:, :NK], sc_[:, :NK], sm[:, 2:3], None, op0=ALU.is_ge, op1=ALU.add, accum=sm[:, 3:4])
                if k < NBIS - 1:
                    b.ts(sm[:, 4:5], sm[:, 3:4], 255.5, -0.5, op0=ALU.is_ge, op1=ALU.add)
                    b.stt(sm[:, 2:3], sm[:, 4:5], halfs[:, k:k + 1], sm[:, 2:3], ALU.mult, ALU.add)
                else:
                    b.ts(sm[:, 4:5], sm[:, 3:4], 255.5, -1.0, op0=ALU.is_ge, op1=ALU.add)
                    b.stt(sm[:, 5:6], sm[:, 4:5], halfs[:, k:k + 1], sm[:, 2:3], ALU.mult, ALU.add)
                yield
            b.ts(sl_[:, :NK], sc_[:, :NK], sm[:, 5:6], None, op0=ALU.is_ge)
            yield
            b.transposes(sT_, sl_, 128, nkb, g["identb"], evac="act")
            yield

        def attend(s):
            nkb = SLOT0 + 1 + s
            qa, sT_ = qaT[s % 2], selT[s % 2]
            pbO = [b.pb[6].rearrange("p (h d) -> p h d", h=4), b.pb[7].rearrange("p (h d) -> p h d", h=4)]
            for j in range(nkb):
                kc = slice(j * 128, (j + 1) * 128)
                for par in range(2):
                    ps = b.bank("mm", [0, 1, 2, 3]).rearrange("p (h q) -> p h q", h=4)
                    base = 64 * par
                    for idx in range(4):
                        b.mm(ps[:, idx, :], kaT[base:base + 64, idx, kc], qa[base:base + 64, idx, :], True, True)
                    p_, pm = pT[ctr["p"] % 3], pTm[ctr["p"] % 3]
                    ctr["p"] += 1
                    b.act(p_, ps, AF.Exp, scale=0.125)
                    b.tt(pm, p_, sT_[:, j, :].unsqueeze(1).to_broadcast([128, 4, 128]), ALU.mult, eng="pool")
                    for idx in range(4):
                        h = 2 * idx + par
                        b.mm(pbO[h // 4][:, h % 4, 0:68], pm[:, idx, :], vaA[:, j, h * 72:h * 72 + 68],
                             j == 0 and par == 0 and idx in (0, 2), j == nkb - 1)
                    yield
            for hg in range(2):
                b.ts(rc[:, 4 * hg:4 * hg + 4], pbO[hg][:, :, 64], 1e-30, None, op0=ALU.max)
            b.recip(rc, rc)
            for hg in range(2):
                b.tt(onb[:, hg * 256:(hg + 1) * 256].rearrange("p (h d) -> p h d", h=4), pbO[hg][:, :, 0:64],
                     rc[:, 4 * hg:4 * hg + 4].unsqueeze(2).to_broadcast([128, 4, 64]), ALU.mult)
            o_T = oT[s % 2]
            b.transposes(o_T, onb, 128, 4, g["identb"])
            b.dma(T["mixTa_s"][s], o_T)
            yield

        def count(gen):
            return gen

        for _ in prep(0):
            pass
        for s in range(NS):
            ga = attend(s)
            gp = prep(s + 1) if s + 1 < NS else iter(())
            n_att = 2 * (SLOT0 + 1 + s) + 1
            n_prep = (8 * ((SLOT0 + 2 + s + 3) // 4) + NBIS + 5) if s + 1 < NS else 0
            acc = 0.0
            ratio = n_prep / float(n_att)
            done_p = False
            for _ in ga:
                acc += ratio
                while acc >= 1.0 and not done_p:
                    acc -= 1.0
                    try:
                        next(gp)
                    except StopIteration:
                        done_p = True
            for _ in gp:
                pass
    b.S.barrier()


def cumsum_blocks(b, g, al, LF, nblk, pfx):
    n = nblk * 8
    ck = al(pfx + "_ck", [128, nblk, 8], F32)
    ta = al(pfx + "_ta", [128, nblk, 8], F32)
    tb = al(pfx + "_tb", [128, nblk, 8], F32)
    tot = al(pfx + "_tot", [128, nblk, 8], F32)
    lf2 = LF.rearrange("p j h -> p (j h)")
    ps1 = b.bank("mm", [0, 1, 2, 3])
    b.mm(ps1[:, :n], g["tinc"], lf2, True, True)
    ps2 = b.bank("mm", [0, 1, 2, 3])
    b.mm(ps2[:, :n], g["ones_f"], lf2, True, True)
    b.cp(tot.rearrange("p j h -> p (j h)"), ps2[:, :n])
    b.cp(ta, tot, eng="dve")
    d = 1
    cur, oth = ta, tb
    while d < nblk:
        b.tt(oth[:, d:, :], cur[:, d:, :], cur[:, :nblk - d, :], ALU.add)
        b.cp(oth[:, :d, :], cur[:, :d, :], eng="dve")
        cur, oth = oth, cur
        d *= 2
    incl = cur
    b.tt(oth, incl, tot, ALU.subtract)
    b.tt(ck.rearrange("p j h -> p (j h)"), ps1[:, :n], oth.rearrange("p j h -> p (j h)"), ALU.add)
    return ck, incl


def phaseB2(b, T, g):
    nc = b.nc
    with ExitStack() as es:
        def al(name, shape, dt):
            return es.enter_context(nc.sbuf_tensor(name, list(shape), dt)).ap()
        kbT = al("b2_kbT", [128, 4, NKB * 128], BF16)
        for i in range(4):
            b.dma(kbT[:, i, :], T["kbT_s"][:, i, :])
        vbA = al("b2_vbA", [128, NKB, 576], BF16)
        for i in range(4):
            b.dma(vbA[:, 8 * i:8 * i + 8, :], T["vb_s"][8 * i:8 * i + 8].rearrange("j p f -> p j f"))
        wout = al("b2_wout", [128, 8, D], BF16)
        for i in range(2):
            b.dma(wout[:, :, i * 512:(i + 1) * 512],
                  T["w_out_ab"][:, i * 512:(i + 1) * 512].rearrange("(k p) n -> p k n", p=128), q="pool")
        gp1 = al("b2_gp1", [128, D], F32); b.dma(gp1, T["modP"][0, :, 2 * D:3 * D])
        lng = al("b2_lng", [128, D], F32); b.dma(lng, T["ln1_g"][0:1, :].partition_broadcast(128))
        lnb = al("b2_lnb", [128, D], F32); b.dma(lnb, T["ln1_b"][0:1, :].partition_broadcast(128))
        ck, incl = cumsum_blocks(b, g, al, g["LF"], NKB, "b2")
        nck = al("b2_nck", [128, NKB, 8], F32)
        b.tt(nck, g["blkbias"].unsqueeze(2).to_broadcast([128, NKB, 8]), ck, ALU.subtract)
        bias = [al("b2_bias%d" % i, [128, NKB, 8], F32) for i in range(2)]
        qbT = [al("b2_qb%d" % i, [128, 4, 128], BF16) for i in range(2)]
        mTa = [al("b2_mTa%d" % i, [128, 4, 128], BF16) for i in range(2)]
        xt = [al("b2_x%d" % i, [128, D], F32) for i in range(2)]
        pT = [al("b2_pT%d" % i, [128, 4, 128], BF16) for i in range(3)]
        tbias = [al("b2_tb%d" % i, [128, 4, 128], F32) for i in range(3)]
        rc = al("b2_rc", [128, 8], F32)
        onb = al("b2_onb", [128, 512], BF16)
        oT = [al("b2_oT%d" % i, [128, 4, 128], BF16) for i in range(2)]
        lt = ln_tiles(al, "b2")
        it = 0
        for s in range(NS):
            nkb = SLOT0 + 1 + s
            qb, bs, mta, x = qbT[s % 2], bias[s % 2], mTa[s % 2], xt[s % 2]
            b.dma(qb, T["qbT_s"][s]); b.dma(mta, T["mixTa_s"][s])
            b.dma(x, T["xk"][(SLOT0 + s) * 128:(SLOT0 + s + 1) * 128, :])
            b.tt(bs[:, :nkb, :], nck[:, :nkb, :], incl[:, nkb - 1, :].unsqueeze(1).to_broadcast([128, nkb, 8]), ALU.add)
            b.ts(bs[:, :nkb, :], bs[:, :nkb, :], 8.0, None, op0=ALU.mult, eng="pool")
            pbO = [b.pb[6].rearrange("p (h d) -> p h d", h=4), b.pb[7].rearrange("p (h d) -> p h d", h=4)]
            for j in range(nkb):
                kc = slice(j * 128, (j + 1) * 128)
                for par in range(2):
                    ps = b.bank("mm", [0, 1, 2, 3]).rearrange("p (h q) -> p h q", h=4)
                    base = 64 * par
                    for idx in range(4):
                        b.mm(ps[:, idx, :], kbT[base:base + 64, idx, kc], qb[base:base + 64, idx, :], True, True)
                    p_ = pT[it % 3]
                    tb_ = tbias[it % 3]
                    it += 1
                    b.tt(tb_, ps, bs[:, j, par::2].unsqueeze(2).to_broadcast([128, 4, 128]), ALU.add)
                    b.act(p_, tb_, AF.Exp, scale=0.125)
                    if j == nkb - 1:
                        b.tt(p_, p_, g["triT01"].unsqueeze(1).to_broadcast([128, 4, 128]), ALU.mult)
                    for idx in range(4):
                        h = 2 * idx + par
                        b.mm(pbO[h // 4][:, h % 4, 0:68], p_[:, idx, :], vbA[:, j, h * 72:h * 72 + 68],
                             j == 0 and par == 0 and idx in (0, 2), j == nkb - 1)
            for hg in range(2):
                b.ts(rc[:, 4 * hg:4 * hg + 4], pbO[hg][:, :, 64], 1e-30, None, op0=ALU.max)
            b.recip(rc, rc)
            for hg in range(2):
                b.tt(onb[:, hg * 256:(hg + 1) * 256].rearrange("p (h d) -> p h d", h=4), pbO[hg][:, :, 0:64],
                     rc[:, 4 * hg:4 * hg + 4].unsqueeze(2).to_broadcast([128, 4, 64]), ALU.mult)
            o_T = oT[s % 2]
            b.transposes(o_T, onb, 128, 4, g["identb"])
            pss = []
            for n2 in range(2):
                ps = b.bank("mm", [0, 1, 2, 3])
                for kc_ in range(8):
                    lhs = mta[:, kc_, :] if kc_ < 4 else o_T[:, kc_ - 4, :]
                    b.mm(ps[:, :512], lhs, wout[:, kc_, n2 * 512:(n2 + 1) * 512], kc_ == 0, kc_ == 7)
                pss.append(ps)
            residual_ln(b, lt, pss, x, 128, gp1, lng, lnb, T["x1_s"][s * 128:(s + 1) * 128, :])
    b.S.barrier()


def to_token_major(b, g, dst_rows, src, nrow, nch, al_row):
    row = al_row
    done = 0
    while done < nch:
        n = min(4, nch - done)
        ps = b.bank("mm", [0, 1, 2, 3])
        for k in range(n):
            b.tr(ps[:nrow, k * 128:(k + 1) * 128], src[:, done + k, :], g["identf"])
        b.cp(row[:nrow, done * 128:(done + n) * 128], ps[:nrow, :n * 128])
        done += n
    b.dma(dst_rows, row[:nrow, :])


def phaseC(b, T, g, li, xin_p, xout_p, xin_s, xout_s, do_sample=True):
    nc = b.nc
    with ExitStack() as es0:
        def al0(name, shape, dt):
            return es0.enter_context(nc.sbuf_tensor(name, list(shape), dt)).ap()
        pfx = "c%d" % li
        lastP = al0(pfx + "_lastP", [128, 44, 2], F32)
        lastS = al0(pfx + "_lastS", [128, 44, 8], F32)
        wcT = al0(pfx + "_wcT", [128, 44, 4], F32)
        stT = al0(pfx + "_stT", [128, 44, 8], F32)
        with ExitStack() as es:
            wrow = es.enter_context(nc.sbuf_tensor(pfx + "_wrow", [8, DUP], F32)).ap()
            b.dma(wrow[0:3, :], T["w_conv"][li])
            b.dma(wrow[3:4, :], T["b_conv"][li:li + 1, :])
            ps = b.bank("mm", [0, 1, 2, 3])
            psv = ps[:, 0:176].rearrange("p (c k) -> p c k", k=4)
            for ch in range(44):
                b.tr(psv[:, ch, :], wrow[0:4, ch * 128:(ch + 1) * 128], g["identf"][0:4, 0:4])
            b.cp(wcT, psv)
            if do_sample:
                b.dma(wrow[0:8, :], T["state_conv"][li])
                ps = b.bank("mm", [0, 1, 2, 3])
                psv = ps[:, 0:352].rearrange("p (c k) -> p c k", k=8)
                for ch in range(44):
                    b.tr(psv[:, ch, :], wrow[0:8, ch * 128:(ch + 1) * 128], g["identf"][0:8, 0:8])
                b.cp(stT, psv)
        b.S.barrier()
        b.memset(lastP, 0.0)
        for fh in range(2):
            with ExitStack() as es:
                def al(name, shape, dt):
                    return es.enter_context(nc.sbuf_tensor(name, list(shape), dt)).ap()
                p2 = pfx + "p%d" % fh
                wug = al(p2 + "_wug", [128, 8, 1408], BF16)
                wuv = al(p2 + "_wuv", [128, 8, 1408], BF16)
                wdn = al(p2 + "_wdn", [128, 11, D], BF16)
                for j in range(2):
                    cs_ = slice(1408 * fh + 704 * j, 1408 * fh + 704 * (j + 1))
                    b.dma(wug[:, :, 704 * j:704 * (j + 1)], T["w_up"][li, :, cs_].rearrange("(k p) n -> p k n", p=128), q="pool")
                    cs_ = slice(DFF + 1408 * fh + 704 * j, DFF + 1408 * fh + 704 * (j + 1))
                    b.dma(wuv[:, :, 704 * j:704 * (j + 1)], T["w_up"][li, :, cs_].rearrange("(k p) n -> p k n", p=128), q="pool")
                b.dma(wdn, T["w_down"][li, 1408 * fh:1408 * (fh + 1), :].rearrange("(k p) n -> p k n", p=128), q="pool")
                bc = {}
                for grp, M, src in (("P", 128, T["modP"]), ("S", 32, T["modS"])):
                    if grp == "S" and not do_sample:
                        continue
                    bc[grp] = {}
                    for nm, c0 in (("sh", 3), ("sc", 4)) + ((("g", 5),) if fh == 1 else ()):
                        t_ = al(p2 + "_%s%s" % (nm, grp), [M, D], F32)
                        b.dma(t_, src[li, :, c0 * D:(c0 + 1) * D])
                        bc[grp][nm] = t_
                if fh == 1:
                    lng = al(p2 + "_lng", [128, D], F32); b.dma(lng, T["ln2_g"][li:li + 1, :].partition_broadcast(128))
                    lnb = al(p2 + "_lnb", [128, D], F32); b.dma(lnb, T["ln2_b"][li:li + 1, :].partition_broadcast(128))
                    lt = ln_tiles(al, p2)
                    ffp = al(p2 + "_ffp", [128, D], F32)
                else:
                    ffo = [al(p2 + "_ffo%d" % i, [128, D], F32) for i in range(2)]
                xt = [al(p2 + "_x%d" % i, [128, D], F32) for i in range(4)]
                hf = al(p2 + "_hf", [128, D], F32)
                hb = al(p2 + "_hb", [128, D], BF16)
                h2T = al(p2 + "_h2T", [128, 8, 512], BF16)
                gT = al(p2 + "_gT", [128, 11, 512], BF16)
                ut = [al(p2 + "_u%d" % i, [128, 520], F32) for i in range(4)]
                yt = [al(p2 + "_y%d" % i, [128, 512], F32) for i in range(4)]
                glt = [al(p2 + "_gl%d" % i, [128, 512], F32) for i in range(2)]
                halo = al(p2 + "_halo", [128, 22, 2], F32)
                b.memset(halo, 0.0)
                groups = [("P", sl) for sl in GROUPS] + ([("S", None)] if do_sample else [])
                for gi, (grp, slots) in enumerate(groups):
                    if grp == "P":
                        M, nsl, nseg, L = 128, len(slots), 1, 128 * len(slots)
                    else:
                        M, nsl, nseg, L = 32, 1, 4, 8
                    Mtot = M * nsl
                    for si in range(nsl):
                        x = xt[si]
                        if grp == "P":
                            b.dma(x, xin_p[slots[si] * 128:(slots[si] + 1) * 128, :])
                        else:
                            b.dma(x[:M, :], xin_s)
                        b.tt(hf[:M, :], x[:M, :], bc[grp]["sc"][:M, :], ALU.mult)
                        b.tt(hb[:M, :], hf[:M, :], bc[grp]["sh"][:M, :], ALU.add, eng="pool")
                        b.transposes(h2T[:, :, si * M:(si + 1) * M], hb, M, 8, g["identb"])
                    for gcl in range(11):
                        ys = []
                        for wi_, (wsrc, choff) in enumerate(((wug, 0), (wuv, 22))):
                            ch = choff + 11 * fh + gcl
                            hch = 11 * wi_ + gcl
                            ps = b.bank("mm", [0, 1, 2, 3])
                            for k in range(8):
                                b.mm(ps[:, :Mtot], wsrc[:, k, gcl * 128:(gcl + 1) * 128], h2T[:, k, :Mtot], k == 0, k == 7)
                            ti = 2 * (gcl % 2) + wi_
                            u = ut[ti][:, 0:nseg * (L + 2)].rearrange("p (s l) -> p s l", s=nseg)
                            b.cp(u[:, :, 2:2 + L], ps[:, :Mtot].rearrange("p (s l) -> p s l", s=nseg))
                            if grp == "P":
                                if gi == 1:
                                    b.ts(u[:, 0, 0:2], halo[:, hch, :], g["haloflag"][:, 0:1], None, op0=ALU.mult, eng="pool")
                                else:
                                    b.cp(u[:, 0, 0:2], halo[:, hch, :], eng="pool")
                                b.cp(halo[:, hch, :], u[:, 0, L:L + 2], eng="pool")
                                if gi == len(GROUPS) - 1:
                                    b.cp(lastP[:, ch, :], u[:, 0, L:L + 2], eng="pool")
                            else:
                                b.cp(u[:, :, 0:2], stT[:, ch, :].rearrange("p (s r) -> p s r", s=4), eng="pool")
                                b.cp(lastS[:, ch, :].rearrange("p (s r) -> p s r", s=4), u[:, :, L:L + 2], eng="pool")
                            y = yt[ti][:, 0:nseg * L].rearrange("p (s l) -> p s l", s=nseg)
                            b.ts(y, u[:, :, 0:L], wcT[:, ch, 0:1], wcT[:, ch, 3:4], op0=ALU.mult, op1=ALU.add)
                            b.stt(y, u[:, :, 1:L + 1], wcT[:, ch, 1:2], y, ALU.mult, ALU.add)
                            b.stt(y, u[:, :, 2:L + 2], wcT[:, ch, 2:3], y, ALU.mult, ALU.add)
                            ys.append(yt[ti][:, 0:Mtot])
                        gl = glt[gcl % 2]
                        b.act(gl[:, :Mtot], ys[0], AF.Gelu)
                        b.tt(gT[:, gcl, :Mtot], gl[:, :Mtot], ys[1], ALU.mult, eng="pool")
                    for si in range(nsl):
                        tok = slice(si * M, (si + 1) * M)
                        pss = []
                        for n2 in range(2):
                            ps = b.bank("mm", [0, 1, 2, 3])
                            for gcl in range(11):
                                b.mm(ps[:M, :512], gT[:, gcl, tok], wdn[:, gcl, n2 * 512:(n2 + 1) * 512], gcl == 0, gcl == 10)
                            pss.append(ps)
                        if grp == "P":
                            rows = slice(slots[si] * 128, (slots[si] + 1) * 128)
                            fdst, xdst = T["ffp_s"][rows, :], xout_p[rows, :]
                        else:
                            fdst, xdst = T["ffps_s"], xout_s
                        if fh == 0:
                            fo = ffo[si % 2]
                            for n2 in range(2):
                                b.cp(fo[:M, n2 * 512:(n2 + 1) * 512], pss[n2][:M, :512])
                            b.dma(fdst, fo[:M, :])
                        else:
                            b.dma(ffp[:M, :], fdst)
                            residual_ln(b, lt, pss, xt[si], M, bc[grp]["g"], lng, lnb, xdst, extra=ffp)
            b.S.barrier()
        with ExitStack() as es:
            row = es.enter_context(nc.sbuf_tensor(pfx + "_row", [8, DUP], F32)).ap()
            to_token_major(b, g, T["conv_p"][li], lastP, 2, 44, row)
            if do_sample:
                to_token_major(b, g, T["conv_s"][li], lastS, 8, 44, row)
    b.S.barrier()


def phaseD(b, T, g, do_sample=True):
    nc = b.nc
    with ExitStack() as es:
        def al(name, shape, dt):
            return es.enter_context(nc.sbuf_tensor(name, list(shape), dt)).ap()
        w = al("d1_w", [128, 8, 2 * DCG], BF16)
        for j in range(8):
            b.dma(w[:, :, j * 512:(j + 1) * 512], T["w_in_c"][:, j * 512:(j + 1) * 512].rearrange("(k p) n -> p k n", p=128), q="pool")
        wsf = al("d1_wsf", [128, 8, 128], F32)
        b.dma(wsf, T["w_spatial"].rearrange("g t s -> t g s"))
        wsb = al("d1_wsb", [128, 8, 128], BF16)
        b.cp(wsb, wsf, eng="dve")
        wsT = al("d1_wsT", [128, 8, 128], BF16)
        b.transposes(wsT, wsb.rearrange("p g s -> p (g s)"), 128, 8, g["identb"])
        b.tt(wsT, wsT, g["triT01"].unsqueeze(1).to_broadcast([128, 8, 128]), ALU.mult)
        bsp = al("d1_bsp", [128, 8, 128], F32)
        b.dma(bsp.rearrange("p g t -> p (g t)"), T["b_spatial"].rearrange("g t -> (g t)").unsqueeze(0).partition_broadcast(128))
        lvg = al("d1_lvg", [128, DCG], F32); b.dma(lvg, T["lnv_g"].partition_broadcast(128))
        lvb = al("d1_lvb", [128, DCG], F32); b.dma(lvb, T["lnv_b"].partition_broadcast(128))
        bc = {}
        for grp, M, src in (("P", 128, T["modP"]), ("S", 32, T["modS"])):
            if grp == "S" and not do_sample:
                continue
            bc[grp] = {}
            for nm, c0 in (("sh", 0), ("sc", 1)):
                t_ = al("d1_%s%s" % (nm, grp), [M, D], F32)
                b.dma(t_, src[1, :, c0 * D:(c0 + 1) * D])
                bc[grp][nm] = t_
        if do_sample:
            wsS = al("d1_wsS", [32, 8, 32], BF16)
            b.memset(wsS, 0.0)
            for i in range(4):
                b.dma(wsS[i * 8:(i + 1) * 8, :, i * 8:(i + 1) * 8], wsT[0:8, :, 0:8])
            bspS = al("d1_bspS", [128, 8, 32], F32)
            for i in range(4):
                b.dma(bspS[:, :, i * 8:(i + 1) * 8], T["b_spatial"][:, 0:8].unsqueeze(0).partition_broadcast(128))
        xt = al("d1_x", [128, D], F32)
        hf = al("d1_hf", [128, D], F32)
        hb = al("d1_hb", [128, D], BF16)
        hT = al("d1_hT", [128, 8, 512], BF16)
        uT = al("d1_uT", [128, 16, 512], BF16)
        vf = al("d1_vf", [128, DCG], F32)
        vt = al("d1_vt", [128, DCG], F32)
        vln = al("d1_vln", [128, DCG], F32)
        vlb = al("d1_vlb", [128, DCG], BF16)
        st = al("d1_st", [128, 4, 6], F32)
        mv = al("d1_mv", [128, 4], F32)
        mx = [al("d1_mx%d" % i, [128, 4, 128], F32) for i in range(2)]
        gTt = [al("d1_gT%d" % i, [128, 16, 128], BF16) for i in range(2)]
        groups = [("P", sl) for sl in GROUPS] + ([("S", None)] if do_sample else [])
        it = 0
        for gi, (grp, slots) in enumerate(groups):
            M = 128 if grp == "P" else 32
            nsl = len(slots) if grp == "P" else 1
            Mtot = M * nsl
            for si in range(nsl):
                if grp == "P":
                    b.dma(xt, T["x2_s"][slots[si] * 128:(slots[si] + 1) * 128, :])
                else:
                    b.dma(xt[:M, :], T["xs2_s"])
                b.tt(hf[:M, :], xt[:M, :], bc[grp]["sc"][:M, :], ALU.mult)
                b.tt(hb[:M, :], hf[:M, :], bc[grp]["sh"][:M, :], ALU.add, eng="pool")
                b.transposes(hT[:, :, si * M:(si + 1) * M], hb, M, 8, g["identb"])
            for cc in range(16):
                ps = b.bank("mm", [0, 1, 2, 3])
                for k in range(8):
                    b.mm(ps[:, :Mtot], w[:, k, cc * 128:(cc + 1) * 128], hT[:, k, :Mtot], k == 0, k == 7)
                b.act(uT[:, cc, :Mtot], ps[:, :Mtot], AF.Gelu)
            for si in range(nsl):
                tok = slice(si * M, (si + 1) * M)
                for c4 in range(4):
                    ps = b.bank("mm", [0, 1, 2, 3])
                    for k in range(8):
                        b.mm(ps[:M, :512], hT[:, k, tok], w[:, k, DCG + c4 * 512:DCG + (c4 + 1) * 512], k == 0, k == 7)
                    b.act(vf[:M, c4 * 512:(c4 + 1) * 512], ps[:M, :512], AF.Gelu)
                b.layernorm_affine(vln, vf, M, DCG, lvg, lvb, vt, st, mv)
                b.cp(vlb[:M, :], vln[:M, :], eng="dve")
                if grp == "S":
                    b.dma(T["cv_s"], vln[:M, :])
                gt = gTt[it % 2]
                it += 1
                for c4 in range(4):
                    ps = b.bank("mm", [0, 1, 2, 3])
                    psv = ps[:, 0:4 * M].rearrange("p (c t) -> p c t", c=4)
                    for k in range(4):
                        cc = 4 * c4 + k
                        rhs = wsT[:, cc // 2, :] if grp == "P" else wsS[:, cc // 2, :]
                        b.mm(psv[:, k, :], vlb[:M, cc * 128:(cc + 1) * 128], rhs, True, True)
                    m_ = mx[c4 % 2]
                    bs_ = bsp if grp == "P" else bspS
                    b.tt(m_[:, :, :M].rearrange("p (a c) t -> p a c t", a=2), psv.rearrange("p (a c) t -> p a c t", a=2),
                         bs_[:, 2 * c4:2 * c4 + 2, :].unsqueeze(2).to_broadcast([128, 2, 2, M]), ALU.add)
                    b.tt(gt[:, 4 * c4:4 * c4 + 4, :M], m_[:, :, :M], uT[:, 4 * c4:4 * c4 + 4, tok], ALU.mult, eng="pool")
                if grp == "P":
                    b.dma(T["gT_s"][slots[si]], gt)
                else:
                    b.dma(T["gTs_s"], gt[:, :, 0:32])
    b.S.barrier()
    with ExitStack() as es:
        def al(name, shape, dt):
            return es.enter_context(nc.sbuf_tensor(name, list(shape), dt)).ap()
        wo = al("d2_wo", [128, 16, D], BF16)
        for j in range(4):
            b.dma(wo[:, 4 * j:4 * j + 4, :], T["w_out_c"][512 * j:512 * (j + 1), :].rearrange("(k p) n -> p k n", p=128), q="pool")
        gp = {"P": al("d2_gP", [128, D], F32)}
        b.dma(gp["P"], T["modP"][1, :, 2 * D:3 * D])
        if do_sample:
            gp["S"] = al("d2_gS", [32, D], F32)
            b.dma(gp["S"], T["modS"][1, :, 2 * D:3 * D])
        lng = al("d2_lng", [128, D], F32); b.dma(lng, T["ln1_g"][1:2, :].partition_broadcast(128))
        lnb = al("d2_lnb", [128, D], F32); b.dma(lnb, T["ln1_b"][1:2, :].partition_broadcast(128))
        lt = ln_tiles(al, "d2")
        xt = [al("d2_x%d" % i, [128, D], F32) for i in range(2)]
        gtt = [al("d2_g%d" % i, [128, 16, 128], BF16) for i in range(2)]
        units = [("P", s) for s in range(NS)] + ([("S", 0)] if do_sample else [])
        for ui, (grp, s) in enumerate(units):
            M = 128 if grp == "P" else 32
            x, gt = xt[ui % 2], gtt[ui % 2]
            if grp == "P":
                b.dma(x, T["x2_s"][s * 128:(s + 1) * 128, :]); b.dma(gt, T["gT_s"][s])
                dst = T["x3_s"][s * 128:(s + 1) * 128, :]
            else:
                b.dma(x[:M, :], T["xs2_s"]); b.dma(gt[:, :, 0:32], T["gTs_s"])
                dst = T["xs3_s"]
            pss = []
            for n2 in range(2):
                ps = b.bank("mm", [0, 1, 2, 3])
                for cc in range(16):
                    b.mm(ps[:M, :512], gt[:, cc, :M], wo[:, cc, n2 * 512:(n2 + 1) * 512], cc == 0, cc == 15)
                pss.append(ps)
            residual_ln(b, lt, pss, x, M, gp[grp], lng, lnb, dst)
    b.S.barrier()


def build_prompt_only(nc, upto=99):
    b = B(nc)
    T = declare(b, with_sample=False)
    g = consts(b, T)
    phase0(b, T, g)
    phaseA(b, T, g, do_sample=False)
    if upto >= 1:
        phaseB1(b, T, g)
    if upto >= 2:
        phaseB2(b, T, g)
    if upto >= 3:
        phaseC(b, T, g, 0, T["x1_s"], T["x2_s"], None, None, do_sample=False)
    if upto >= 4:
        phaseD(b, T, g, do_sample=False)
    if upto >= 5:
        phaseC(b, T, g, 1, T["x3_s"], T["y_p"], None, None, do_sample=False)
    return b, T


def phaseSA(b, T, g):
    nc = b.nc
    NPK = NPG * 128
    with ExitStack() as es:
        def al(name, shape, dt):
            return es.enter_context(nc.sbuf_tensor(name, list(shape), dt)).ap()
        ptb = al("sa_ptb", [128, 4 * NPG], I32)
        b.dma(ptb, T["page_table"].partition_broadcast(128))
        io = al("sa_io", [128, 1], I32)
        b.dma(io, T["iota128"])
        idx = al("sa_idx", [128, 4 * NPG], I32)
        b.ts(idx, ptb, 128.0, io[:, 0:1], op0=ALU.mult, op1=ALU.add)
        qT = [al("sa_q%d" % k, [128, 4, 32], BF16) for k in range(3)]
        for k in range(3):
            b.dma(qT[k], T["sq_s"][k])
        newtok = al("sa_newtok", [32, 4, 512], BF16)
        for k in range(4):
            b.dma(newtok[:, k, :], T["sk_s"][k][:, 0:512])
        kaTn = al("sa_kaTn", [128, 4, 32], BF16)
        kbTn = al("sa_kbTn", [128, 4, 32], BF16)
        b.transposes(kaTn, newtok[:, 0, :], 32, 4, g["identb"])
        b.transposes(kbTn, newtok[:, 2, :], 32, 4, g["identb"])
        skiT = al("sa_skiT", [128, 32], BF16); b.dma(skiT, T["skiT_s"])
        swi = al("sa_swi", [32, 8], F32); b.dma(swi, T["swi_s"])
        slf = al("sa_slf", [32, 8], F32); b.dma(slf, T["slf_s"])
        smask = al("sa_smask", [32, 4, 8], F32); b.dma(smask, T["smask01"])
        smask_b = al("sa_smaskb", [32, 4, 8], BF16); b.cp(smask_b, smask, eng="dve")
        sidxm = al("sa_sidxm", [32, 32], F32); b.dma(sidxm, T["sidxmask"])
        segm = al("sa_segm", [32, 4], F32); b.dma(segm, T["segmask"])
        tinc32 = al("sa_tinc32", [32, 32], F32); b.dma(tinc32, T["tinc32"])
        selrow = al("sa_selrow", [32, 4, 128], F32); b.dma(selrow, T["selrow"])
        bd01 = al("sa_bd01", [64, 8, 64], F32); b.dma(bd01, T["bd01"])
        mixTs = al("sa_mixTs", [128, 8, 32], BF16)
        kpg = [al("sa_kpg%d" % k, [128, 4, 512], BF16) for k in range(2)]
        vpg = [al("sa_vpg%d" % k, [128, 4, 512], BF16) for k in range(2)]
        kTp = [al("sa_kT%d" % k, [128, 16, 128], BF16) for k in range(2)]
        tmpS = [al("sa_tmpS%d" % k, [128, 128], F32) for k in range(2)]
        pTs = [al("sa_pT%d" % k, [128, 4, 64], BF16) for k in range(2)]
        pN = al("sa_pN", [32, 64], BF16)
        tmpN = al("sa_tmpN", [32, 32], F32)
        fin = al("sa_fin", [64, 512], F32)
        osel = al("sa_osel", [64, 64], F32)
        rcs = al("sa_rcs", [64, 2], F32)
        psV = b.pb[6]
        psD = b.pb[7]
        grp_ctr = [0]

        def attend(i, qsel, kcache, vcache, kTn, vnew, bias8, biasN8, selT, selTn, out_chunk0):
            q_ = qT[qsel]
            for gq in range(NPG // 4):
                gi = grp_ctr[0]
                grp_ctr[0] += 1
                kp, vp, kT_, pT_ = kpg[gi % 2], vpg[gi % 2], kTp[gi % 2], pTs[gi % 2]
                for p4 in range(4):
                    col = idx[:, i * NPG + gq * 4 + p4:i * NPG + gq * 4 + p4 + 1]
                    b.gather(kp[:, p4, :], kcache, col)
                    b.gather(vp[:, p4, :], vcache, col)
                b.transposes(kT_, kp.rearrange("p a f -> p (a f)"), 128, 16, g["identb"])
                pss = []
                for par in range(2):
                    ps = b.bank("mm", [0, 1, 2, 3])
                    psv = ps[:, 0:128].rearrange("p (a c q) -> p a c q", a=4, c=4)
                    base = 64 * par
                    for p4 in range(4):
                        for c_ in range(4):
                            b.mm(psv[:, p4, c_, :], kT_[base:base + 64, p4 * 4 + c_, :],
                                 q_[base:base + 64, c_, 8 * i:8 * i + 8], True, True)
                    pss.append(ps)
                pv = pT_.rearrange("p a (r c q) -> p a r c q", r=2, c=4)
                for par in range(2):
                    src = pss[par][:, 0:128].rearrange("p (a q) -> p a q", q=8)
                    dst = pv[:, :, par, :, :]
                    if bias8 is not None:
                        t_ = tmpS[par]
                        bview = bias8[:, par, gq * 4:gq * 4 + 4, :].rearrange("p a c -> p (a c)")
                        b.tt(t_.rearrange("p (a q) -> p a q", q=8), src, bview.unsqueeze(2).to_broadcast([128, 16, 8]), ALU.add)
                        b.act(dst, t_.rearrange("p (a c q) -> p a c q", a=4, c=4), AF.Exp, scale=0.125)
                    else:
                        b.act(dst, pss[par][:, 0:128].rearrange("p (a c q) -> p a c q", a=4, c=4), AF.Exp, scale=0.125)
                if selT is not None:
                    sl = selT[:, gq * 4:gq * 4 + 4, 8 * i:8 * i + 8]
                    pv3 = pT_.rearrange("p a (r q) -> p a r q", q=8)
                    b.tt(pv3, pv3, sl.unsqueeze(2).to_broadcast([128, 4, 8, 8]), ALU.mult, eng="pool")
                for p4 in range(4):
                    first = (gq == 0 and p4 == 0)
                    b.mm(psV[0:64, :], pT_[:, p4, :], vp[:, p4, :], first, False)
                    b.mm(psD[0:64, 0:2], pT_[:, p4, :], g["ones_b"][:, 0:2], first, False)
            pnv = pN.rearrange("p (r c q) -> p r c q", r=2, c=4)
            for par in range(2):
                ps = b.bank("mm", [0, 1, 2, 3])
                psv = ps[0:32, 0:32].rearrange("p (c q) -> p c q", c=4)
                base = 64 * par
                for c_ in range(4):
                    b.mm(psv[:, c_, :], kTn[base:base + 64, c_, :], q_[base:base + 64, c_, 8 * i:8 * i + 8], True, True)
                if biasN8 is not None:
                    b.tt(tmpN.rearrange("p (c q) -> p c q", c=4), psv,
                         biasN8[:, par, :].unsqueeze(2).to_broadcast([32, 4, 8]), ALU.add)
                    b.act(pnv[:, par, :, :], tmpN.rearrange("p (c q) -> p c q", c=4), AF.Exp, scale=0.125)
                else:
                    b.act(pnv[:, par, :, :], psv, AF.Exp, scale=0.125)
            pn3 = pN.rearrange("p (r q) -> p r q", q=8)
            if selTn is not None:
                b.tt(pn3, pn3, selTn[:, 8 * i:8 * i + 8].unsqueeze(1).to_broadcast([32, 8, 8]), ALU.mult)
            else:
                b.tt(pn3, pn3, smask_b[:, i, :].unsqueeze(1).to_broadcast([32, 8, 8]), ALU.mult)
            b.mm(psV[0:64, :], pN, vnew, False, True)
            b.mm(psD[0:64, 0:2], pN, g["ones_b"][0:32, 0:2], False, True)
            b.ts(rcs[:, 0:1], psD[0:64, 0:1], 1e-30, None, op0=ALU.max)
            b.recip(rcs[:, 1:2], rcs[:, 0:1])
            b.ts(fin, psV[0:64, :], rcs[:, 1:2], None, op0=ALU.mult)
            b.tt(fin, fin, bd01.rearrange("p h d -> p (h d)"), ALU.mult, eng="pool")
            b.reduce(osel, fin.rearrange("p (h d) -> p d h", h=8), ALU.add)
            ps = b.bank("mm", [0, 1, 2, 3])
            b.tr(ps[0:64, 0:64], osel, g["identf"][0:64, 0:64])
            for par in range(2):
                b.cp(mixTs[64 * par:64 * par + 64, out_chunk0:out_chunk0 + 4, 8 * i:8 * i + 8],
                     ps[0:64, par * 32:(par + 1) * 32].rearrange("p (c q) -> p c q", c=4), eng="dve")

        with ExitStack() as es_1:
            def al(name, shape, dt, es_=es_1):
                return es_.enter_context(nc.sbuf_tensor(name, list(shape), dt)).ap()
            LFs = al("sa_LFs", [128, 4, NPG, 8], F32)
            for i in range(4):
                for pg in range(NPG):
                    b.gather(LFs[:, i, pg, :], T["cache_b_logf"], idx[:, i * NPG + pg:i * NPG + pg + 1])
            bias8 = []
            cks, incls = [], []
            for i in range(4):
                ck, incl = cumsum_blocks(b, g, al, LFs[:, i, :, :], NPG, "sa_c%d" % i)
                cks.append(ck); incls.append(incl)
            totsel = al("sa_totsel", [32, 8], F32)
            b.ts(totsel, incls[0][0:32, NPG - 1, :], segm[:, 0:1], None, op0=ALU.mult)
            for i in range(1, 4):
                b.stt(totsel, incls[i][0:32, NPG - 1, :], segm[:, i:i + 1], totsel, ALU.mult, ALU.add)
            ckn = al("sa_ckn", [32, 8], F32)
            ps = b.bank("mm", [0, 1, 2, 3])
            b.mm(ps[0:32, 0:8], tinc32, slf, True, True)
            b.tt(ckn, ps[0:32, 0:8], totsel, ALU.add)
            biasN8 = []
            for i in range(4):
                ps = b.bank("mm", [0, 1, 2, 3])
                b.mm(ps[:, 0:8], selrow[:, i, :], ckn, True, True)
                cend = al("sa_cend%d" % i, [128, 8], F32)
                b.cp(cend, ps[:, 0:8])
                b8 = al("sa_b8_%d" % i, [128, 2, NPG, 4], F32)
                bn8 = al("sa_bn8_%d" % i, [32, 2, 4], F32)
                for par in range(2):
                    b.tt(b8[:, par, :, :], cend[:, par::2].unsqueeze(1).to_broadcast([128, NPG, 4]),
                         cks[i][:, :, par::2], ALU.subtract)
                    b.tt(bn8[:, par, :], cend[0:32, par::2], ckn[:, par::2], ALU.subtract)
                b.ts(b8, b8, 8.0, None, op0=ALU.mult, eng="pool")
                b.ts(bn8, bn8, 8.0, None, op0=ALU.mult, eng="pool")
                bias8.append(b8)
                biasN8.append(bn8)
            for i in range(4):
                attend(i, 2, T["cache_b_k"], T["cache_b_v"], kbTn, newtok[:, 3, :], bias8[i], biasN8[i], None, None, 4)

        b.S.barrier()
        with ExitStack() as es_2:
            def al(name, shape, dt, es_=es_2):
                return es_.enter_context(nc.sbuf_tensor(name, list(shape), dt)).ap()
            NKS = NPK + 32
            score = al("sa_score", [32, NKS], F32)
            junk = al("sa_junk", [32, NKS], BF16)
            sel = al("sa_sel", [32, NKS], BF16)
            selTs = al("sa_selTs", [128, NPG, 32], BF16)
            selTn = al("sa_selTn", [32, 32], BF16)
            qpad = al("sa_qpad", [128, 4, 4, 32], BF16)
            b.memset(qpad, 0.0)
            for i in range(4):
                b.cp(qpad[:, i, :, 8 * i:8 * i + 8], qT[1][:, :, 8 * i:8 * i + 8], eng="dve")
            kig = [al("sa_kig%d" % k, [128, 16, 128], BF16) for k in range(2)]
            kiT = [al("sa_kiT%d" % k, [128, 16, 128], BF16) for k in range(2)]
            rr = [al("sa_r%d" % k, [32, 512], F32) for k in range(3)]
            sm = al("sa_sm", [32, 8 + NBIS], F32)
            halfs = sm[:, 8:8 + NBIS]
            it = 0
            for cch in range(NPG // 4):
                kg, kt = kig[cch % 2], kiT[cch % 2]
                for i in range(4):
                    for p4 in range(4):
                        b.gather(kg[:, i * 4 + p4, 0:64], T["cache_a_kidx"], idx[:, i * NPG + cch * 4 + p4:i * NPG + cch * 4 + p4 + 1])
                b.cp(kg[:, :, 64:128], kg[:, :, 0:64], eng="pool")
                b.transposes(kt, kg.rearrange("p a f -> p (a f)"), 128, 16, g["identb"])
                cols = slice(cch * 512, (cch + 1) * 512)
                for h in range(8):
                    c_, base = h // 2, 64 * (h % 2)
                    ps = b.bank("mm", [0, 1, 2, 3])
                    for i in range(4):
                        b.mm(ps[0:32, :], qpad[base:base + 64, i, c_, :],
                             kt[base:base + 64, i * 4:(i + 1) * 4, :].rearrange("p a k -> p (a k)"), i == 0, i == 3)
                    r = rr[it % 3]
                    it += 1
                    b.act(r, ps[0:32, :], AF.Relu)
                    if h == 0:
                        b.ts(score[:, cols], r, swi[:, 0:1], None, op0=ALU.mult)
                    else:
                        b.stt(score[:, cols], r, swi[:, h:h + 1], score[:, cols], ALU.mult, ALU.add)
            ncols = slice(NPK, NKS)
            for h in range(8):
                c_, base = h // 2, 64 * (h % 2)
                ps = b.bank("mm", [0, 1, 2, 3])
                b.mm(ps[0:32, 0:32], qT[1][base:base + 64, c_, :], skiT[base:base + 64, :], True, True)
                r = rr[it % 3]
                it += 1
                b.act(r[:, 0:32], ps[0:32, 0:32], AF.Relu)
                if h == 0:
                    b.ts(score[:, ncols], r[:, 0:32], swi[:, 0:1], None, op0=ALU.mult)
                else:
                    b.stt(score[:, ncols], r[:, 0:32], swi[:, h:h + 1], score[:, ncols], ALU.mult, ALU.add)
            b.reduce(sm[:, 0:1], score, ALU.max, absval=True)
            b.ts(sm[:, 1:2], sm[:, 0:1], 1.0001, 1e-20, op0=ALU.mult, op1=ALU.add)
            b.ts(halfs, g["pow2"][0:32, :], sm[:, 1:2], None, op0=ALU.mult)
            b.tt(score[:, ncols], score[:, ncols], sidxm, ALU.add)
            b.memset(sm[:, 2:3], 0.0, eng="dve")
            for k in range(NBIS):
                b.ts(junk, score, sm[:, 2:3], None, op0=ALU.is_ge, op1=ALU.add, accum=sm[:, 3:4])
                if k < NBIS - 1:
                    b.ts(sm[:, 4:5], sm[:, 3:4], 255.5, -0.5, op0=ALU.is_ge, op1=ALU.add)
                    b.stt(sm[:, 2:3], sm[:, 4:5], halfs[:, k:k + 1], sm[:, 2:3], ALU.mult, ALU.add)
                else:
                    b.ts(sm[:, 4:5], sm[:, 3:4], 255.5, -1.0, op0=ALU.is_ge, op1=ALU.add)
                    b.stt(sm[:, 5:6], sm[:, 4:5], halfs[:, k:k + 1], sm[:, 2:3], ALU.mult, ALU.add)
            b.ts(sel, score, sm[:, 5:6], None, op0=ALU.is_ge)
            b.transposes(selTs, sel, 32, NPG, g["identb"])
            pbk = b.bank("tr", [4, 5]).bitcast(BF16)
            b.tr(pbk[0:32, 0:32], sel[:, ncols], g["identb"][0:32, 0:32])
            b.cp(selTn, pbk[0:32, 0:32])
            for i in range(4):
                attend(i, 0, T["cache_a_k"], T["cache_a_v"], kaTn, newtok[:, 1, :], None, None, selTs, selTn, 0)

        b.S.barrier()
        with ExitStack() as es_3:
            def al(name, shape, dt, es_=es_3):
                return es_.enter_context(nc.sbuf_tensor(name, list(shape), dt)).ap()
            wout = al("sa_wout", [128, 8, D], BF16)
            for j in range(2):
                b.dma(wout[:, :, j * 512:(j + 1) * 512],
                      T["w_out_ab"][:, j * 512:(j + 1) * 512].rearrange("(k p) n -> p k n", p=128), q="pool")
            gp1 = al("sa_gp1", [32, D], F32); b.dma(gp1, T["modS"][0, :, 2 * D:3 * D])
            lng = al("sa_lng", [32, D], F32); b.dma(lng, T["ln1_g"][0:1, :].partition_broadcast(32))
            lnb = al("sa_lnb", [32, D], F32); b.dma(lnb, T["ln1_b"][0:1, :].partition_broadcast(32))
            xs = al("sa_xs", [32, D], F32); b.dma(xs, T["xs"])
            lt = (al("sa_acc", [32, D], F32), al("sa_tmp", [32, D], F32), al("sa_st", [32, 4, 6], F32),
                  al("sa_mv", [32, 4], F32), al("sa_res", [32, D], F32))
            pss = []
            for n2 in range(2):
                ps = b.bank("mm", [0, 1, 2, 3])
                for kc_ in range(8):
                    b.mm(ps[0:32, :512], mixTs[:, kc_, :], wout[:, kc_, n2 * 512:(n2 + 1) * 512], kc_ == 0, kc_ == 7)
                pss.append(ps)
            residual_ln(b, lt, pss, xs, 32, gp1, lng, lnb, T["xs1_s"])
    b.S.barrier()


def build_full(nc, n_phys=NPHYS):
    b = B(nc, n_phys=n_phys)
    T = declare(b, with_sample=True)
    g = consts(b, T)
    phase0(b, T, g)
    phaseA(b, T, g, do_sample=True)
    phaseSA(b, T, g)
    phaseB1(b, T, g)
    phaseB2(b, T, g)
    phaseC(b, T, g, 0, T["x1_s"], T["x2_s"], T["xs1_s"], T["xs2_s"])
    phaseD(b, T, g)
    phaseC(b, T, g, 1, T["x3_s"], T["y_p"], T["xs3_s"], T["y_s"])
    return b, T


def rope_tab(pos):
    half = 32
    inv = (10000.0 ** (-np.arange(half, dtype=np.float32) / half)).astype(np.float32)
    ang = pos.astype(np.float32)[:, None] * inv[None, :]
    return np.concatenate([np.cos(ang), np.sin(ang)], axis=1).astype(np.float32)


def prep_inputs(inp, with_sample=True, n_phys=NPHYS, remap=None):
    f = np.float32
    ar = np.arange(128)
    identf = np.eye(128, dtype=f)
    tinc = (ar[:, None] <= ar[None, :]).astype(f)
    triT01 = (ar[:, None] <= ar[None, :]).astype(f)
    triqk = np.where(ar[None, :] <= ar[:, None], 0.0, NEGBIG).astype(f)
    pow2 = np.tile((2.0 ** -np.arange(NBIS)).astype(f)[None, :], (128, 1))
    shared = {
        "identf": identf, "tinc": tinc, "triT01": triT01, "triqk": triqk, "pow2": pow2,
        "w_mod": inp["w_mod"], "b_mod": inp["b_mod"],
        "ln1_g": inp["ln1_g"], "ln1_b": inp["ln1_b"], "ln2_g": inp["ln2_g"], "ln2_b": inp["ln2_b"],
        "w_in_ab": inp["w_in_ab"][0], "b_forget": inp["b_forget"], "w_out_ab": inp["w_out_ab"][0],
        "w_in_c": inp["w_in_c"][0], "lnv_g": inp["lnv_g"], "lnv_b": inp["lnv_b"],
        "w_spatial": inp["w_spatial"][0], "b_spatial": inp["b_spatial"][0], "w_out_c": inp["w_out_c"][0],
        "w_up": inp["w_up"], "w_conv": inp["w_conv"], "b_conv": inp["b_conv"], "w_down": inp["w_down"],
    }
    if with_sample:
        t8 = np.arange(8)
        smask01 = np.zeros((32, 4, 8), f)
        sidx = np.full((32, 32), NEGBIG, f)
        segmask = np.zeros((32, 4), f)
        for i in range(4):
            for t in range(8):
                segmask[i * 8 + t, i] = 1.0
                for q in range(8):
                    if t <= q:
                        smask01[i * 8 + t, i, q] = 1.0
                        sidx[i * 8 + q, i * 8 + t] = 0.0
        tinc32 = np.zeros((32, 32), f)
        for i in range(4):
            for a in range(8):
                for c_ in range(a, 8):
                    tinc32[i * 8 + a, i * 8 + c_] = 1.0
        selrow = np.zeros((32, 4, 128), f)
        for i in range(4):
            selrow[i * 8 + 7, i, :] = 1.0
        bd01 = np.zeros((64, 8, 64), f)
        for r_ in range(64):
            bd01[r_, 2 * ((r_ // 8) % 4) + (r_ // 32), :] = 1.0
        shared.update({"smask01": smask01, "sidxmask": sidx, "segmask": segmask, "tinc32": tinc32,
                       "selrow": selrow, "bd01": bd01, "iota128": np.arange(128, dtype=np.int32).reshape(128, 1)})
        if remap is None:
            shared.update({
                "cache_a_k": inp["cache_a_k"][0].reshape(n_phys * 128, 512),
                "cache_a_v": inp["cache_a_v"][0].reshape(n_phys * 128, 512),
                "cache_a_kidx": inp["cache_a_kidx"][0].reshape(n_phys * 128, 64),
                "cache_b_k": inp["cache_b_k"][0].reshape(n_phys * 128, 512),
                "cache_b_v": inp["cache_b_v"][0].reshape(n_phys * 128, 512),
                "cache_b_logf": inp["cache_b_logf"][0].reshape(n_phys * 128, 8),
            })
    maps = []
    for c in range(8):
        bb, half = c // 2, c % 2
        xp = inp["x_prompt"][bb]
        if half == 1:
            xk = np.ascontiguousarray(xp)
            pos = np.arange(4096)
            jm = np.zeros((1, 4096), f)
            blk = np.zeros((1, 32), f)
        else:
            xk = np.concatenate([np.zeros((2048, D), f), xp[:2048]], axis=0)
            pos = np.concatenate([np.zeros(2048), np.arange(2048)])
            jm = np.concatenate([np.full((1, 2048), NEGBIG, f), np.zeros((1, 2048), f)], axis=1)
            blk = np.concatenate([np.full((1, 16), NEG, f), np.zeros((1, 16), f)], axis=1)
        m = dict(shared)
        m.update({
            "xk": xk, "cstab": rope_tab(pos), "cP": np.tile(inp["c_prompt"][bb][None, :], (128, 1)),
            "jm": jm, "blkbias": blk, "haloflag": np.full((128, 1), float(half), f),
            "xs": np.ascontiguousarray(inp["x_sample"][4 * c:4 * c + 4].reshape(32, D)),
            "cstab_s": np.tile(rope_tab(8192 + np.arange(8)), (4, 1)),
            "cS": np.repeat(inp["c_sample"][4 * c:4 * c + 4], 8, axis=0),
        })
        if with_sample:
            m["state_conv"] = np.ascontiguousarray(inp["state_ffn_conv"][:, 4 * c:4 * c + 4].reshape(2, 8, DUP))
            pt = inp["page_table"][4 * c:4 * c + 4].reshape(1, 4 * NPG).astype(np.int32)
            if remap is not None:
                uniq = pt.reshape(-1)
                m["page_table"] = np.arange(uniq.size, dtype=np.int32).reshape(1, -1)
                for nm, w_ in (("cache_a_k", 512), ("cache_a_v", 512), ("cache_a_kidx", 64), ("cache_b_k", 512),
                               ("cache_b_v", 512), ("cache_b_logf", 8)):
                    m[nm] = np.ascontiguousarray(inp[nm][0][uniq].reshape(uniq.size * 128, w_))
            else:
                m["page_table"] = pt
        maps.append(m)
    return maps


def assemble(res):
    f = np.float32
    odd = [res[2 * bb + 1] for bb in range(4)]
    y_p = np.stack([np.concatenate([res[2 * bb]["y_p"][256:], res[2 * bb + 1]["y_p"][256:]], axis=0) for bb in range(4)])
    y_s = np.concatenate([r["y_s"] for r in res], axis=0).reshape(32, 8, D)
    ak_p = np.stack([r["ak_p"] for r in odd]).reshape(1, 4, 4096, 8, 64)
    av_p = np.stack([r["av_p"] for r in odd]).reshape(1, 4, 4096, 8, 64)
    aki_p = np.stack([r["aki_p"] for r in odd]).reshape(1, 4, 4096, 64)
    bk_p = np.stack([r["bk_p"] for r in odd]).reshape(1, 4, 4096, 8, 64)
    bv_p = np.stack([r["bv_p"] for r in odd]).reshape(1, 4, 4096, 8, 64)
    blf_p = np.stack([r["blf_p"] for r in odd]).reshape(1, 4, 4096, 8)
    conv_p = np.stack([r["conv_p"] for r in odd], axis=1)
    cat = lambda n: np.concatenate([r[n] for r in res], axis=0)
    ak_s = cat("ak_s").reshape(1, 32, 8, 8, 64)
    av_s = cat("av_s").reshape(1, 32, 8, 8, 64)
    aki_s = cat("aki_s").reshape(1, 32, 8, 64)
    bk_s = cat("bk_s").reshape(1, 32, 8, 8, 64)
    bv_s = cat("bv_s").reshape(1, 32, 8, 8, 64)
    blf_s = cat("blf_s").reshape(1, 32, 8, 8)
    conv_s = np.concatenate([r["conv_s"].reshape(2, 4, 2, DUP) for r in res], axis=1)
    cv_s = cat("cv_s").reshape(1, 32, 8, DCG)
    outs = (y_p, y_s, ak_p, av_p, aki_p, bk_p, bv_p, blf_p, conv_p, ak_s, av_s, aki_s, bk_s, bv_s, blf_s, conv_s, cv_s)
    return tuple(np.ascontiguousarray(o, dtype=f) for o in outs)


def kernel(**inputs):
    inp = {k: np.asarray(v) for k, v in inputs.items()}
    nc = bass.Bass("TRN2", target_bir_lowering=False)
    bld, T = build_full(nc)
    bld.S.emit()
    maps = prep_inputs(inp, with_sample=True)
    res = run_bass_kernel_spmd(nc, maps, core_ids=list(range(8)))
    return assemble(res.results)
```

```python
from concourse.bass_utils import run_bass_kernel_spmd
import concourse.bass as bass
import concourse.mybir as mybir

SEM_MAX = 30000


def _rect(ap):
    t = ap.tensor
    name = t.name
    pat = list(ap.ap)
    esz = mybir.dt.size(ap.dtype) if hasattr(mybir.dt, "size") else None
    if esz is None:
        esz = {"float32": 4, "bfloat16": 2, "int32": 4, "uint32": 4, "float16": 2,
               "uint8": 1, "int8": 1, "uint16": 2, "int16": 2}[str(ap.dtype).split(".")[-1]]
    space = str(ap.space) if hasattr(ap, "space") else ""
    if "DRAM" in space.upper() or "Dram" in type(t).__name__ or "DRam" in type(t).__name__:
        lo = ap.offset
        hi = lo
        for (s, c) in pat:
            if c > 1:
                if s >= 0:
                    hi += s * (c - 1)
                else:
                    lo += s * (c - 1)
        return (name, 0, 1, lo * esz, (hi + 1) * esz)
    pstep, pcount = pat[0]
    if pstep == 0:
        pstep = 1 << 40
    p0 = ap.offset // pstep if pstep < (1 << 40) else 0
    foff = ap.offset - p0 * pstep if pstep < (1 << 40) else ap.offset
    lo = foff
    hi = foff
    for (s, c) in pat[1:]:
        if c > 1:
            if s >= 0:
                hi += s * (c - 1)
            else:
                lo += s * (c - 1)
    if name.startswith("pb"):
        return (name, (p0 // 32) * 32, ((p0 + pcount + 31) // 32) * 32, 0, 2048)
    return (name, p0, p0 + pcount, lo * esz, (hi + 1) * esz)


def _overlap(a, b):
    return a[1] < b[2] and b[1] < a[2] and a[3] < b[4] and b[3] < a[4]


def _covers(a, b):
    return a[1] <= b[1] and a[2] >= b[2] and a[3] <= b[3] and a[4] >= b[4]


class Op:
    __slots__ = ("eng", "fn", "deps", "sig", "sigval", "is_dma", "dsem", "idx", "seq")


class Sched:
    COMPUTE = ("pe", "act", "dve", "pool")

    def __init__(self, nc):
        self.nc = nc
        self.ops = []
        self.recs = {}
        self.engs = {
            "pe": nc.tensor, "act": nc.scalar, "dve": nc.vector, "pool": nc.gpsimd, "sp": nc.sync,
        }

    def add(self, eng, fn, reads=(), writes=(), dma=False):
        op = Op()
        op.eng = eng
        op.fn = fn
        op.is_dma = dma
        op.sig = False
        op.sigval = None
        op.dsem = None
        op.idx = len(self.ops)
        deps = set()
        accs = [(_rect(a), False) for a in reads] + [(_rect(a), True) for a in writes]
        for rect, is_w in accs:
            lst = self.recs.get(rect[0])
            if lst is None:
                continue
            for r in lst:
                if (is_w or r[1]) and _overlap(rect, r[0]):
                    deps.add(r[2])
        for rect, is_w in accs:
            lst = self.recs.setdefault(rect[0], [])
            if is_w:
                lst[:] = [r for r in lst if not _covers(rect, r[0])]
                lst.append([rect, True, op.idx])
            else:
                done = False
                if not dma:
                    for r in lst:
                        if (not r[1]) and r[0] == rect and self.ops[r[2]].eng == eng \
                                and not self.ops[r[2]].is_dma:
                            r[2] = op.idx
                            done = True
                            break
                if not done:
                    lst.append([rect, False, op.idx])
        deps.discard(op.idx)
        if eng == "pe" and not dma:
            deps = {d for d in deps if not (self.ops[d].eng == "pe" and not self.ops[d].is_dma)}
        op.deps = sorted(deps)
        for d in op.deps:
            self.ops[d].sig = True
        self.ops.append(op)
        return op

    def barrier(self):
        last = {}
        for op in self.ops:
            if op.eng != "barrier" and not op.is_dma:
                last[op.eng] = op
        op = Op()
        op.eng = "barrier"
        op.fn = None
        op.is_dma = False
        op.sig = False
        op.sigval = None
        op.dsem = None
        op.idx = len(self.ops)
        op.deps = [o.idx for o in last.values()]
        for o in last.values():
            o.sig = True
        self.ops.append(op)
        self.recs.clear()

    def emit(self, final_wait_eng="sp", n_dma_sems=24):
        nc = self.nc
        sems = {}
        cnt = {}
        epoch = {}

        def new_sem(tag):
            return nc.alloc_semaphore(name=f"s_{tag}_{len(sems)}")

        for e in self.COMPUTE:
            sems[e] = new_sem(e)
            cnt[e] = 0
        dma_sems = [new_sem("dma%d" % i) for i in range(n_dma_sems)]
        dma_cnt = [0] * n_dma_sems
        dma_rr = 0
        waited = {}
        all_sem_objs = []

        def do_wait(weng, semobj, key, val):
            k = (weng, key)
            if waited.get(k, 0) >= val:
                return
            waited[k] = val
            self.engs[weng].wait_ge(semobj, val)

        last_dma = []
        dma_last = {}
        for op in self.ops:
            e = op.eng
            if e == "barrier":
                for w in ("pe", "act", "dve", "pool", "sp"):
                    for d in op.deps:
                        semobj, val = self.ops[d].sigval
                        do_wait(w, semobj, ("c", id(semobj)), val)
                    for so, v in dma_last.values():
                        do_wait(w, so, ("d", id(so)), v)
                continue
            for d in op.deps:
                dop = self.ops[d]
                if dop.is_dma:
                    so, val = dop.dsem
                    do_wait(e, so, ("d", id(so)), val)
                else:
                    semobj, val = dop.sigval
                    do_wait(e, semobj, ("c", id(semobj)), val)
            if op.is_dma:
                si = dma_rr
                dma_rr = (dma_rr + 1) % n_dma_sems
                if dma_cnt[si] + 16 > SEM_MAX:
                    dma_sems[si] = new_sem("dmaX")
                    dma_cnt[si] = 0
                if dma_cnt[si] > 0:
                    do_wait(e, dma_sems[si], ("d", id(dma_sems[si])), dma_cnt[si])
                ins = op.fn()
                dma_cnt[si] += 16
                ins.then_inc(dma_sems[si], 16)
                op.dsem = (dma_sems[si], dma_cnt[si])
                dma_last[id(dma_sems[si])] = (dma_sems[si], dma_cnt[si])
                last_dma.append((si, dma_sems[si], dma_cnt[si]))
            else:
                ins = op.fn()
                if op.sig:
                    if cnt[e] + 1 > SEM_MAX:
                        sems[e] = new_sem(e + "X")
                        cnt[e] = 0
                    cnt[e] += 1
                    ins.then_inc(sems[e], 1)
                    op.sigval = (sems[e], cnt[e])
        fin = {}
        for si, so, v in last_dma:
            fin[id(so)] = (so, max(v, fin.get(id(so), (None, 0))[1]), si)
        for so, v, si in fin.values():
            self.engs[final_wait_eng].wait_ge(so, v)
        return len(self.ops)


import math
from contextlib import ExitStack
import numpy as np
import concourse.bass as bass
import concourse.mybir as mybir

F32 = mybir.dt.float32
BF16 = mybir.dt.bfloat16
I32 = mybir.dt.int32
U32 = mybir.dt.uint32
AF = mybir.ActivationFunctionType
ALU = mybir.AluOpType
AX = mybir.AxisListType

D = 1024
NKB = 32
NS = 18
SLOT0 = 14
DFF = 2816
DUP = 5632
PAB = 3664
DCG = 2048
NEG = -30000.0
NEGBIG = -1.0e30
ALPHA = 4.0 ** 0.25
LN_EPS = 1e-5
NBIS = 16
WI_SCALE = (64 ** -0.5) * (8 ** -0.5)
NPHYS = 2560
NPG = 64

CH = [("qa", 0, 512, False), ("ka", 512, 1024, True), ("va", 1024, 1536, True), ("qi", 1536, 2048, False),
      ("kiwi", 2048, 2120, True), ("qb", 2120, 2632, False), ("kb", 2632, 3144, True),
      ("vb", 3144, 3656, True), ("fb", 3656, 3664, True)]


class B:
    def __init__(self, nc, n_phys=NPHYS):
        self.nc = nc
        self.S = Sched(nc)
        self.n_phys = n_phys
        self.pb = [nc.alloc_psum_tensor("pb%d" % i, [128, 512], F32).ap() for i in range(8)]
        self.rot = {}
        self.dram = {}

    def bank(self, group, banks):
        i = self.rot.get(group, 0)
        self.rot[group] = i + 1
        return self.pb[banks[i % len(banks)]]

    def dma(self, out, in_, q="sp"):
        eng = {"sp": self.nc.sync, "pool": self.nc.gpsimd, "act": self.nc.scalar}[q]
        self.S.add(q, lambda: eng.dma_start(out=out, in_=in_), [in_], [out], dma=True)

    def gather(self, out, src2d, idxcol):
        nc = self.nc
        self.S.add("pool", lambda: nc.gpsimd.indirect_dma_start(
            out=out, out_offset=None, in_=src2d,
            in_offset=bass.IndirectOffsetOnAxis(ap=idxcol.bitcast(U32), axis=0)),
            [src2d, idxcol], [out], dma=True)

    def _e(self, eng):
        return {"dve": self.nc.vector, "pool": self.nc.gpsimd}[eng]

    def tt(self, out, in0, in1, op, eng="dve"):
        e = self._e(eng)
        self.S.add(eng, lambda: e.tensor_tensor(out=out, in0=in0, in1=in1, op=op), [in0, in1], [out])

    def ts(self, out, in0, s1, s2=None, op0=ALU.mult, op1=None, accum=None, eng="dve"):
        e = self._e(eng)
        rd = [in0] + [x for x in (s1, s2) if hasattr(x, "tensor")]
        wr = [out] + ([accum] if accum is not None else [])
        kw = {}
        if op1 is not None:
            kw["op1"] = op1
        if accum is not None:
            kw["accum_out"] = accum
        self.S.add(eng, lambda: e.tensor_scalar(out=out, in0=in0, scalar1=s1, scalar2=s2, op0=op0, **kw), rd, wr)

    def stt(self, out, in0, scalar, in1, op0, op1):
        nc = self.nc
        rd = [in0, in1] + ([scalar] if hasattr(scalar, "tensor") else [])
        self.S.add("dve", lambda: nc.vector.scalar_tensor_tensor(out=out, in0=in0, scalar=scalar, in1=in1,
                                                                 op0=op0, op1=op1), rd, [out])

    def act(self, out, in_, func, bias=None, scale=1.0, accum=None):
        nc = self.nc
        rd = [in_] + [x for x in (bias, scale) if hasattr(x, "tensor")]
        wr = [out] + ([accum] if accum is not None else [])
        kw = {}
        if bias is not None:
            kw["bias"] = bias
        if accum is not None:
            kw["accum_out"] = accum
        self.S.add("act", lambda: nc.scalar.activation(out=out, in_=in_, func=func, scale=scale, **kw), rd, wr)

    def cp(self, out, in_, eng="act"):
        nc = self.nc
        if eng == "act":
            self.S.add("act", lambda: nc.scalar.copy(out=out, in_=in_), [in_], [out])
        else:
            e = self._e(eng)
            self.S.add(eng, lambda: e.tensor_copy(out=out, in_=in_), [in_], [out])

    def memset(self, out, val, eng="pool"):
        e = self._e(eng)
        self.S.add(eng, lambda: e.memset(out, val), [], [out])

    def mm(self, out, lhsT, rhs, start, stop):
        nc = self.nc
        self.S.add("pe", lambda: nc.tensor.matmul(out, lhsT=lhsT, rhs=rhs, start=start, stop=stop),
                   [lhsT, rhs], [out])

    def tr(self, out, in_, ident):
        nc = self.nc
        self.S.add("pe", lambda: nc.tensor.transpose(out, in_, ident), [in_, ident], [out])

    def reduce(self, out, in_, op, absval=False):
        nc = self.nc
        kw = {"apply_absolute_value": True} if absval else {}
        self.S.add("dve", lambda: nc.vector.tensor_reduce(out=out, in_=in_, axis=AX.X, op=op, **kw), [in_], [out])

    def recip(self, out, in_):
        nc = self.nc
        self.S.add("dve", lambda: nc.vector.reciprocal(out=out, in_=in_), [in_], [out])

    def bn(self, mv, x, M, n, stats):
        nc = self.nc
        nch = (n + 511) // 512
        for c in range(nch):
            a = x[:M, c * 512:min(n, (c + 1) * 512)]
            o = stats[:M, c, :]
            self.S.add("dve", lambda a=a, o=o: nc.vector.bn_stats(out=o, in_=a), [a], [o])
        si = stats[:M, 0:nch, :]
        mo = mv[:M, 0:2]
        self.S.add("dve", lambda: nc.vector.bn_aggr(out=mo, in_=si), [si], [mo])

    def transposes(self, dst, src_bf, M, nchunk, ident_b, evac="act"):
        done = 0
        while done < nchunk:
            n = min(8, nchunk - done)
            pbk = self.bank("tr", [4, 5]).bitcast(BF16).rearrange("p (a c) -> p a c", a=8)
            for k in range(n):
                self.tr(pbk[:, k, 0:M], src_bf[:M, (done + k) * 128:(done + k + 1) * 128], ident_b[:M, :M])
            self.cp(dst[:, done:done + n, 0:M], pbk[:, 0:n, 0:M], eng=evac)
            done += n

    def layernorm_affine(self, out, x, M, n, g_bc, b_bc, tmp, stats, mv):
        self.bn(mv, x, M, n, stats)
        self.act(mv[:M, 2:3], mv[:M, 1:2], AF.Sqrt, bias=self.eps_col[:M, :], scale=1.0)
        self.recip(mv[:M, 3:4], mv[:M, 2:3])
        self.ts(tmp[:M, :n], x[:M, :n], mv[:M, 0:1], mv[:M, 3:4], op0=ALU.subtract, op1=ALU.mult)
        self.tt(tmp[:M, :n], tmp[:M, :n], g_bc[:M, :n], ALU.mult, eng="pool")
        self.tt(out[:M, :n], tmp[:M, :n], b_bc[:M, :n], ALU.add, eng="pool")


DEBUG_OUT = set()


def declare(b, with_sample=True):
    nc = b.nc
    T = {}

    def I(name, shape, dt=F32):
        T[name] = nc.dram_tensor(name, list(shape), dt, kind="ExternalInput").ap()

    def O(name, shape, dt=F32):
        T[name] = nc.dram_tensor(name, list(shape), dt, kind="ExternalOutput").ap()

    def Sx(name, shape, dt=F32):
        kind = "ExternalOutput" if name in DEBUG_OUT else "Internal"
        T[name] = nc.dram_tensor(name, list(shape), dt, kind=kind).ap()

    I("xk", [NKB * 128, D]); I("cstab", [NKB * 128, 64]); I("cP", [128, D])
    I("identf", [128, 128]); I("tinc", [128, 128]); I("triT01", [128, 128]); I("triqk", [128, 128])
    I("jm", [1, NKB * 128]); I("blkbias", [1, NKB]); I("haloflag", [128, 1]); I("pow2", [128, NBIS])
    I("w_mod", [2, D, 6 * D]); I("b_mod", [2, 6 * D])
    for n in ("ln1_g", "ln1_b", "ln2_g", "ln2_b"):
        I(n, [2, D])
    I("w_in_ab", [D, PAB]); I("b_forget", [1, 8]); I("w_out_ab", [D, D])
    I("w_in_c", [D, 2 * DCG]); I("lnv_g", [1, DCG]); I("lnv_b", [1, DCG])
    I("w_spatial", [8, 128, 128]); I("b_spatial", [8, 128]); I("w_out_c", [DCG, D])
    I("w_up", [2, D, DUP]); I("w_conv", [2, 3, DUP]); I("b_conv", [2, DUP]); I("w_down", [2, DFF, D])
    I("xs", [32, D]); I("cstab_s", [32, 64]); I("cS", [32, D])
    if with_sample:
        I("cache_a_k", [b.n_phys * 128, 512]); I("cache_a_v", [b.n_phys * 128, 512])
        I("cache_a_kidx", [b.n_phys * 128, 64])
        I("cache_b_k", [b.n_phys * 128, 512]); I("cache_b_v", [b.n_phys * 128, 512])
        I("cache_b_logf", [b.n_phys * 128, 8])
        I("state_conv", [2, 8, DUP]); I("page_table", [1, 4 * NPG], I32); I("iota128", [128, 1], I32)
        I("smask01", [32, 4, 8]); I("sidxmask", [32, 32]); I("segmask", [32, 4]); I("tinc32", [32, 32])
        I("selrow", [32, 4, 128]); I("bd01", [64, 8, 64])
    O("y_p", [NS * 128, D]); O("y_s", [32, D])
    O("ak_p", [NKB * 128, 512]); O("av_p", [NKB * 128, 512]); O("aki_p", [NKB * 128, 64])
    O("bk_p", [NKB * 128, 512]); O("bv_p", [NKB * 128, 512]); O("blf_p", [NKB * 128, 8])
    O("conv_p", [2, 2, DUP])
    O("ak_s", [32, 512]); O("av_s", [32, 512]); O("aki_s", [32, 64])
    O("bk_s", [32, 512]); O("bv_s", [32, 512]); O("blf_s", [32, 8])
    O("conv_s", [2, 8, DUP]); O("cv_s", [32, DCG])
    if "dbg_u" in DEBUG_OUT:
        O("dbg_u", [128, 520]); O("dbg_y", [128, 512]); O("dbg_w", [128, 176])
    Sx("modP", [2, 128, 6 * D]); Sx("modS", [2, 32, 6 * D])
    Sx("kaT_s", [128, 4, NKB * 128], BF16); Sx("kbT_s", [128, 4, NKB * 128], BF16)
    Sx("kiT2_s", [128, NKB * 128], BF16)
    Sx("va_s", [NKB, 128, 576], BF16); Sx("vb_s", [NKB, 128, 576], BF16)
    Sx("qaT_s", [NS, 128, 4, 128], BF16); Sx("qiT_s", [NS, 128, 4, 128], BF16); Sx("qbT_s", [NS, 128, 4, 128], BF16)
    Sx("wi_s", [NS, 128, 8])
    Sx("x1_s", [NS * 128, D]); Sx("x2_s", [NS * 128, D]); Sx("x3_s", [NS * 128, D]); Sx("ffp_s", [NS * 128, D])
    Sx("gT_s", [NS, 128, 16, 128], BF16)
    Sx("mixTa_s", [NS, 128, 4, 128], BF16)
    Sx("xs1_s", [32, D]); Sx("xs2_s", [32, D]); Sx("xs3_s", [32, D]); Sx("ffps_s", [32, D])
    Sx("gTs_s", [128, 16, 32], BF16)
    Sx("sq_s", [3, 128, 4, 32], BF16)
    Sx("sk_s", [4, 32, 576], BF16)
    Sx("skiT_s", [128, 32], BF16); Sx("swi_s", [32, 8]); Sx("slf_s", [32, 8])
    Sx("mixTs_s", [128, 8, 32], BF16)
    return T


def consts(b, T):
    nc = b.nc
    g = {}

    def al(name, shape, dt):
        return nc.alloc_sbuf_tensor("g_" + name, list(shape), dt).ap()

    g["identf"] = al("identf", [128, 128], F32)
    g["identb"] = al("identb", [128, 128], BF16)
    g["ones_f"] = al("ones_f", [128, 128], F32)
    g["ones_b"] = al("ones_b", [128, 128], BF16)
    g["tinc"] = al("tinc", [128, 128], F32)
    g["triT01"] = al("triT01", [128, 128], BF16)
    g["triqk"] = al("triqk", [128, 128], F32)
    g["LF"] = al("LF", [128, NKB, 8], F32)
    g["haloflag"] = al("haloflag", [128, 1], F32)
    g["blkbias"] = al("blkbias", [128, NKB], F32)
    g["pow2"] = al("pow2", [128, NBIS], F32)
    b.eps_col = al("eps_col", [128, 1], F32)
    tmp = al("c_tmp", [128, 128], F32)
    b.dma(g["identf"], T["identf"])
    b.cp(g["identb"], g["identf"], eng="dve")
    b.memset(g["ones_f"], 1.0)
    b.memset(g["ones_b"], 1.0)
    b.memset(b.eps_col, LN_EPS)
    b.dma(g["tinc"], T["tinc"])
    b.dma(tmp, T["triT01"])
    b.cp(g["triT01"], tmp, eng="dve")
    b.dma(g["triqk"], T["triqk"])
    b.dma(g["haloflag"], T["haloflag"])
    b.dma(g["blkbias"], T["blkbias"].partition_broadcast(128))
    b.dma(g["pow2"], T["pow2"])
    return g


def phase0(b, T, g):
    nc = b.nc
    with ExitStack() as es:
        def al(name, shape, dt):
            return es.enter_context(nc.sbuf_tensor(name, list(shape), dt)).ap()
        cf = al("p0_cf", [128, D], F32)
        cb = al("p0_cb", [128, D], BF16)
        cT = {"P": al("p0_cTP", [128, 8, 128], BF16), "S": al("p0_cTS", [128, 8, 32], BF16)}
        for grp, M, src in (("P", 128, T["cP"]), ("S", 32, T["cS"])):
            b.dma(cf[:M, :], src)
            b.act(cb[:M, :], cf[:M, :], AF.Silu)
            b.transposes(cT[grp], cb, M, 8, g["identb"])
        wch = [al("p0_w%d" % i, [128, 8, 512], BF16) for i in range(2)]
        bch = [al("p0_b%d" % i, [128, 512], F32) for i in range(2)]
        och = [al("p0_o%d" % i, [128, 512], F32) for i in range(4)]
        it = 0
        for i in range(2):
            for j in range(12):
                w = wch[it % 2]
                bb = bch[it % 2]
                b.dma(w, T["w_mod"][i, :, j * 512:(j + 1) * 512].rearrange("(k p) n -> p k n", p=128), q="pool")
                b.dma(bb, T["b_mod"][i:i + 1, j * 512:(j + 1) * 512].partition_broadcast(128))
                for gi, (grp, M, dst) in enumerate((("P", 128, T["modP"]), ("S", 32, T["modS"]))):
                    ps = b.bank("mm", [0, 1, 2, 3])
                    for k in range(8):
                        b.mm(ps[:M, :], cT[grp][:, k, :M], w[:, k, :], k == 0, k == 7)
                    o = och[(it * 2 + gi) % 4]
                    b.tt(o[:M, :], ps[:M, :], bb[:M, :], ALU.add)
                    if (j // 2) in (1, 2, 4, 5):
                        b.ts(o[:M, :], o[:M, :], 1.0, None, op0=ALU.add, eng="pool")
                    b.dma(dst[i, :, j * 512:(j + 1) * 512], o[:M, :])
                it += 1
    b.S.barrier()


def rope(b, out, ps, M, H, cs, t):
    x = ps[:M, 0:H * 64].rearrange("p (h d) -> p h d", h=H)
    o = out[:M, 0:H * 64].rearrange("p (h d) -> p h d", h=H)
    x1, x2 = x[:, :, 0:32], x[:, :, 32:64]
    cosb = cs[:M, 0:32].unsqueeze(1).to_broadcast([M, H, 32])
    sinb = cs[:M, 32:64].unsqueeze(1).to_broadcast([M, H, 32])
    ta = t[0][:M, 0:H * 32].rearrange("p (h d) -> p h d", h=H)
    tb = t[1][:M, 0:H * 32].rearrange("p (h d) -> p h d", h=H)
    b.tt(ta, x1, cosb, ALU.mult)
    b.tt(tb, x2, sinb, ALU.mult)
    b.tt(o[:, :, 0:32], ta, tb, ALU.subtract, eng="pool")
    b.tt(ta, x2, cosb, ALU.mult)
    b.tt(tb, x1, sinb, ALU.mult)
    b.tt(o[:, :, 32:64], ta, tb, ALU.add, eng="pool")


def phaseA(b, T, g, do_sample=True):
    nc = b.nc
    with ExitStack() as es:
        def al(name, shape, dt):
            return es.enter_context(nc.sbuf_tensor(name, list(shape), dt)).ap()
        w = al("pa_w", [128, 8, PAB], BF16)
        for (_, c0, c1, _) in CH:
            b.dma(w[:, :, c0:c1], T["w_in_ab"][:, c0:c1].rearrange("(k p) n -> p k n", p=128), q="pool")
        sc = {"P": al("pa_scP", [128, D], F32), "S": al("pa_scS", [32, D], F32)}
        sh = {"P": al("pa_shP", [128, D], F32), "S": al("pa_shS", [32, D], F32)}
        b.dma(sc["P"], T["modP"][0, :, D:2 * D]); b.dma(sh["P"], T["modP"][0, :, 0:D])
        b.dma(sc["S"], T["modS"][0, :, D:2 * D]); b.dma(sh["S"], T["modS"][0, :, 0:D])
        bfb = al("pa_bfb", [128, 8], F32)
        b.dma(bfb, T["b_forget"].partition_broadcast(128))
        xt = [al("pa_x%d" % i, [128, D], F32) for i in range(2)]
        cs = [al("pa_cs%d" % i, [128, 64], F32) for i in range(2)]
        hf = al("pa_hf", [128, D], F32)
        hb = al("pa_hb", [128, D], BF16)
        hT = [al("pa_hT%d" % i, [128, 8, 128], BF16) for i in range(2)]
        rt = [al("pa_rt%d" % i, [128, 256], F32) for i in range(2)]
        of = [al("pa_of%d" % i, [128, 512], F32) for i in range(3)]
        ob = [al("pa_ob%d" % i, [128, 512], BF16) for i in range(4)]
        oT = [al("pa_oT%d" % i, [128, 4, 128], BF16) for i in range(4)]
        vaug = [al("pa_va%d" % i, [128, 8, 72], BF16) for i in range(3)]
        for v in vaug:
            b.memset(v, 1.0)
        kid = al("pa_kid", [128, 128], BF16)
        sm = al("pa_sm", [128, 32], F32)
        cnt = {"of": 0, "ob": 0, "oT": 0, "va": 0}

        def nxt(lst, key):
            i = cnt[key]
            cnt[key] += 1
            return lst[i % len(lst)]

        blocks = [("P", p) for p in range(NKB)] + ([("S", 0)] if do_sample else [])
        deferred = []

        def prep_h(bi):
            grp, p = blocks[bi]
            M = 128 if grp == "P" else 32
            x = xt[bi % 2]
            c = cs[bi % 2]
            if grp == "P":
                b.dma(x, T["xk"][p * 128:(p + 1) * 128, :])
                b.dma(c, T["cstab"][p * 128:(p + 1) * 128, :])
            else:
                b.dma(x[:M, :], T["xs"])
                b.dma(c[:M, :], T["cstab_s"])
            b.tt(hf[:M, :], x[:M, :], sc[grp][:M, :], ALU.mult)
            b.tt(hb[:M, :], hf[:M, :], sh[grp][:M, :], ALU.add)
            b.transposes(hT[bi % 2], hb, M, 8, g["identb"])

        prep_h(0)
        for bi, (grp, p) in enumerate(blocks):
            M = 128 if grp == "P" else 32
            full = (grp == "S") or (p >= SLOT0)
            s = p - SLOT0
            c = cs[bi % 2]
            h_T = hT[bi % 2]
            rows = slice(p * 128, (p + 1) * 128)
            nch = 0
            for (name, c0, c1, konly) in CH:
                if not (full or konly):
                    continue
                n = c1 - c0
                ps = b.bank("mm", [0, 1, 2, 3])
                for k in range(8):
                    b.mm(ps[:M, :n], h_T[:, k, :M], w[:, k, c0:c1], k == 0, k == 7)
                while deferred:
                    deferred.pop(0)()
                nch += 1
                if nch == 3 and bi + 1 < len(blocks):
                    prep_h(bi + 1)
                if name in ("qa", "qi", "qb"):
                    o_b = nxt(ob, "ob")
                    if name == "qb":
                        b.cp(o_b[:M, :], ps[:M, :512])
                    else:
                        rope(b, o_b, ps, M, 8, c, rt)
                    o_T = nxt(oT, "oT")
                    qi_ = {"qa": 0, "qi": 1, "qb": 2}[name]

                    def fn(o_T=o_T, o_b=o_b, M=M, name=name, grp=grp, s=s, qi_=qi_):
                        b.transposes(o_T, o_b, M, 4, g["identb"])
                        if grp == "P":
                            b.dma(T[name + "T_s"][s], o_T)
                        else:
                            b.dma(T["sq_s"][qi_], o_T[:, :, 0:32])
                    deferred.append(fn)
                elif name in ("ka", "kb"):
                    o_f = nxt(of, "of")
                    if name == "ka":
                        rope(b, o_f, ps, M, 8, c, rt)
                    else:
                        b.cp(o_f[:M, :], ps[:M, :512])
                    o_b = nxt(ob, "ob")
                    b.cp(o_b[:M, :], o_f[:M, :], eng="pool")
                    if grp == "P":
                        b.dma(T["a" + "k_p" if name == "ka" else "bk_p"][rows, :], o_f)
                        o_T = nxt(oT, "oT")

                        def fn(o_T=o_T, o_b=o_b, M=M, name=name, rows=rows):
                            b.transposes(o_T, o_b, M, 4, g["identb"])
                            b.dma(T[name + "T_s"][:, :, rows], o_T)
                        deferred.append(fn)
                    else:
                        b.dma(T["ak_s" if name == "ka" else "bk_s"], o_f[:M, :])
                        b.dma(T["sk_s"][0 if name == "ka" else 2][:, 0:512], o_b[:M, :])
                elif name in ("va", "vb"):
                    o_f = nxt(of, "of")
                    b.cp(o_f[:M, :], ps[:M, :512])
                    va = nxt(vaug, "va")
                    b.cp(va[:M, :, 0:64], o_f[:M, :].rearrange("p (h d) -> p h d", h=8), eng="pool")
                    if grp == "P":
                        b.dma(T["av_p" if name == "va" else "bv_p"][rows, :], o_f)
                        b.dma(T[name + "_s"][p], va.rearrange("p h d -> p (h d)"))
                    else:
                        b.dma(T["av_s" if name == "va" else "bv_s"], o_f[:M, :])
                        o_b = nxt(ob, "ob")
                        b.cp(o_b[:M, :], o_f[:M, :], eng="dve")
                        b.dma(T["sk_s"][1 if name == "va" else 3][:, 0:512], o_b[:M, :])
                elif name == "kiwi":
                    o_f = nxt(of, "of")
                    rope(b, o_f, ps, M, 1, c, rt)
                    b.cp(kid[:M, 0:64], o_f[:M, 0:64], eng="pool")
                    b.cp(kid[:M, 64:128], o_f[:M, 0:64], eng="pool")
                    o_T = nxt(oT, "oT")
                    if grp == "P":
                        b.dma(T["aki_p"][rows, :], o_f[:, 0:64])
                    else:
                        b.dma(T["aki_s"], o_f[:M, 0:64])

                    def fn(o_T=o_T, M=M, grp=grp, rows=rows):
                        b.transposes(o_T, kid, M, 1, g["identb"])
                        if grp == "P":
                            b.dma(T["kiT2_s"][:, rows], o_T[:, 0, :])
                        else:
                            b.dma(T["skiT_s"], o_T[:, 0, 0:32])
                    deferred.append(fn)
                    if full:
                        b.ts(sm[:M, 0:8], ps[:M, 64:72], WI_SCALE, None, op0=ALU.mult)
                        b.dma(T["wi_s"][s] if grp == "P" else T["swi_s"], sm[:M, 0:8])
                elif name == "fb":
                    b.tt(sm[:M, 8:16], ps[:M, 0:8], bfb[:M, :], ALU.add)
                    b.act(sm[:M, 16:24], sm[:M, 8:16], AF.Exp, scale=-1.0)
                    b.act(sm[:M, 24:32], sm[:M, 16:24], AF.Ln, bias=g["ones_f"][:M, 0:1], scale=1.0)
                    if grp == "P":
                        b.ts(g["LF"][:, p, :], sm[:, 24:32], -1.0, None, op0=ALU.mult)
                        b.dma(T["blf_p"][rows, :], g["LF"][:, p, :])
                    else:
                        b.ts(sm[:M, 8:16], sm[:M, 24:32], -1.0, None, op0=ALU.mult)
                        b.dma(T["blf_s"], sm[:M, 8:16])
                        b.dma(T["slf_s"], sm[:M, 8:16])
        while deferred:
            deferred.pop(0)()
    b.S.barrier()


GROUPS = [[0, 1], [2, 3, 4, 5], [6, 7, 8, 9], [10, 11, 12, 13], [14, 15, 16, 17]]


def residual_ln(b, al_tiles, ps_list, xt, M, gp1, lng, lnb, out_dram, extra=None):
    acc, tmp, stats, mv, res = al_tiles
    for n2, ps in enumerate(ps_list):
        cols = slice(n2 * 512, (n2 + 1) * 512)
        if extra is not None:
            b.tt(acc[:M, cols], ps[:M, :512], extra[:M, cols], ALU.add)
            b.tt(acc[:M, cols], acc[:M, cols], gp1[:M, cols], ALU.mult, eng="pool")
        else:
            b.tt(acc[:M, cols], ps[:M, :512], gp1[:M, cols], ALU.mult)
    b.stt(tmp[:M, :], xt[:M, :], ALPHA, acc[:M, :], ALU.mult, ALU.add)
    b.layernorm_affine(res, tmp, M, D, lng, lnb, acc, stats, mv)
    b.dma(out_dram, res[:M, :])


def ln_tiles(al, pfx):
    return (al(pfx + "_acc", [128, D], F32), al(pfx + "_tmp", [128, D], F32), al(pfx + "_st", [128, 4, 6], F32),
            al(pfx + "_mv", [128, 4], F32), al(pfx + "_res", [128, D], F32))


def phaseB1(b, T, g):
    nc = b.nc
    with ExitStack() as es:
        def al(name, shape, dt):
            return es.enter_context(nc.sbuf_tensor(name, list(shape), dt)).ap()
        kaT = al("b1_kaT", [128, 4, NKB * 128], BF16)
        for i in range(4):
            b.dma(kaT[:, i, :], T["kaT_s"][:, i, :])
        kiT2 = al("b1_kiT2", [128, NKB * 128], BF16)
        b.dma(kiT2, T["kiT2_s"])
        vaA = al("b1_vaA", [128, NKB, 576], BF16)
        for i in range(4):
            b.dma(vaA[:, 8 * i:8 * i + 8, :], T["va_s"][8 * i:8 * i + 8].rearrange("j p f -> p j f"))
        JM = al("b1_JM", [128, NKB * 128], BF16)
        b.dma(JM, T["jm"].partition_broadcast(128), q="pool")
        qaT = [al("b1_qa%d" % i, [128, 4, 128], BF16) for i in range(2)]
        qiT = [al("b1_qi%d" % i, [128, 4, 128], BF16) for i in range(2)]
        wi = [al("b1_wi%d" % i, [128, 8], F32) for i in range(2)]
        score = [al("b1_score%d" % i, [128, NKB * 128], F32) for i in range(2)]
        junk = al("b1_junk", [128, NKB * 128], BF16)
        sel = [al("b1_sel%d" % i, [128, NKB * 128], BF16) for i in range(2)]
        selT = [al("b1_selT%d" % i, [128, NKB, 128], BF16) for i in range(2)]
        rr = [al("b1_r%d" % i, [128, 512], F32) for i in range(3)]
        smt = [al("b1_sm%d" % i, [128, 8 + NBIS], F32) for i in range(2)]
        pT = [al("b1_pT%d" % i, [128, 4, 128], BF16) for i in range(4)]
        pTm = [al("b1_pTm%d" % i, [128, 4, 128], BF16) for i in range(4)]
        rc = al("b1_rc", [128, 8], F32)
        onb = al("b1_onb", [128, 512], BF16)
        oT = [al("b1_oT%d" % i, [128, 4, 128], BF16) for i in range(2)]
        ctr = {"r": 0, "p": 0}

        def prep(s):
            nkb = SLOT0 + 1 + s
            NK = nkb * 128
            diag = slice((nkb - 1) * 128, nkb * 128)
            qa, qi, w_ = qaT[s % 2], qiT[s % 2], wi[s % 2]
            sc_, sl_, sT_, sm = score[s % 2], sel[s % 2], selT[s % 2], smt[s % 2]
            halfs = sm[:, 8:8 + NBIS]
            b.dma(qa, T["qaT_s"][s]); b.dma(qi, T["qiT_s"][s]); b.dma(w_, T["wi_s"][s])
            yield
            for cix in range((nkb + 3) // 4):
                nblk = min(4, nkb - 4 * cix)
                n = nblk * 128
                cols = slice(512 * cix, 512 * cix + n)
                for h in range(8):
                    hp, base = h // 2, 64 * (h % 2)
                    ps = b.bank("mm", [0, 1, 2, 3])
                    b.mm(ps[:, :n], qi[base:base + 64, hp, :], kiT2[base:base + 64, cols], True, True)
                    r = rr[ctr["r"] % 3]
                    ctr["r"] += 1
                    b.act(r[:, :n], ps[:, :n], AF.Relu)
                    if h == 0:
                        b.ts(sc_[:, cols], r[:, :n], w_[:, 0:1], None, op0=ALU.mult)
                    else:
                        b.stt(sc_[:, cols], r[:, :n], w_[:, h:h + 1], sc_[:, cols], ALU.mult, ALU.add)
                    yield
            b.reduce(sm[:, 0:1], sc_[:, :NK], ALU.max, absval=True)
            b.ts(sm[:, 1:2], sm[:, 0:1], 1.0001, 1e-20, op0=ALU.mult, op1=ALU.add)
            b.ts(halfs, g["pow2"], sm[:, 1:2], None, op0=ALU.mult)
            b.tt(sc_[:, :NK], sc_[:, :NK], JM[:, :NK], ALU.add)
            b.tt(sc_[:, diag], sc_[:, diag], g["triqk"], ALU.add)
            b.memset(sm[:, 2:3], 0.0, eng="dve")
            yield
            for k in range(NBIS):
                b.ts(junk[:, :NK], sc_[:, :NK], sm[:, 2:3], None, op0=ALU.is_ge, op1=ALU.add, accum=sm[:, 3:4])
                if k < NBIS - 1:
                    b.ts(sm[:, 4:5], sm[:, 3:4], 255.5, -0.5, op0=ALU.is_ge, op1=ALU.add)
                    b.stt(sm[:, 2:3], sm[:, 4:5], halfs[:, k:k + 1], sm[:, 2:3], ALU.mult, ALU.add)
                else:
                    b.ts(sm[:, 4:5], sm[:, 3:4], 255.5, -1.0, op0=ALU.is_ge, op1=ALU.add)
                    b.stt(sm[:, 5:6], sm[:, 4:5], halfs[:, k:k + 1], sm[:, 2:3], ALU.mult, ALU.add)
                yield
            b.ts(sl_[:, :NK], sc_[:, :NK], sm[:, 5:6], None, op0=ALU.is_ge)
            yield
            b.transposes(sT_, sl_, 128, nkb, g["identb"], evac="act")
            yield

        def attend(s):
            nkb = SLOT0 + 1 + s
            qa, sT_ = qaT[s % 2], selT[s % 2]
            pbO = [b.pb[6].rearrange("p (h d) -> p h d", h=4), b.pb[7].rearrange("p (h d) -> p h d", h=4)]
            pend = []

            def pv(item):
                pm, j, par = item
                for idx in range(4):
                    h = 2 * idx + par
                    b.mm(pbO[h // 4][:, h % 4, 0:68], pm[:, idx, :], vaA[:, j, h * 72:h * 72 + 68],
                         j == 0 and par == 0 and idx in (0, 2), j == nkb - 1)

            for j in range(nkb):
                kc = slice(j * 128, (j + 1) * 128)
                for par in range(2):
                    ps = b.bank("mm", [0, 1, 2, 3]).rearrange("p (h q) -> p h q", h=4)
                    base = 64 * par
                    for idx in range(4):
                        b.mm(ps[:, idx, :], kaT[base:base + 64, idx, kc], qa[base:base + 64, idx, :], True, True)
                    p_, pm = pT[ctr["p"] % 4], pTm[ctr["p"] % 4]
                    ctr["p"] += 1
                    b.act(p_, ps, AF.Exp, scale=0.125)
                    b.tt(pm, p_, sT_[:, j, :].unsqueeze(1).to_broadcast([128, 4, 128]), ALU.mult,
                         eng=("dve" if ctr["p"] % 4 == 0 else "pool"))
                    pend.append((pm, j, par))
                    if len(pend) > 2:
                        pv(pend.pop(0))
                    yield
            while pend:
                pv(pend.pop(0))
            for hg in range(2):
                b.ts(rc[:, 4 * hg:4 * hg + 4], pbO[hg][:, :, 64], 1e-30, None, op0=ALU.max)
            b.recip(rc, rc)
            for hg in range(2):
                b.tt(onb[:, hg * 256:(hg + 1) * 256].rearrange("p (h d) -> p h d", h=4), pbO[hg][:, :, 0:64],
                     rc[:, 4 * hg:4 * hg + 4].unsqueeze(2).to_broadcast([128, 4, 64]), ALU.mult)
            o_T = oT[s % 2]
            b.transposes(o_T, onb, 128, 4, g["identb"])
            b.dma(T["mixTa_s"][s], o_T)
            yield

        def count(gen):
            return gen

        for _ in prep(0):
            pass
        for s in range(NS):
            ga = attend(s)
            gp = prep(s + 1) if s + 1 < NS else iter(())
            n_att = 2 * (SLOT0 + 1 + s) + 1
            n_prep = (8 * ((SLOT0 + 2 + s + 3) // 4) + NBIS + 5) if s + 1 < NS else 0
            acc = 0.0
            ratio = n_prep / float(n_att)
            done_p = False
            for _ in ga:
                acc += ratio
                while acc >= 1.0 and not done_p:
                    acc -= 1.0
                    try:
                        next(gp)
                    except StopIteration:
                        done_p = True
            for _ in gp:
                pass
    b.S.barrier()


def cumsum_blocks(b, g, al, LF, nblk, pfx):
    n = nblk * 8
    ck = al(pfx + "_ck", [128, nblk, 8], F32)
    ta = al(pfx + "_ta", [128, nblk, 8], F32)
    tb = al(pfx + "_tb", [128, nblk, 8], F32)
    tot = al(pfx + "_tot", [128, nblk, 8], F32)
    lf2 = LF.rearrange("p j h -> p (j h)")
    ps1 = b.bank("mm", [0, 1, 2, 3])
    b.mm(ps1[:, :n], g["tinc"], lf2, True, True)
    ps2 = b.bank("mm", [0, 1, 2, 3])
    b.mm(ps2[:, :n], g["ones_f"], lf2, True, True)
    b.cp(tot.rearrange("p j h -> p (j h)"), ps2[:, :n])
    b.cp(ta, tot, eng="dve")
    d = 1
    cur, oth = ta, tb
    while d < nblk:
        b.tt(oth[:, d:, :], cur[:, d:, :], cur[:, :nblk - d, :], ALU.add)
        b.cp(oth[:, :d, :], cur[:, :d, :], eng="dve")
        cur, oth = oth, cur
        d *= 2
    incl = cur
    b.tt(oth, incl, tot, ALU.subtract)
    b.tt(ck.rearrange("p j h -> p (j h)"), ps1[:, :n], oth.rearrange("p j h -> p (j h)"), ALU.add)
    return ck, incl


def phaseB2(b, T, g):
    nc = b.nc
    with ExitStack() as es:
        def al(name, shape, dt):
            return es.enter_context(nc.sbuf_tensor(name, list(shape), dt)).ap()
        kbT = al("b2_kbT", [128, 4, NKB * 128], BF16)
        for i in range(4):
            b.dma(kbT[:, i, :], T["kbT_s"][:, i, :])
        vbA = al("b2_vbA", [128, NKB, 576], BF16)
        for i in range(4):
            b.dma(vbA[:, 8 * i:8 * i + 8, :], T["vb_s"][8 * i:8 * i + 8].rearrange("j p f -> p j f"))
        wout = al("b2_wout", [128, 8, D], BF16)
        for i in range(2):
            b.dma(wout[:, :, i * 512:(i + 1) * 512],
                  T["w_out_ab"][:, i * 512:(i + 1) * 512].rearrange("(k p) n -> p k n", p=128), q="pool")
        gp1 = al("b2_gp1", [128, D], F32); b.dma(gp1, T["modP"][0, :, 2 * D:3 * D])
        lng = al("b2_lng", [128, D], F32); b.dma(lng, T["ln1_g"][0:1, :].partition_broadcast(128))
        lnb = al("b2_lnb", [128, D], F32); b.dma(lnb, T["ln1_b"][0:1, :].partition_broadcast(128))
        ck, incl = cumsum_blocks(b, g, al, g["LF"], NKB, "b2")
        nck = al("b2_nck", [128, NKB, 8], F32)
        b.tt(nck, g["blkbias"].unsqueeze(2).to_broadcast([128, NKB, 8]), ck, ALU.subtract)
        bias = [al("b2_bias%d" % i, [128, NKB, 8], F32) for i in range(2)]
        qbT = [al("b2_qb%d" % i, [128, 4, 128], BF16) for i in range(2)]
        mTa = [al("b2_mTa%d" % i, [128, 4, 128], BF16) for i in range(2)]
        xt = [al("b2_x%d" % i, [128, D], F32) for i in range(2)]
        pT = [al("b2_pT%d" % i, [128, 4, 128], BF16) for i in range(4)]
        tbias = [al("b2_tb%d" % i, [128, 4, 128], F32) for i in range(4)]
        rc = al("b2_rc", [128, 8], F32)
        onb = al("b2_onb", [128, 512], BF16)
        oT = [al("b2_oT%d" % i, [128, 4, 128], BF16) for i in range(2)]
        lt = ln_tiles(al, "b2")
        it = 0
        for s in range(NS):
            nkb = SLOT0 + 1 + s
            qb, bs, mta, x = qbT[s % 2], bias[s % 2], mTa[s % 2], xt[s % 2]
            b.dma(qb, T["qbT_s"][s]); b.dma(mta, T["mixTa_s"][s])
            b.dma(x, T["xk"][(SLOT0 + s) * 128:(SLOT0 + s + 1) * 128, :])
            b.tt(bs[:, :nkb, :], nck[:, :nkb, :], incl[:, nkb - 1, :].unsqueeze(1).to_broadcast([128, nkb, 8]), ALU.add)
            b.ts(bs[:, :nkb, :], bs[:, :nkb, :], 8.0, None, op0=ALU.mult, eng="pool")
            pbO = [b.pb[6].rearrange("p (h d) -> p h d", h=4), b.pb[7].rearrange("p (h d) -> p h d", h=4)]
            pend = []

            def pv(item, nkb=nkb, pbO=pbO):
                p_, j, par = item
                for idx in range(4):
                    h = 2 * idx + par
                    b.mm(pbO[h // 4][:, h % 4, 0:68], p_[:, idx, :], vbA[:, j, h * 72:h * 72 + 68],
                         j == 0 and par == 0 and idx in (0, 2), j == nkb - 1)

            for j in range(nkb):
                kc = slice(j * 128, (j + 1) * 128)
                for par in range(2):
                    ps = b.bank("mm", [0, 1, 2, 3]).rearrange("p (h q) -> p h q", h=4)
                    base = 64 * par
                    for idx in range(4):
                        b.mm(ps[:, idx, :], kbT[base:base + 64, idx, kc], qb[base:base + 64, idx, :], True, True)
                    p_ = pT[it % 4]
                    tb_ = tbias[it % 4]
                    it += 1
                    b.tt(tb_, ps, bs[:, j, par::2].unsqueeze(2).to_broadcast([128, 4, 128]), ALU.add)
                    b.act(p_, tb_, AF.Exp, scale=0.125)
                    if j == nkb - 1:
                        b.tt(p_, p_, g["triT01"].unsqueeze(1).to_broadcast([128, 4, 128]), ALU.mult, eng="pool")
                    pend.append((p_, j, par))
                    if len(pend) > 2:
                        pv(pend.pop(0))
            while pend:
                pv(pend.pop(0))
            for hg in range(2):
                b.ts(rc[:, 4 * hg:4 * hg + 4], pbO[hg][:, :, 64], 1e-30, None, op0=ALU.max)
            b.recip(rc, rc)
            for hg in range(2):
                b.tt(onb[:, hg * 256:(hg + 1) * 256].rearrange("p (h d) -> p h d", h=4), pbO[hg][:, :, 0:64],
                     rc[:, 4 * hg:4 * hg + 4].unsqueeze(2).to_broadcast([128, 4, 64]), ALU.mult)
            o_T = oT[s % 2]
            b.transposes(o_T, onb, 128, 4, g["identb"])
            pss = []
            for n2 in range(2):
                ps = b.bank("mm", [0, 1, 2, 3])
                for kc_ in range(8):
                    lhs = mta[:, kc_, :] if kc_ < 4 else o_T[:, kc_ - 4, :]
                    b.mm(ps[:, :512], lhs, wout[:, kc_, n2 * 512:(n2 + 1) * 512], kc_ == 0, kc_ == 7)
                pss.append(ps)
            residual_ln(b, lt, pss, x, 128, gp1, lng, lnb, T["x1_s"][s * 128:(s + 1) * 128, :])
    b.S.barrier()


def to_token_major(b, g, dst_rows, src, nrow, nch, al_row):
    row = al_row
    done = 0
    while done < nch:
        n = min(4, nch - done)
        ps = b.bank("mm", [0, 1, 2, 3])
        for k in range(n):
            b.tr(ps[:nrow, k * 128:(k + 1) * 128], src[:, done + k, :], g["identf"])
        b.cp(row[:nrow, done * 128:(done + n) * 128], ps[:nrow, :n * 128])
        done += n
    b.dma(dst_rows, row[:nrow, :])


def phaseC(b, T, g, li, xin_p, xout_p, xin_s, xout_s, do_sample=True):
    nc = b.nc
    with ExitStack() as es0:
        def al0(name, shape, dt):
            return es0.enter_context(nc.sbuf_tensor(name, list(shape), dt)).ap()
        pfx = "c%d" % li
        lastP = al0(pfx + "_lastP", [128, 44, 2], F32)
        lastS = al0(pfx + "_lastS", [128, 44, 8], F32)
        wcT = al0(pfx + "_wcT", [128, 44, 4], F32)
        stT = al0(pfx + "_stT", [128, 44, 8], F32)
        with ExitStack() as es:
            wrow = es.enter_context(nc.sbuf_tensor(pfx + "_wrow", [8, DUP], F32)).ap()
            b.dma(wrow[0:3, :], T["w_conv"][li])
            b.dma(wrow[3:4, :], T["b_conv"][li:li + 1, :])
            ps = b.bank("mm", [0, 1, 2, 3])
            psv = ps[:, 0:176].rearrange("p (c k) -> p c k", k=4)
            for ch in range(44):
                b.tr(psv[:, ch, :], wrow[0:4, ch * 128:(ch + 1) * 128], g["identf"][0:4, 0:4])
            b.cp(wcT, psv)
            if do_sample:
                b.dma(wrow[0:8, :], T["state_conv"][li])
                ps = b.bank("mm", [0, 1, 2, 3])
                psv = ps[:, 0:352].rearrange("p (c k) -> p c k", k=8)
                for ch in range(44):
                    b.tr(psv[:, ch, :], wrow[0:8, ch * 128:(ch + 1) * 128], g["identf"][0:8, 0:8])
                b.cp(stT, psv)
        b.S.barrier()
        b.memset(lastP, 0.0)
        for fh in range(2):
            with ExitStack() as es:
                def al(name, shape, dt):
                    return es.enter_context(nc.sbuf_tensor(name, list(shape), dt)).ap()
                p2 = pfx + "p%d" % fh
                wug = al(p2 + "_wug", [128, 8, 1408], BF16)
                wuv = al(p2 + "_wuv", [128, 8, 1408], BF16)
                wdn = al(p2 + "_wdn", [128, 11, D], BF16)
                for j in range(2):
                    cs_ = slice(1408 * fh + 704 * j, 1408 * fh + 704 * (j + 1))
                    b.dma(wug[:, :, 704 * j:704 * (j + 1)], T["w_up"][li, :, cs_].rearrange("(k p) n -> p k n", p=128), q="pool")
                    cs_ = slice(DFF + 1408 * fh + 704 * j, DFF + 1408 * fh + 704 * (j + 1))
                    b.dma(wuv[:, :, 704 * j:704 * (j + 1)], T["w_up"][li, :, cs_].rearrange("(k p) n -> p k n", p=128), q="pool")
                b.dma(wdn, T["w_down"][li, 1408 * fh:1408 * (fh + 1), :].rearrange("(k p) n -> p k n", p=128), q="pool")
                bc = {}
                for grp, M, src in (("P", 128, T["modP"]), ("S", 32, T["modS"])):
                    if grp == "S" and not do_sample:
                        continue
                    bc[grp] = {}
                    for nm, c0 in (("sh", 3), ("sc", 4)) + ((("g", 5),) if fh == 1 else ()):
                        t_ = al(p2 + "_%s%s" % (nm, grp), [M, D], F32)
                        b.dma(t_, src[li, :, c0 * D:(c0 + 1) * D])
                        bc[grp][nm] = t_
                if fh == 1:
                    lng = al(p2 + "_lng", [128, D], F32); b.dma(lng, T["ln2_g"][li:li + 1, :].partition_broadcast(128))
                    lnb = al(p2 + "_lnb", [128, D], F32); b.dma(lnb, T["ln2_b"][li:li + 1, :].partition_broadcast(128))
                    lt = ln_tiles(al, p2)
                    ffp = al(p2 + "_ffp", [128, D], F32)
                else:
                    ffo = [al(p2 + "_ffo%d" % i, [128, D], F32) for i in range(2)]
                xt = [al(p2 + "_x%d" % i, [128, D], F32) for i in range(4)]
                hf = al(p2 + "_hf", [128, D], F32)
                hb = al(p2 + "_hb", [128, D], BF16)
                h2T = al(p2 + "_h2T", [128, 8, 512], BF16)
                gT = al(p2 + "_gT", [128, 11, 512], BF16)
                ut = [al(p2 + "_u%d" % i, [128, 520], F32) for i in range(4)]
                yt = [al(p2 + "_y%d" % i, [128, 512], F32) for i in range(4)]
                glt = [al(p2 + "_gl%d" % i, [128, 512], F32) for i in range(2)]
                halo = al(p2 + "_halo", [128, 22, 2], F32)
                b.memset(halo, 0.0)
                groups = [("P", sl) for sl in GROUPS] + ([("S", None)] if do_sample else [])
                for gi, (grp, slots) in enumerate(groups):
                    if grp == "P":
                        M, nsl, nseg, L = 128, len(slots), 1, 128 * len(slots)
                    else:
                        M, nsl, nseg, L = 32, 1, 4, 8
                    Mtot = M * nsl
                    for si in range(nsl):
                        x = xt[si]
                        if grp == "P":
                            b.dma(x, xin_p[slots[si] * 128:(slots[si] + 1) * 128, :])
                        else:
                            b.dma(x[:M, :], xin_s)
                        b.tt(hf[:M, :], x[:M, :], bc[grp]["sc"][:M, :], ALU.mult)
                        b.tt(hb[:M, :], hf[:M, :], bc[grp]["sh"][:M, :], ALU.add, eng="pool")
                        b.transposes(h2T[:, :, si * M:(si + 1) * M], hb, M, 8, g["identb"])
                    for gcl in range(11):
                        ys = []
                        for wi_, (wsrc, choff) in enumerate(((wug, 0), (wuv, 22))):
                            ch = choff + 11 * fh + gcl
                            hch = 11 * wi_ + gcl
                            ps = b.bank("mm", [0, 1, 2, 3])
                            for k in range(8):
                                b.mm(ps[:, :Mtot], wsrc[:, k, gcl * 128:(gcl + 1) * 128], h2T[:, k, :Mtot], k == 0, k == 7)
                            ti = 2 * (gcl % 2) + wi_
                            u = ut[ti][:, 0:nseg * (L + 2)].rearrange("p (s l) -> p s l", s=nseg)
                            b.cp(u[:, :, 2:2 + L], ps[:, :Mtot].rearrange("p (s l) -> p s l", s=nseg))
                            if grp == "P":
                                if gi == 1:
                                    b.ts(u[:, 0, 0:2], halo[:, hch, :], g["haloflag"][:, 0:1], None, op0=ALU.mult, eng="pool")
                                else:
                                    b.cp(u[:, 0, 0:2], halo[:, hch, :], eng="pool")
                                b.cp(halo[:, hch, :], u[:, 0, L:L + 2], eng="pool")
                                if gi == len(GROUPS) - 1:
                                    b.cp(lastP[:, ch, :], u[:, 0, L:L + 2], eng="pool")
                            else:
                                b.cp(u[:, :, 0:2], stT[:, ch, :].rearrange("p (s r) -> p s r", s=4), eng="pool")
                                b.cp(lastS[:, ch, :].rearrange("p (s r) -> p s r", s=4), u[:, :, L:L + 2], eng="pool")
                            y = yt[ti][:, 0:nseg * L].rearrange("p (s l) -> p s l", s=nseg)
                            b.ts(y, u[:, :, 0:L], wcT[:, ch, 0:1], wcT[:, ch, 3:4], op0=ALU.mult, op1=ALU.add)
                            b.stt(y, u[:, :, 1:L + 1], wcT[:, ch, 1:2], y, ALU.mult, ALU.add)
                            b.stt(y, u[:, :, 2:L + 2], wcT[:, ch, 2:3], y, ALU.mult, ALU.add)
                            ys.append(yt[ti][:, 0:Mtot])
                        gl = glt[gcl % 2]
                        b.act(gl[:, :Mtot], ys[0], AF.Gelu)
                        b.tt(gT[:, gcl, :Mtot], gl[:, :Mtot], ys[1], ALU.mult, eng="pool")
                    for si in range(nsl):
                        tok = slice(si * M, (si + 1) * M)
                        pss = []
                        for n2 in range(2):
                            ps = b.bank("mm", [0, 1, 2, 3])
                            for gcl in range(11):
                                b.mm(ps[:M, :512], gT[:, gcl, tok], wdn[:, gcl, n2 * 512:(n2 + 1) * 512], gcl == 0, gcl == 10)
                            pss.append(ps)
                        if grp == "P":
                            rows = slice(slots[si] * 128, (slots[si] + 1) * 128)
                            fdst, xdst = T["ffp_s"][rows, :], xout_p[rows, :]
                        else:
                            fdst, xdst = T["ffps_s"], xout_s
                        if fh == 0:
                            fo = ffo[si % 2]
                            for n2 in range(2):
                                b.cp(fo[:M, n2 * 512:(n2 + 1) * 512], pss[n2][:M, :512])
                            b.dma(fdst, fo[:M, :])
                        else:
                            b.dma(ffp[:M, :], fdst)
                            residual_ln(b, lt, pss, xt[si], M, bc[grp]["g"], lng, lnb, xdst, extra=ffp)
            b.S.barrier()
        with ExitStack() as es:
            row = es.enter_context(nc.sbuf_tensor(pfx + "_row", [8, DUP], F32)).ap()
            to_token_major(b, g, T["conv_p"][li], lastP, 2, 44, row)
            if do_sample:
                to_token_major(b, g, T["conv_s"][li], lastS, 8, 44, row)
    b.S.barrier()


def phaseD(b, T, g, do_sample=True):
    nc = b.nc
    with ExitStack() as es:
        def al(name, shape, dt):
            return es.enter_context(nc.sbuf_tensor(name, list(shape), dt)).ap()
        w = al("d1_w", [128, 8, 2 * DCG], BF16)
        for j in range(8):
            b.dma(w[:, :, j * 512:(j + 1) * 512], T["w_in_c"][:, j * 512:(j + 1) * 512].rearrange("(k p) n -> p k n", p=128), q="pool")
        wsf = al("d1_wsf", [128, 8, 128], F32)
        b.dma(wsf, T["w_spatial"].rearrange("g t s -> t g s"))
        wsb = al("d1_wsb", [128, 8, 128], BF16)
        b.cp(wsb, wsf, eng="dve")
        wsT = al("d1_wsT", [128, 8, 128], BF16)
        b.transposes(wsT, wsb.rearrange("p g s -> p (g s)"), 128, 8, g["identb"])
        b.tt(wsT, wsT, g["triT01"].unsqueeze(1).to_broadcast([128, 8, 128]), ALU.mult)
        bsp = al("d1_bsp", [128, 8, 128], F32)
        b.dma(bsp.rearrange("p g t -> p (g t)"), T["b_spatial"].rearrange("g t -> (g t)").unsqueeze(0).partition_broadcast(128))
        lvg = al("d1_lvg", [128, DCG], F32); b.dma(lvg, T["lnv_g"].partition_broadcast(128))
        lvb = al("d1_lvb", [128, DCG], F32); b.dma(lvb, T["lnv_b"].partition_broadcast(128))
        bc = {}
        for grp, M, src in (("P", 128, T["modP"]), ("S", 32, T["modS"])):
            if grp == "S" and not do_sample:
                continue
            bc[grp] = {}
            for nm, c0 in (("sh", 0), ("sc", 1)):
                t_ = al("d1_%s%s" % (nm, grp), [M, D], F32)
                b.dma(t_, src[1, :, c0 * D:(c0 + 1) * D])
                bc[grp][nm] = t_
        if do_sample:
            wsS = al("d1_wsS", [32, 8, 32], BF16)
            b.memset(wsS, 0.0)
            for i in range(4):
                b.dma(wsS[i * 8:(i + 1) * 8, :, i * 8:(i + 1) * 8], wsT[0:8, :, 0:8])
            bspS = al("d1_bspS", [128, 8, 32], F32)
            for i in range(4):
                b.dma(bspS[:, :, i * 8:(i + 1) * 8], T["b_spatial"][:, 0:8].unsqueeze(0).partition_broadcast(128))
        xt = al("d1_x", [128, D], F32)
        hf = al("d1_hf", [128, D], F32)
        hb = al("d1_hb", [128, D], BF16)
        hT = al("d1_hT", [128, 8, 512], BF16)
        uT = al("d1_uT", [128, 16, 512], BF16)
        vf = al("d1_vf", [128, DCG], F32)
        vt = al("d1_vt", [128, DCG], F32)
        vln = al("d1_vln", [128, DCG], F32)
        vlb = al("d1_vlb", [128, DCG], BF16)
        st = al("d1_st", [128, 4, 6], F32)
        mv = al("d1_mv", [128, 4], F32)
        mx = [al("d1_mx%d" % i, [128, 4, 128], F32) for i in range(2)]
        gTt = [al("d1_gT%d" % i, [128, 16, 128], BF16) for i in range(2)]
        groups = [("P", sl) for sl in GROUPS] + ([("S", None)] if do_sample else [])
        it = 0
        for gi, (grp, slots) in enumerate(groups):
            M = 128 if grp == "P" else 32
            nsl = len(slots) if grp == "P" else 1
            Mtot = M * nsl
            for si in range(nsl):
                if grp == "P":
                    b.dma(xt, T["x2_s"][slots[si] * 128:(slots[si] + 1) * 128, :])
                else:
                    b.dma(xt[:M, :], T["xs2_s"])
                b.tt(hf[:M, :], xt[:M, :], bc[grp]["sc"][:M, :], ALU.mult)
                b.tt(hb[:M, :], hf[:M, :], bc[grp]["sh"][:M, :], ALU.add, eng="pool")
                b.transposes(hT[:, :, si * M:(si + 1) * M], hb, M, 8, g["identb"])
            for cc in range(16):
                ps = b.bank("mm", [0, 1, 2, 3])
                for k in range(8):
                    b.mm(ps[:, :Mtot], w[:, k, cc * 128:(cc + 1) * 128], hT[:, k, :Mtot], k == 0, k == 7)
                b.act(uT[:, cc, :Mtot], ps[:, :Mtot], AF.Gelu)
            for si in range(nsl):
                tok = slice(si * M, (si + 1) * M)
                for c4 in range(4):
                    ps = b.bank("mm", [0, 1, 2, 3])
                    for k in range(8):
                        b.mm(ps[:M, :512], hT[:, k, tok], w[:, k, DCG + c4 * 512:DCG + (c4 + 1) * 512], k == 0, k == 7)
                    b.act(vf[:M, c4 * 512:(c4 + 1) * 512], ps[:M, :512], AF.Gelu)
                b.layernorm_affine(vln, vf, M, DCG, lvg, lvb, vt, st, mv)
                b.cp(vlb[:M, :], vln[:M, :], eng="dve")
                if grp == "S":
                    b.dma(T["cv_s"], vln[:M, :])
                gt = gTt[it % 2]
                it += 1
                for c4 in range(4):
                    ps = b.bank("mm", [0, 1, 2, 3])
                    psv = ps[:, 0:4 * M].rearrange("p (c t) -> p c t", c=4)
                    for k in range(4):
                        cc = 4 * c4 + k
                        rhs = wsT[:, cc // 2, :] if grp == "P" else wsS[:, cc // 2, :]
                        b.mm(psv[:, k, :], vlb[:M, cc * 128:(cc + 1) * 128], rhs, True, True)
                    m_ = mx[c4 % 2]
                    bs_ = bsp if grp == "P" else bspS
                    b.tt(m_[:, :, :M].rearrange("p (a c) t -> p a c t", a=2), psv.rearrange("p (a c) t -> p a c t", a=2),
                         bs_[:, 2 * c4:2 * c4 + 2, :].unsqueeze(2).to_broadcast([128, 2, 2, M]), ALU.add)
                    b.tt(gt[:, 4 * c4:4 * c4 + 4, :M], m_[:, :, :M], uT[:, 4 * c4:4 * c4 + 4, tok], ALU.mult, eng="pool")
                if grp == "P":
                    b.dma(T["gT_s"][slots[si]], gt)
                else:
                    b.dma(T["gTs_s"], gt[:, :, 0:32])
    b.S.barrier()
    with ExitStack() as es:
        def al(name, shape, dt):
            return es.enter_context(nc.sbuf_tensor(name, list(shape), dt)).ap()
        wo = al("d2_wo", [128, 16, D], BF16)
        for j in range(4):
            b.dma(wo[:, 4 * j:4 * j + 4, :], T["w_out_c"][512 * j:512 * (j + 1), :].rearrange("(k p) n -> p k n", p=128), q="pool")
        gp = {"P": al("d2_gP", [128, D], F32)}
        b.dma(gp["P"], T["modP"][1, :, 2 * D:3 * D])
        if do_sample:
            gp["S"] = al("d2_gS", [32, D], F32)
            b.dma(gp["S"], T["modS"][1, :, 2 * D:3 * D])
        lng = al("d2_lng", [128, D], F32); b.dma(lng, T["ln1_g"][1:2, :].partition_broadcast(128))
        lnb = al("d2_lnb", [128, D], F32); b.dma(lnb, T["ln1_b"][1:2, :].partition_broadcast(128))
        lt = ln_tiles(al, "d2")
        xt = [al("d2_x%d" % i, [128, D], F32) for i in range(2)]
        gtt = [al("d2_g%d" % i, [128, 16, 128], BF16) for i in range(2)]
        units = [("P", s) for s in range(NS)] + ([("S", 0)] if do_sample else [])
        for ui, (grp, s) in enumerate(units):
            M = 128 if grp == "P" else 32
            x, gt = xt[ui % 2], gtt[ui % 2]
            if grp == "P":
                b.dma(x, T["x2_s"][s * 128:(s + 1) * 128, :]); b.dma(gt, T["gT_s"][s])
                dst = T["x3_s"][s * 128:(s + 1) * 128, :]
            else:
                b.dma(x[:M, :], T["xs2_s"]); b.dma(gt[:, :, 0:32], T["gTs_s"])
                dst = T["xs3_s"]
            pss = []
            for n2 in range(2):
                ps = b.bank("mm", [0, 1, 2, 3])
                for cc in range(16):
                    b.mm(ps[:M, :512], gt[:, cc, :M], wo[:, cc, n2 * 512:(n2 + 1) * 512], cc == 0, cc == 15)
                pss.append(ps)
            residual_ln(b, lt, pss, x, M, gp[grp], lng, lnb, dst)
    b.S.barrier()


def build_prompt_only(nc, upto=99):
    b = B(nc)
    T = declare(b, with_sample=False)
    g = consts(b, T)
    phase0(b, T, g)
    phaseA(b, T, g, do_sample=False)
    if upto >= 1:
        phaseB1(b, T, g)
    if upto >= 2:
        phaseB2(b, T, g)
    if upto >= 3:
        phaseC(b, T, g, 0, T["x1_s"], T["x2_s"], None, None, do_sample=False)
    if upto >= 4:
        phaseD(b, T, g, do_sample=False)
    if upto >= 5:
        phaseC(b, T, g, 1, T["x3_s"], T["y_p"], None, None, do_sample=False)
    return b, T


def phaseSA(b, T, g):
    nc = b.nc
    NPK = NPG * 128
    with ExitStack() as es:
        def al(name, shape, dt):
            return es.enter_context(nc.sbuf_tensor(name, list(shape), dt)).ap()
        ptb = al("sa_ptb", [128, 4 * NPG], I32)
        b.dma(ptb, T["page_table"].partition_broadcast(128))
        io = al("sa_io", [128, 1], I32)
        b.dma(io, T["iota128"])
        idx = al("sa_idx", [128, 4 * NPG], I32)
        b.ts(idx, ptb, 128.0, io[:, 0:1], op0=ALU.mult, op1=ALU.add)
        qT = [al("sa_q%d" % k, [128, 4, 32], BF16) for k in range(3)]
        for k in range(3):
            b.dma(qT[k], T["sq_s"][k])
        newtok = al("sa_newtok", [32, 4, 512], BF16)
        for k in range(4):
            b.dma(newtok[:, k, :], T["sk_s"][k][:, 0:512])
        kaTn = al("sa_kaTn", [128, 4, 32], BF16)
        kbTn = al("sa_kbTn", [128, 4, 32], BF16)
        b.transposes(kaTn, newtok[:, 0, :], 32, 4, g["identb"])
        b.transposes(kbTn, newtok[:, 2, :], 32, 4, g["identb"])
        skiT = al("sa_skiT", [128, 32], BF16); b.dma(skiT, T["skiT_s"])
        swi = al("sa_swi", [32, 8], F32); b.dma(swi, T["swi_s"])
        slf = al("sa_slf", [32, 8], F32); b.dma(slf, T["slf_s"])
        smask = al("sa_smask", [32, 4, 8], F32); b.dma(smask, T["smask01"])
        smask_b = al("sa_smaskb", [32, 4, 8], BF16); b.cp(smask_b, smask, eng="dve")
        sidxm = al("sa_sidxm", [32, 32], F32); b.dma(sidxm, T["sidxmask"])
        segm = al("sa_segm", [32, 4], F32); b.dma(segm, T["segmask"])
        tinc32 = al("sa_tinc32", [32, 32], F32); b.dma(tinc32, T["tinc32"])
        selrow = al("sa_selrow", [32, 4, 128], F32); b.dma(selrow, T["selrow"])
        bd01 = al("sa_bd01", [64, 8, 64], F32); b.dma(bd01, T["bd01"])
        mixTs = al("sa_mixTs", [128, 8, 32], BF16)
        kpg = [al("sa_kpg%d" % k, [128, 4, 512], BF16) for k in range(2)]
        vpg = [al("sa_vpg%d" % k, [128, 4, 512], BF16) for k in range(3)]
        kTp = [al("sa_kT%d" % k, [128, 16, 128], BF16) for k in range(2)]
        tmpS = [al("sa_tmpS%d" % k, [128, 128], F32) for k in range(2)]
        pTs = [al("sa_pT%d" % k, [128, 4, 64], BF16) for k in range(3)]
        pN = al("sa_pN", [32, 64], BF16)
        tmpN = al("sa_tmpN", [32, 32], F32)
        fin = al("sa_fin", [64, 512], F32)
        osel = al("sa_osel", [64, 64], F32)
        rcs = al("sa_rcs", [64, 2], F32)
        psV = b.pb[6]
        psD = b.pb[7]
        grp_ctr = [0]

        def attend(i, qsel, kcache, vcache, kTn, vnew, bias8, biasN8, selT, selTn, out_chunk0):
            q_ = qT[qsel]
            pend = []

            def pvs(item):
                pT_, vp, gq = item
                for p4 in range(4):
                    first = (gq == 0 and p4 == 0)
                    b.mm(psV[0:64, :], pT_[:, p4, :], vp[:, p4, :], first, False)
                    b.mm(psD[0:64, 0:2], pT_[:, p4, :], g["ones_b"][:, 0:2], first, False)

            for gq in range(NPG // 4):
                gi = grp_ctr[0]
                grp_ctr[0] += 1
                kp, vp, kT_, pT_ = kpg[gi % 2], vpg[gi % 3], kTp[gi % 2], pTs[gi % 3]
                for p4 in range(4):
                    col = idx[:, i * NPG + gq * 4 + p4:i * NPG + gq * 4 + p4 + 1]
                    b.gather(kp[:, p4, :], kcache, col)
                    b.gather(vp[:, p4, :], vcache, col)
                b.transposes(kT_, kp.rearrange("p a f -> p (a f)"), 128, 16, g["identb"])
                pss = []
                for par in range(2):
                    ps = b.bank("mm", [0, 1, 2, 3])
                    psv = ps[:, 0:128].rearrange("p (a c q) -> p a c q", a=4, c=4)
                    base = 64 * par
                    for p4 in range(4):
                        for c_ in range(4):
                            b.mm(psv[:, p4, c_, :], kT_[base:base + 64, p4 * 4 + c_, :],
                                 q_[base:base + 64, c_, 8 * i:8 * i + 8], True, True)
                    pss.append(ps)
                pv = pT_.rearrange("p a (r c q) -> p a r c q", r=2, c=4)
                for par in range(2):
                    src = pss[par][:, 0:128].rearrange("p (a q) -> p a q", q=8)
                    dst = pv[:, :, par, :, :]
                    if bias8 is not None:
                        t_ = tmpS[par]
                        bview = bias8[:, par, gq * 4:gq * 4 + 4, :].rearrange("p a c -> p (a c)")
                        b.tt(t_.rearrange("p (a q) -> p a q", q=8), src, bview.unsqueeze(2).to_broadcast([128, 16, 8]), ALU.add)
                        b.act(dst, t_.rearrange("p (a c q) -> p a c q", a=4, c=4), AF.Exp, scale=0.125)
                    else:
                        b.act(dst, pss[par][:, 0:128].rearrange("p (a c q) -> p a c q", a=4, c=4), AF.Exp, scale=0.125)
                if selT is not None:
                    sl = selT[:, gq * 4:gq * 4 + 4, 8 * i:8 * i + 8]
                    pv3 = pT_.rearrange("p a (r q) -> p a r q", q=8)
                    b.tt(pv3, pv3, sl.unsqueeze(2).to_broadcast([128, 4, 8, 8]), ALU.mult, eng="pool")
                pend.append((pT_, vp, gq))
                if len(pend) > 1:
                    pvs(pend.pop(0))
            while pend:
                pvs(pend.pop(0))
            pnv = pN.rearrange("p (r c q) -> p r c q", r=2, c=4)
            for par in range(2):
                ps = b.bank("mm", [0, 1, 2, 3])
                psv = ps[0:32, 0:32].rearrange("p (c q) -> p c q", c=4)
                base = 64 * par
                for c_ in range(4):
                    b.mm(psv[:, c_, :], kTn[base:base + 64, c_, :], q_[base:base + 64, c_, 8 * i:8 * i + 8], True, True)
                if biasN8 is not None:
                    b.tt(tmpN.rearrange("p (c q) -> p c q", c=4), psv,
                         biasN8[:, par, :].unsqueeze(2).to_broadcast([32, 4, 8]), ALU.add)
                    b.act(pnv[:, par, :, :], tmpN.rearrange("p (c q) -> p c q", c=4), AF.Exp, scale=0.125)
                else:
                    b.act(pnv[:, par, :, :], psv, AF.Exp, scale=0.125)
            pn3 = pN.rearrange("p (r q) -> p r q", q=8)
            if selTn is not None:
                b.tt(pn3, pn3, selTn[:, 8 * i:8 * i + 8].unsqueeze(1).to_broadcast([32, 8, 8]), ALU.mult)
            else:
                b.tt(pn3, pn3, smask_b[:, i, :].unsqueeze(1).to_broadcast([32, 8, 8]), ALU.mult)
            b.mm(psV[0:64, :], pN, vnew, False, True)
            b.mm(psD[0:64, 0:2], pN, g["ones_b"][0:32, 0:2], False, True)
            b.ts(rcs[:, 0:1], psD[0:64, 0:1], 1e-30, None, op0=ALU.max)
            b.recip(rcs[:, 1:2], rcs[:, 0:1])
            b.ts(fin, psV[0:64, :], rcs[:, 1:2], None, op0=ALU.mult)
            b.tt(fin, fin, bd01.rearrange("p h d -> p (h d)"), ALU.mult, eng="pool")
            b.reduce(osel, fin.rearrange("p (h d) -> p d h", h=8), ALU.add)
            ps = b.bank("mm", [0, 1, 2, 3])
            b.tr(ps[0:64, 0:64], osel, g["identf"][0:64, 0:64])
            for par in range(2):
                b.cp(mixTs[64 * par:64 * par + 64, out_chunk0:out_chunk0 + 4, 8 * i:8 * i + 8],
                     ps[0:64, par * 32:(par + 1) * 32].rearrange("p (c q) -> p c q", c=4), eng="dve")

        with ExitStack() as es_1:
            def al(name, shape, dt, es_=es_1):
                return es_.enter_context(nc.sbuf_tensor(name, list(shape), dt)).ap()
            LFs = al("sa_LFs", [128, 4, NPG, 8], F32)
            for i in range(4):
                for pg in range(NPG):
                    b.gather(LFs[:, i, pg, :], T["cache_b_logf"], idx[:, i * NPG + pg:i * NPG + pg + 1])
            bias8 = []
            cks, incls = [], []
            for i in range(4):
                ck, incl = cumsum_blocks(b, g, al, LFs[:, i, :, :], NPG, "sa_c%d" % i)
                cks.append(ck); incls.append(incl)
            totsel = al("sa_totsel", [32, 8], F32)
            b.ts(totsel, incls[0][0:32, NPG - 1, :], segm[:, 0:1], None, op0=ALU.mult)
            for i in range(1, 4):
                b.stt(totsel, incls[i][0:32, NPG - 1, :], segm[:, i:i + 1], totsel, ALU.mult, ALU.add)
            ckn = al("sa_ckn", [32, 8], F32)
            ps = b.bank("mm", [0, 1, 2, 3])
            b.mm(ps[0:32, 0:8], tinc32, slf, True, True)
            b.tt(ckn, ps[0:32, 0:8], totsel, ALU.add)
            biasN8 = []
            for i in range(4):
                ps = b.bank("mm", [0, 1, 2, 3])
                b.mm(ps[:, 0:8], selrow[:, i, :], ckn, True, True)
                cend = al("sa_cend%d" % i, [128, 8], F32)
                b.cp(cend, ps[:, 0:8])
                b8 = al("sa_b8_%d" % i, [128, 2, NPG, 4], F32)
                bn8 = al("sa_bn8_%d" % i, [32, 2, 4], F32)
                for par in range(2):
                    b.tt(b8[:, par, :, :], cend[:, par::2].unsqueeze(1).to_broadcast([128, NPG, 4]),
                         cks[i][:, :, par::2], ALU.subtract)
                    b.tt(bn8[:, par, :], cend[0:32, par::2], ckn[:, par::2], ALU.subtract)
                b.ts(b8, b8, 8.0, None, op0=ALU.mult, eng="pool")
                b.ts(bn8, bn8, 8.0, None, op0=ALU.mult, eng="pool")
                bias8.append(b8)
                biasN8.append(bn8)
            for i in range(4):
                attend(i, 2, T["cache_b_k"], T["cache_b_v"], kbTn, newtok[:, 3, :], bias8[i], biasN8[i], None, None, 4)

        b.S.barrier()
        with ExitStack() as es_2:
            def al(name, shape, dt, es_=es_2):
                return es_.enter_context(nc.sbuf_tensor(name, list(shape), dt)).ap()
            NKS = NPK + 32
            score = al("sa_score", [32, NKS], F32)
            junk = al("sa_junk", [32, NKS], BF16)
            sel = al("sa_sel", [32, NKS], BF16)
            selTs = al("sa_selTs", [128, NPG, 32], BF16)
            selTn = al("sa_selTn", [32, 32], BF16)
            qpad = al("sa_qpad", [128, 4, 4, 32], BF16)
            b.memset(qpad, 0.0)
            for i in range(4):
                b.cp(qpad[:, i, :, 8 * i:8 * i + 8], qT[1][:, :, 8 * i:8 * i + 8], eng="dve")
            kig = [al("sa_kig%d" % k, [128, 16, 128], BF16) for k in range(2)]
            kiT = [al("sa_kiT%d" % k, [128, 16, 128], BF16) for k in range(2)]
            rr = [al("sa_r%d" % k, [32, 512], F32) for k in range(3)]
            sm = al("sa_sm", [32, 8 + NBIS], F32)
            halfs = sm[:, 8:8 + NBIS]
            it = 0
            for cch in range(NPG // 4):
                kg, kt = kig[cch % 2], kiT[cch % 2]
                for i in range(4):
                    for p4 in range(4):
                        b.gather(kg[:, i * 4 + p4, 0:64], T["cache_a_kidx"], idx[:, i * NPG + cch * 4 + p4:i * NPG + cch * 4 + p4 + 1])
                b.cp(kg[:, :, 64:128], kg[:, :, 0:64], eng="pool")
                b.transposes(kt, kg.rearrange("p a f -> p (a f)"), 128, 16, g["identb"])
                cols = slice(cch * 512, (cch + 1) * 512)
                for h in range(8):
                    c_, base = h // 2, 64 * (h % 2)
                    ps = b.bank("mm", [0, 1, 2, 3])
                    for i in range(4):
                        b.mm(ps[0:32, :], qpad[base:base + 64, i, c_, :],
                             kt[base:base + 64, i * 4:(i + 1) * 4, :].rearrange("p a k -> p (a k)"), i == 0, i == 3)
                    r = rr[it % 3]
                    it += 1
                    b.act(r, ps[0:32, :], AF.Relu)
                    if h == 0:
                        b.ts(score[:, cols], r, swi[:, 0:1], None, op0=ALU.mult)
                    else:
                        b.stt(score[:, cols], r, swi[:, h:h + 1], score[:, cols], ALU.mult, ALU.add)
            ncols = slice(NPK, NKS)
            for h in range(8):
                c_, base = h // 2, 64 * (h % 2)
                ps = b.bank("mm", [0, 1, 2, 3])
                b.mm(ps[0:32, 0:32], qT[1][base:base + 64, c_, :], skiT[base:base + 64, :], True, True)
                r = rr[it % 3]
                it += 1
                b.act(r[:, 0:32], ps[0:32, 0:32], AF.Relu)
                if h == 0:
                    b.ts(score[:, ncols], r[:, 0:32], swi[:, 0:1], None, op0=ALU.mult)
                else:
                    b.stt(score[:, ncols], r[:, 0:32], swi[:, h:h + 1], score[:, ncols], ALU.mult, ALU.add)
            b.reduce(sm[:, 0:1], score, ALU.max, absval=True)
            b.ts(sm[:, 1:2], sm[:, 0:1], 1.0001, 1e-20, op0=ALU.mult, op1=ALU.add)
            b.ts(halfs, g["pow2"][0:32, :], sm[:, 1:2], None, op0=ALU.mult)
            b.tt(score[:, ncols], score[:, ncols], sidxm, ALU.add)
            b.memset(sm[:, 2:3], 0.0, eng="dve")
            for k in range(NBIS):
                b.ts(junk, score, sm[:, 2:3], None, op0=ALU.is_ge, op1=ALU.add, accum=sm[:, 3:4])
                if k < NBIS - 1:
                    b.ts(sm[:, 4:5], sm[:, 3:4], 255.5, -0.5, op0=ALU.is_ge, op1=ALU.add)
                    b.stt(sm[:, 2:3], sm[:, 4:5], halfs[:, k:k + 1], sm[:, 2:3], ALU.mult, ALU.add)
                else:
                    b.ts(sm[:, 4:5], sm[:, 3:4], 255.5, -1.0, op0=ALU.is_ge, op1=ALU.add)
                    b.stt(sm[:, 5:6], sm[:, 4:5], halfs[:, k:k + 1], sm[:, 2:3], ALU.mult, ALU.add)
            b.ts(sel, score, sm[:, 5:6], None, op0=ALU.is_ge)
            b.transposes(selTs, sel, 32, NPG, g["identb"])
            pbk = b.bank("tr", [4, 5]).bitcast(BF16)
            b.tr(pbk[0:32, 0:32], sel[:, ncols], g["identb"][0:32, 0:32])
            b.cp(selTn, pbk[0:32, 0:32])
            for i in range(4):
                attend(i, 0, T["cache_a_k"], T["cache_a_v"], kaTn, newtok[:, 1, :], None, None, selTs, selTn, 0)

        b.S.barrier()
        with ExitStack() as es_3:
            def al(name, shape, dt, es_=es_3):
                return es_.enter_context(nc.sbuf_tensor(name, list(shape), dt)).ap()
            wout = al("sa_wout", [128, 8, D], BF16)
            for j in range(2):
                b.dma(wout[:, :, j * 512:(j + 1) * 512],
                      T["w_out_ab"][:, j * 512:(j + 1) * 512].rearrange("(k p) n -> p k n", p=128), q="pool")
            gp1 = al("sa_gp1", [32, D], F32); b.dma(gp1, T["modS"][0, :, 2 * D:3 * D])
            lng = al("sa_lng", [32, D], F32); b.dma(lng, T["ln1_g"][0:1, :].partition_broadcast(32))
            lnb = al("sa_lnb", [32, D], F32); b.dma(lnb, T["ln1_b"][0:1, :].partition_broadcast(32))
            xs = al("sa_xs", [32, D], F32); b.dma(xs, T["xs"])
            lt = (al("sa_acc", [32, D], F32), al("sa_tmp", [32, D], F32), al("sa_st", [32, 4, 6], F32),
                  al("sa_mv", [32, 4], F32), al("sa_res", [32, D], F32))
            pss = []
            for n2 in range(2):
                ps = b.bank("mm", [0, 1, 2, 3])
                for kc_ in range(8):
                    b.mm(ps[0:32, :512], mixTs[:, kc_, :], wout[:, kc_, n2 * 512:(n2 + 1) * 512], kc_ == 0, kc_ == 7)
                pss.append(ps)
            residual_ln(b, lt, pss, xs, 32, gp1, lng, lnb, T["xs1_s"])
    b.S.barrier()


def build_full(nc, n_phys=NPHYS):
    b = B(nc, n_phys=n_phys)
    T = declare(b, with_sample=True)
    g = consts(b, T)
    phase0(b, T, g)
    phaseA(b, T, g, do_sample=True)
    phaseSA(b, T, g)
    phaseB1(b, T, g)
    phaseB2(b, T, g)
    phaseC(b, T, g, 0, T["x1_s"], T["x2_s"], T["xs1_s"], T["xs2_s"])
    phaseD(b, T, g)
    phaseC(b, T, g, 1, T["x3_s"], T["y_p"], T["xs3_s"], T["y_s"])
    return b, T


def rope_tab(pos):
    half = 32
    inv = (10000.0 ** (-np.arange(half, dtype=np.float32) / half)).astype(np.float32)
    ang = pos.astype(np.float32)[:, None] * inv[None, :]
    return np.concatenate([np.cos(ang), np.sin(ang)], axis=1).astype(np.float32)


def prep_inputs(inp, with_sample=True, n_phys=NPHYS, remap=None):
    f = np.float32
    ar = np.arange(128)
    identf = np.eye(128, dtype=f)
    tinc = (ar[:, None] <= ar[None, :]).astype(f)
    triT01 = (ar[:, None] <= ar[None, :]).astype(f)
    triqk = np.where(ar[None, :] <= ar[:, None], 0.0, NEGBIG).astype(f)
    pow2 = np.tile((2.0 ** -np.arange(NBIS)).astype(f)[None, :], (128, 1))
    shared = {
        "identf": identf, "tinc": tinc, "triT01": triT01, "triqk": triqk, "pow2": pow2,
        "w_mod": inp["w_mod"], "b_mod": inp["b_mod"],
        "ln1_g": inp["ln1_g"], "ln1_b": inp["ln1_b"], "ln2_g": inp["ln2_g"], "ln2_b": inp["ln2_b"],
        "w_in_ab": inp["w_in_ab"][0], "b_forget": inp["b_forget"], "w_out_ab": inp["w_out_ab"][0],
        "w_in_c": inp["w_in_c"][0], "lnv_g": inp["lnv_g"], "lnv_b": inp["lnv_b"],
        "w_spatial": inp["w_spatial"][0], "b_spatial": inp["b_spatial"][0], "w_out_c": inp["w_out_c"][0],
        "w_up": inp["w_up"], "w_conv": inp["w_conv"], "b_conv": inp["b_conv"], "w_down": inp["w_down"],
    }
    if with_sample:
        t8 = np.arange(8)
        smask01 = np.zeros((32, 4, 8), f)
        sidx = np.full((32, 32), NEGBIG, f)
        segmask = np.zeros((32, 4), f)
        for i in range(4):
            for t in range(8):
                segmask[i * 8 + t, i] = 1.0
                for q in range(8):
                    if t <= q:
                        smask01[i * 8 + t, i, q] = 1.0
                        sidx[i * 8 + q, i * 8 + t] = 0.0
        tinc32 = np.zeros((32, 32), f)
        for i in range(4):
            for a in range(8):
                for c_ in range(a, 8):
                    tinc32[i * 8 + a, i * 8 + c_] = 1.0
        selrow = np.zeros((32, 4, 128), f)
        for i in range(4):
            selrow[i * 8 + 7, i, :] = 1.0
        bd01 = np.zeros((64, 8, 64), f)
        for r_ in range(64):
            bd01[r_, 2 * ((r_ // 8) % 4) + (r_ // 32), :] = 1.0
        shared.update({"smask01": smask01, "sidxmask": sidx, "segmask": segmask, "tinc32": tinc32,
                       "selrow": selrow, "bd01": bd01, "iota128": np.arange(128, dtype=np.int32).reshape(128, 1)})
        if remap is None:
            shared.update({
                "cache_a_k": inp["cache_a_k"][0].reshape(n_phys * 128, 512),
                "cache_a_v": inp["cache_a_v"][0].reshape(n_phys * 128, 512),
                "cache_a_kidx": inp["cache_a_kidx"][0].reshape(n_phys * 128, 64),
                "cache_b_k": inp["cache_b_k"][0].reshape(n_phys * 128, 512),
                "cache_b_v": inp["cache_b_v"][0].reshape(n_phys * 128, 512),
                "cache_b_logf": inp["cache_b_logf"][0].reshape(n_phys * 128, 8),
            })
    maps = []
    for c in range(8):
        bb, half = c // 2, c % 2
        xp = inp["x_prompt"][bb]
        if half == 1:
            xk = np.ascontiguousarray(xp)
            pos = np.arange(4096)
            jm = np.zeros((1, 4096), f)
            blk = np.zeros((1, 32), f)
        else:
            xk = np.concatenate([np.zeros((2048, D), f), xp[:2048]], axis=0)
            pos = np.concatenate([np.zeros(2048), np.arange(2048)])
            jm = np.concatenate([np.full((1, 2048), NEGBIG, f), np.zeros((1, 2048), f)], axis=1)
            blk = np.concatenate([np.full((1, 16), NEG, f), np.zeros((1, 16), f)], axis=1)
        m = dict(shared)
        m.update({
            "xk": xk, "cstab": rope_tab(pos), "cP": np.tile(inp["c_prompt"][bb][None, :], (128, 1)),
            "jm": jm, "blkbias": blk, "haloflag": np.full((128, 1), float(half), f),
            "xs": np.ascontiguousarray(inp["x_sample"][4 * c:4 * c + 4].reshape(32, D)),
            "cstab_s": np.tile(rope_tab(8192 + np.arange(8)), (4, 1)),
            "cS": np.repeat(inp["c_sample"][4 * c:4 * c + 4], 8, axis=0),
        })
        if with_sample:
            m["state_conv"] = np.ascontiguousarray(inp["state_ffn_conv"][:, 4 * c:4 * c + 4].reshape(2, 8, DUP))
            pt = inp["page_table"][4 * c:4 * c + 4].reshape(1, 4 * NPG).astype(np.int32)
            if remap is not None:
                uniq = pt.reshape(-1)
                m["page_table"] = np.arange(uniq.size, dtype=np.int32).reshape(1, -1)
                for nm, w_ in (("cache_a_k", 512), ("cache_a_v", 512), ("cache_a_kidx", 64), ("cache_b_k", 512),
                               ("cache_b_v", 512), ("cache_b_logf", 8)):
                    m[nm] = np.ascontiguousarray(inp[nm][0][uniq].reshape(uniq.size * 128, w_))
            else:
                m["page_table"] = pt
        maps.append(m)
    return maps


def assemble(res):
    f = np.float32
    odd = [res[2 * bb + 1] for bb in range(4)]
    y_p = np.stack([np.concatenate([res[2 * bb]["y_p"][256:], res[2 * bb + 1]["y_p"][256:]], axis=0) for bb in range(4)])
    y_s = np.concatenate([r["y_s"] for r in res], axis=0).reshape(32, 8, D)
    ak_p = np.stack([r["ak_p"] for r in odd]).reshape(1, 4, 4096, 8, 64)
    av_p = np.stack([r["av_p"] for r in odd]).reshape(1, 4, 4096, 8, 64)
    aki_p = np.stack([r["aki_p"] for r in odd]).reshape(1, 4, 4096, 64)
    bk_p = np.stack([r["bk_p"] for r in odd]).reshape(1, 4, 4096, 8, 64)
    bv_p = np.stack([r["bv_p"] for r in odd]).reshape(1, 4, 4096, 8, 64)
    blf_p = np.stack([r["blf_p"] for r in odd]).reshape(1, 4, 4096, 8)
    conv_p = np.stack([r["conv_p"] for r in odd], axis=1)
    cat = lambda n: np.concatenate([r[n] for r in res], axis=0)
    ak_s = cat("ak_s").reshape(1, 32, 8, 8, 64)
    av_s = cat("av_s").reshape(1, 32, 8, 8, 64)
    aki_s = cat("aki_s").reshape(1, 32, 8, 64)
    bk_s = cat("bk_s").reshape(1, 32, 8, 8, 64)
    bv_s = cat("bv_s").reshape(1, 32, 8, 8, 64)
    blf_s = cat("blf_s").reshape(1, 32, 8, 8)
    conv_s = np.concatenate([r["conv_s"].reshape(2, 4, 2, DUP) for r in res], axis=1)
    cv_s = cat("cv_s").reshape(1, 32, 8, DCG)
    outs = (y_p, y_s, ak_p, av_p, aki_p, bk_p, bv_p, blf_p, conv_p, ak_s, av_s, aki_s, bk_s, bv_s, blf_s, conv_s, cv_s)
    return tuple(np.ascontiguousarray(o, dtype=f) for o in outs)


def kernel(**inputs):
    inp = {k: np.asarray(v) for k, v in inputs.items()}
    nc = bass.Bass("TRN2", target_bir_lowering=False)
    bld, T = build_full(nc)
    bld.S.emit()
    maps = prep_inputs(inp, with_sample=True)
    res = run_bass_kernel_spmd(nc, maps, core_ids=list(range(8)))
    return assemble(res.results)
```
